# Optimizing a Trainium2 kernel written in Bass

```python
import math
import jax, jax.numpy as jnp
from jax import lax
import numpy as np

D_MODEL = 1024
BATCH = 4
SEQ = 4096
DEPTH = 4
DEC_BATCH = 128
DEC_SEQ = 4
PAST_LEN = 2048
PAGE_SIZE = 128

N_MIXERS = 2
N_RWKV = (DEPTH + 1) // 2
N_NSA = DEPTH // 2

RW_HEAD = 64
RW_HEADS = D_MODEL // RW_HEAD
D_DECAY_LORA = 64
D_AAA_LORA = 64
D_MV_LORA = 32
D_GATE_LORA = 160
GN_EPS = 64e-5

NSA_HEADS = 16
NSA_KV_HEADS = 4
NSA_HD = D_MODEL // NSA_HEADS
NSA_GROUP = NSA_HEADS // NSA_KV_HEADS
NSA_QDIM = NSA_HEADS * NSA_HD
NSA_KVDIM = NSA_KV_HEADS * NSA_HD
NSA_IN_DIM = 2 * NSA_QDIM + 6 * NSA_KVDIM + 3 * NSA_HEADS
CMP_BLOCK = 32
CMP_STRIDE = 16
CMP_HIDDEN = NSA_HD
SEL_BLOCK = 64
SEL_TOPN = 16
WINDOW = 512
Q_BLOCK = 128

NUM_BUCKETS = 32
MAX_DISTANCE = 1024
RMS_EPS = 1e-6

kernel_name = 'rwkv7_nsa_hybrid_step'


def rms_norm(x, w, eps=RMS_EPS):
    xf = x.astype(jnp.float32)
    y = xf * lax.rsqrt(jnp.mean(xf * xf, axis=-1, keepdims=True) + eps)
    return (y * w.astype(jnp.float32)).astype(x.dtype)


def masked_softmax(logits, valid, axes):
    lf = jnp.where(valid, logits.astype(jnp.float32), -jnp.inf)
    m = jnp.max(lf, axis=axes, keepdims=True)
    m = jnp.where(jnp.isfinite(m), m, 0.0)
    e = jnp.exp(lf - m)
    return e / jnp.maximum(jnp.sum(e, axis=axes, keepdims=True), 1e-30)


def rel_bucket(dist):
    n = jnp.maximum(dist, 0)
    max_exact = NUM_BUCKETS // 2
    nf = jnp.maximum(n, 1).astype(jnp.float32)
    large = max_exact + (jnp.log(nf / max_exact) / math.log(MAX_DISTANCE / max_exact) * (NUM_BUCKETS - max_exact)).astype(jnp.int32)
    large = jnp.minimum(large, NUM_BUCKETS - 1)
    return jnp.where(n < max_exact, n, large)


def cmp_sel_overlap(n_cmp, n_sel):
    cs = jnp.arange(n_cmp)[:, None] * CMP_STRIDE
    ss = jnp.arange(n_sel)[None, :] * SEL_BLOCK
    ov = jnp.maximum(jnp.minimum(cs + CMP_BLOCK, ss + SEL_BLOCK) - jnp.maximum(cs, ss), 0)
    return ov.astype(jnp.float32) / CMP_BLOCK


def compress(rows, pos_emb, w1, w2):
    L = rows.shape[1]
    n_cmp = (L - CMP_BLOCK) // CMP_STRIDE + 1
    idx = jnp.arange(n_cmp)[:, None] * CMP_STRIDE + jnp.arange(CMP_BLOCK)[None, :]
    blocks = rows[:, idx] + pos_emb[None, None, :, None, :]
    h = jax.nn.silu(jnp.einsum('bnjkd,jde->bnke', blocks, w1.reshape(CMP_BLOCK, NSA_HD, CMP_HIDDEN)))
    return jnp.einsum('bnke,ed->bnkd', h, w2)


def nsa_compressed(kc_rows, vc_rows, k_norm, cmp_pos, cmp_w1, cmp_w2):
    k_cmp = rms_norm(compress(kc_rows, cmp_pos[0], cmp_w1[0], cmp_w2[0]), k_norm[2])
    v_cmp = compress(vc_rows, cmp_pos[1], cmp_w1[1], cmp_w2[1])
    cmp_end = jnp.arange(k_cmp.shape[1]) * CMP_STRIDE + (CMP_BLOCK - 1)
    return k_cmp, v_cmp, cmp_end


def nsa_project(h, w_in, q_norm, k_norm):
    B, T, _ = h.shape
    cuts = [NSA_QDIM + i * NSA_KVDIM for i in range(7)] + [2 * NSA_QDIM + 6 * NSA_KVDIM]
    q, kc, vc, ks, vs, kw, vw, z, gl = jnp.split(h @ w_in, cuts, axis=-1)
    q = rms_norm(q.reshape(B, T, NSA_HEADS, NSA_HD), q_norm)
    kc, vc, vs, vw = (t.reshape(B, T, NSA_KV_HEADS, NSA_HD) for t in (kc, vc, vs, vw))
    ks = rms_norm(ks.reshape(B, T, NSA_KV_HEADS, NSA_HD), k_norm[0])
    kw = rms_norm(kw.reshape(B, T, NSA_KV_HEADS, NSA_HD), k_norm[1])
    gates = jax.nn.sigmoid(gl).reshape(B, T, NSA_HEADS, 3)
    return q, kc, vc, ks, vs, kw, vw, z, gates


def nsa_attend(q, gates, qpos, k_cmp, v_cmp, cmp_end, k_sel, v_sel, k_win, v_win, win_pos, rel_table):
    tq = q.shape[0]
    scale = NSA_HD ** -0.5
    qg = q.reshape(tq, NSA_KV_HEADS, NSA_GROUP, NSA_HD)
    tg = rel_table.reshape(NUM_BUCKETS, NSA_KV_HEADS, NSA_GROUP)
    d_c = qpos[:, None] - cmp_end[None, :]
    b_c = jnp.transpose(tg[rel_bucket(d_c)], (0, 2, 3, 1))
    l_c = jnp.einsum('tkgd,nkd->tkgn', qg, k_cmp) * scale + b_c
    p_c = masked_softmax(l_c, (d_c >= 0)[:, None, None, :], -1)
    o_c = jnp.einsum('tkgn,nkd->tkgd', p_c.astype(v_cmp.dtype), v_cmp)
    n_sel = k_sel.shape[0]
    imp = jnp.einsum('tkgn,ns->tks', p_c, cmp_sel_overlap(k_cmp.shape[0], n_sel))
    blk = jnp.arange(n_sel)
    cur = (qpos // SEL_BLOCK)[:, None, None]
    forced = (blk == 0) | (blk == cur) | (blk == cur - 1)
    future = (blk * SEL_BLOCK)[None, None, :] > qpos[:, None, None]
    score = jnp.where(future, -jnp.inf, jnp.where(forced, jnp.inf, imp))
    _, idx = lax.top_k(score, min(SEL_TOPN, n_sel))
    kv_ix = jnp.arange(NSA_KV_HEADS)[None, :, None]
    kg = jnp.transpose(k_sel, (2, 0, 1, 3))[kv_ix, idx]
    vg = jnp.transpose(v_sel, (2, 0, 1, 3))[kv_ix, idx]
    pos_s = idx[..., None] * SEL_BLOCK + jnp.arange(SEL_BLOCK)
    d_s = qpos[:, None, None, None] - pos_s
    b_s = jnp.moveaxis(tg[rel_bucket(d_s), kv_ix[..., None]], -1, 2)
    l_s = jnp.einsum('tkgd,tkjsd->tkgjs', qg, kg) * scale + b_s
    p_s = masked_softmax(l_s, (d_s >= 0)[:, :, None], (-2, -1))
    o_s = jnp.einsum('tkgjs,tkjsd->tkgd', p_s.astype(vg.dtype), vg)
    d_w = qpos[:, None] - win_pos[None, :]
    valid_w = (d_w >= 0) & (d_w < WINDOW) & (win_pos >= 0)[None, :]
    b_w = jnp.transpose(tg[rel_bucket(d_w)], (0, 2, 3, 1))
    l_w = jnp.einsum('tkgd,lkd->tkgl', qg, k_win) * scale + b_w
    p_w = masked_softmax(l_w, valid_w[:, None, None, :], -1)
    o_w = jnp.einsum('tkgl,lkd->tkgd', p_w.astype(v_win.dtype), v_win)
    g = gates.reshape(tq, NSA_KV_HEADS, NSA_GROUP, 3).astype(o_c.dtype)
    o = g[..., 0:1] * o_c + g[..., 1:2] * o_s + g[..., 2:3] * o_w
    return o.reshape(tq, NSA_HEADS, NSA_HD)


def nsa_out(o, z, w_o):
    B, T = o.shape[:2]
    return (o.reshape(B, T, NSA_QDIM).astype(z.dtype) * jax.nn.silu(z)) @ w_o


def nsa_layer_prompt(h, w_in, q_norm, k_norm, cmp_pos, cmp_w1, cmp_w2, w_o, rel_table):
    B, T, _ = h.shape
    q, kc, vc, ks, vs, kw, vw, z, gates = nsa_project(h, w_in, q_norm, k_norm)
    k_cmp, v_cmp, cmp_end = nsa_compressed(kc, vc, k_norm, cmp_pos, cmp_w1, cmp_w2)
    n_sel = T // SEL_BLOCK
    ks_b = ks.reshape(B, n_sel, SEL_BLOCK, NSA_KV_HEADS, NSA_HD)
    vs_b = vs.reshape(B, n_sel, SEL_BLOCK, NSA_KV_HEADS, NSA_HD)
    pad = jnp.zeros((B, WINDOW, NSA_KV_HEADS, NSA_HD), kw.dtype)
    kw_p = jnp.concatenate([pad, kw], axis=1)
    vw_p = jnp.concatenate([pad, vw], axis=1)
    n_qb = T // Q_BLOCK
    q_items = q.reshape(B * n_qb, Q_BLOCK, NSA_HEADS, NSA_HD)
    g_items = gates.reshape(B * n_qb, Q_BLOCK, NSA_HEADS, 3)
    items = jnp.arange(B * n_qb)

    def body(args):
        qi, gi, it = args
        b = it // n_qb
        s = (it % n_qb) * Q_BLOCK
        qpos = s + jnp.arange(Q_BLOCK)
        band = (1, WINDOW + Q_BLOCK, NSA_KV_HEADS, NSA_HD)
        kwb = lax.dynamic_slice(kw_p, (b, s, 0, 0), band)[0]
        vwb = lax.dynamic_slice(vw_p, (b, s, 0, 0), band)[0]
        wpos = s - WINDOW + jnp.arange(WINDOW + Q_BLOCK)
        return nsa_attend(qi, gi, qpos, k_cmp[b], v_cmp[b], cmp_end, ks_b[b], vs_b[b], kwb, vwb, wpos, rel_table)

    o = lax.map(body, (q_items, g_items, items)).reshape(B, T, NSA_HEADS, NSA_HD)
    y = nsa_out(o, z, w_o)
    wb = min(WINDOW, T)
    return y, (kc, vc, ks, vs, kw[:, T - wb:], vw[:, T - wb:])


def nsa_layer_sample(h, c_ck, c_cv, c_sk, c_sv, win_k, win_v, page_table, w_in, q_norm, k_norm, cmp_pos, cmp_w1, cmp_w2, w_o, rel_table):
    B, T, _ = h.shape
    past = page_table.shape[1] * PAGE_SIZE
    q, kc, vc, ks, vs, kw, vw, z, gates = nsa_project(h, w_in, q_norm, k_norm)

    def with_past(cache, new):
        old = cache[page_table].reshape(B, past, NSA_KV_HEADS, NSA_HD).astype(new.dtype)
        return jnp.concatenate([old, new], axis=1)

    kc_all, vc_all, ks_all, vs_all = with_past(c_ck, kc), with_past(c_cv, vc), with_past(c_sk, ks), with_past(c_sv, vs)
    k_cmp, v_cmp, cmp_end = nsa_compressed(kc_all, vc_all, k_norm, cmp_pos, cmp_w1, cmp_w2)
    L = past + T
    n_sel = -(-L // SEL_BLOCK)
    padw = ((0, 0), (0, n_sel * SEL_BLOCK - L), (0, 0), (0, 0))
    ks_b = jnp.pad(ks_all, padw).reshape(B, n_sel, SEL_BLOCK, NSA_KV_HEADS, NSA_HD)
    vs_b = jnp.pad(vs_all, padw).reshape(B, n_sel, SEL_BLOCK, NSA_KV_HEADS, NSA_HD)
    wb = win_k.shape[1]
    kw_all = jnp.concatenate([win_k.astype(kw.dtype), kw], axis=1)
    vw_all = jnp.concatenate([win_v.astype(vw.dtype), vw], axis=1)
    wpos = past - wb + jnp.arange(wb + T)
    qpos = past + jnp.arange(T)

    def body(a):
        qi, gi, kcb, vcb, ksb, vsb, kwb, vwb = a
        return nsa_attend(qi, gi, qpos, kcb, vcb, cmp_end, ksb, vsb, kwb, vwb, wpos, rel_table)

    o = lax.map(body, (q, gates, k_cmp, v_cmp, ks_b, vs_b, kw_all, vw_all))
    y = nsa_out(o, z, w_o)
    return y, (kc, vc, ks, vs, kw_all[:, -wb:], vw_all[:, -wb:])


def wkv_scan(S0, r, decay, k, v, a_vec, b_vec):
    def step(S, inp):
        r_t, w_t, k_t, v_t, a_t, b_t = inp
        sa = jnp.einsum('bhij,bhj->bhi', S, a_t)
        S = S * w_t[:, :, None, :] + sa[..., None] * b_t[:, :, None, :] + v_t[..., None] * k_t[:, :, None, :]
        return S, jnp.einsum('bhij,bhj->bhi', S, r_t)
    xs = tuple(jnp.moveaxis(t.astype(jnp.float32), 1, 0) for t in (r, decay, k, v, a_vec, b_vec))
    S, ys = lax.scan(step, S0.astype(jnp.float32), xs)
    return jnp.moveaxis(ys, 0, 1), S


def rwkv_layer(h, shift, S0, v_first, mu, w_rkvz, w0, w1, w2, a0, a1, a2, g1, g2, k_k, k_a, r_k, ln_w, ln_b, w_o, vres):
    B, T, D = h.shape
    prev = jnp.concatenate([shift[:, None, :].astype(h.dtype), h[:, :-1]], axis=1)
    xx = prev - h
    xr, xw, xk, xv, xa, xg = [h + xx * mu[i] for i in range(6)]
    r, k, v, z = jnp.einsum('pbtc,pcd->pbtd', jnp.stack([xr, xk, xv, xg]), w_rkvz)
    w = -jax.nn.softplus(-(w0 + jnp.tanh(xw @ w1) @ w2)) - 0.5
    decay = jnp.exp(-jnp.exp(w.astype(jnp.float32)))
    if vres is None:
        v_first = v
    else:
        v0, v1, v2 = vres
        v = v + (v_first - v) * jax.nn.sigmoid(v0 + (xv @ v1) @ v2)
    a = jax.nn.sigmoid(a0 + (xa @ a1) @ a2)
    g = jax.nn.sigmoid(xg @ g1) @ g2
    heads = lambda t: t.reshape(B, T, RW_HEADS, RW_HEAD)
    kk = heads(k * k_k).astype(jnp.float32)
    kk = kk / jnp.maximum(jnp.sqrt(jnp.sum(kk * kk, axis=-1, keepdims=True)), 1e-12)
    k = k * (1 + (a - 1) * k_a)
    rh, kh, vh, ah = (heads(t).astype(jnp.float32) for t in (r, k, v, a))
    y, S = wkv_scan(S0, rh, heads(decay), kh, vh, -kk, kk * ah)
    mean = jnp.mean(y, axis=-1, keepdims=True)
    var = jnp.mean(jnp.square(y - mean), axis=-1, keepdims=True)
    yn = ((y - mean) * lax.rsqrt(var + GN_EPS)).reshape(B, T, D) * ln_w + ln_b
    bonus = (jnp.sum(rh * kh * r_k, axis=-1, keepdims=True) * vh).reshape(B, T, D)
    o = (yn + bonus) * g * jax.nn.silu(z)
    return o.astype(h.dtype) @ w_o, h[:, -1], S, v_first


def setup_inputs(seed: int = 0) -> dict:
    key = jax.random.key(seed)
    ks = iter(jax.random.split(key, 64))
    nrm = lambda shape, s=1.0: jax.random.normal(next(ks), shape, jnp.float32) * s
    uni = lambda shape, lo, hi: jax.random.uniform(next(ks), shape, jnp.float32, lo, hi)
    D = D_MODEL
    n_pages = PAST_LEN // PAGE_SIZE
    n_pool = (DEC_BATCH * n_pages * 5) // 4
    win_buf = min(WINDOW, PAST_LEN)
    page_shape = (N_NSA, n_pool, PAGE_SIZE, NSA_KV_HEADS, NSA_HD)
    win_shape = (N_NSA, DEC_BATCH, win_buf, NSA_KV_HEADS, NSA_HD)
    perm = jax.random.permutation(next(ks), n_pool)
    page_table = perm[:DEC_BATCH * n_pages].reshape(DEC_BATCH, n_pages).astype(jnp.int32)
    nv = max(N_RWKV - 1, 0)
    return {
        'x_prompt': nrm((BATCH, SEQ, D)),
        'x_sample': nrm((DEC_BATCH, DEC_SEQ, D)),
        'cache_cmp_k': nrm(page_shape),
        'cache_cmp_v': nrm(page_shape),
        'cache_sel_k': nrm(page_shape),
        'cache_sel_v': nrm(page_shape),
        'state_win_k': nrm(win_shape),
        'state_win_v': nrm(win_shape),
        'state_wkv': nrm((N_RWKV, DEC_BATCH, RW_HEADS, RW_HEAD, RW_HEAD), 0.5),
        'state_shift': nrm((N_RWKV, DEC_BATCH, D)),
        'page_table': page_table,
        'norm_w': 1.0 + nrm((DEPTH, D), 0.02),
        'rel_bias': nrm((NUM_BUCKETS, NSA_HEADS), 0.2),
        'rw_mu': uni((N_RWKV, 6, D), 0.0, 1.0),
        'rw_w_rkvz': nrm((N_RWKV, 4, D, D), D ** -0.5),
        'rw_w0': uni((N_RWKV, D), -3.0, 1.0),
        'rw_w1': nrm((N_RWKV, D, D_DECAY_LORA), D ** -0.5),
        'rw_w2': nrm((N_RWKV, D_DECAY_LORA, D), 0.5 * D_DECAY_LORA ** -0.5),
        'rw_a0': nrm((N_RWKV, D), 0.1),
        'rw_a1': nrm((N_RWKV, D, D_AAA_LORA), D ** -0.5),
        'rw_a2': nrm((N_RWKV, D_AAA_LORA, D), 0.5 * D_AAA_LORA ** -0.5),
        'rw_v0': nrm((nv, D), 0.1),
        'rw_v1': nrm((nv, D, D_MV_LORA), D ** -0.5),
        'rw_v2': nrm((nv, D_MV_LORA, D), 0.5 * D_MV_LORA ** -0.5),
        'rw_g1': nrm((N_RWKV, D, D_GATE_LORA), D ** -0.5),
        'rw_g2': nrm((N_RWKV, D_GATE_LORA, D), D_GATE_LORA ** -0.5),
        'rw_k_k': 0.85 + nrm((N_RWKV, D), 0.02),
        'rw_k_a': 1.0 + nrm((N_RWKV, D), 0.02),
        'rw_r_k': nrm((N_RWKV, RW_HEADS, RW_HEAD), 0.1),
        'rw_ln_w': 1.0 + nrm((N_RWKV, D), 0.02),
        'rw_ln_b': nrm((N_RWKV, D), 0.02),
        'rw_w_o': nrm((N_RWKV, D, D), D ** -0.5),
        'nsa_w_in': nrm((N_NSA, D, NSA_IN_DIM), D ** -0.5),
        'nsa_q_norm': 1.0 + nrm((N_NSA, NSA_HD), 0.02),
        'nsa_k_norm': 1.0 + nrm((N_NSA, 3, NSA_HD), 0.02),
        'nsa_cmp_pos': nrm((N_NSA, 2, CMP_BLOCK, NSA_HD), 0.2),
        'nsa_cmp_w1': nrm((N_NSA, 2, CMP_BLOCK * NSA_HD, CMP_HIDDEN), (CMP_BLOCK * NSA_HD) ** -0.5),
        'nsa_cmp_w2': nrm((N_NSA, 2, CMP_HIDDEN, NSA_HD), CMP_HIDDEN ** -0.5),
        'nsa_w_o': nrm((N_NSA, NSA_QDIM, D), NSA_QDIM ** -0.5),
    }


def reference(x_prompt, x_sample, cache_cmp_k, cache_cmp_v, cache_sel_k, cache_sel_v, state_win_k, state_win_v, state_wkv, state_shift, page_table,
              norm_w, rel_bias, rw_mu, rw_w_rkvz, rw_w0, rw_w1, rw_w2, rw_a0, rw_a1, rw_a2, rw_v0, rw_v1, rw_v2, rw_g1, rw_g2, rw_k_k, rw_k_a, rw_r_k,
              rw_ln_w, rw_ln_b, rw_w_o, nsa_w_in, nsa_q_norm, nsa_k_norm, nsa_cmp_pos, nsa_cmp_w1, nsa_cmp_w2, nsa_w_o):
    xp, xs = x_prompt, x_sample
    vf_p = vf_s = None
    p_wkv, p_shift, s_wkv, s_shift = [], [], [], []
    p_kv = [[] for _ in range(6)]
    s_kv = [[] for _ in range(6)]
    for i in range(DEPTH):
        j = i // N_MIXERS
        hp = rms_norm(xp, norm_w[i])
        hs = rms_norm(xs, norm_w[i])
        if i % N_MIXERS == 0:
            lw = (rw_mu[j], rw_w_rkvz[j], rw_w0[j], rw_w1[j], rw_w2[j], rw_a0[j], rw_a1[j], rw_a2[j], rw_g1[j], rw_g2[j],
                  rw_k_k[j], rw_k_a[j], rw_r_k[j], rw_ln_w[j], rw_ln_b[j], rw_w_o[j])
            vres = None if j == 0 else (rw_v0[j - 1], rw_v1[j - 1], rw_v2[j - 1])
            bp = hp.shape[0]
            yp, shp, Sp, vf_p = rwkv_layer(hp, jnp.zeros((bp, D_MODEL), hp.dtype), jnp.zeros((bp, RW_HEADS, RW_HEAD, RW_HEAD), jnp.float32), vf_p, *lw, vres)
            ys, shs, Ss, vf_s = rwkv_layer(hs, state_shift[j], state_wkv[j], vf_s, *lw, vres)
            p_wkv.append(Sp); p_shift.append(shp); s_wkv.append(Ss); s_shift.append(shs)
        else:
            nl = (nsa_w_in[j], nsa_q_norm[j], nsa_k_norm[j], nsa_cmp_pos[j], nsa_cmp_w1[j], nsa_cmp_w2[j], nsa_w_o[j])
            yp, newp = nsa_layer_prompt(hp, *nl, rel_bias)
            ys, news = nsa_layer_sample(hs, cache_cmp_k[j], cache_cmp_v[j], cache_sel_k[j], cache_sel_v[j], state_win_k[j], state_win_v[j], page_table, *nl, rel_bias)
            for m in range(6):
                p_kv[m].append(newp[m])
                s_kv[m].append(news[m])
        xp = xp + yp.astype(xp.dtype)
        xs = xs + ys.astype(xs.dtype)
    prompt_wkv, prompt_shift = jnp.stack(p_wkv), jnp.stack(p_shift)
    sample_wkv, sample_shift = jnp.stack(s_wkv), jnp.stack(s_shift)
    prompt_cmp_k, prompt_cmp_v, prompt_sel_k, prompt_sel_v, prompt_win_k, prompt_win_v = [jnp.stack(t) for t in p_kv]
    sample_cmp_k, sample_cmp_v, sample_sel_k, sample_sel_v, sample_win_k, sample_win_v = [jnp.stack(t) for t in s_kv]
    return (xp, xs, prompt_wkv, prompt_shift, prompt_cmp_k, prompt_cmp_v, prompt_sel_k, prompt_sel_v, prompt_win_k, prompt_win_v,
            sample_wkv, sample_shift, sample_cmp_k, sample_cmp_v, sample_sel_k, sample_sel_v, sample_win_k, sample_win_v)
```

```python
import contextlib
import numpy as np
import concourse.bass as bass
import concourse.mybir as mybir

F32 = mybir.dt.float32
BF16 = mybir.dt.bfloat16
I32 = mybir.dt.int32
AF = mybir.ActivationFunctionType
ALU = mybir.AluOpType
AX = mybir.AxisListType


class Sem:
    def __init__(self, h, is_dma):
        self.h = h
        self.is_dma = is_dma
        self.total = 0


class Res:
    __slots__ = ("w", "r", "dsem", "name", "isem")

    def __init__(self, name=""):
        self.w = None
        self.r = {}
        self.dsem = None
        self.isem = None
        self.name = name


class V:
    __slots__ = ("ap", "res")

    def __init__(self, ap, res):
        self.ap = ap
        self.res = res if isinstance(res, (list, tuple)) else [res]


class Tn:
    def __init__(self, t, name, res=None):
        self.t = t
        self.name = name
        self.res = res if res is not None else Res(name)

    def __getitem__(self, k):
        return V(self.t[k], self.res)

    def v(self, ap):
        return V(ap, self.res)


class KB:
    def __init__(self):
        self.nc = bass.Bass("TRN2", target_bir_lowering=False)
        nc = self.nc
        self.es = contextlib.ExitStack()
        self.engs = {"pe": nc.tensor, "dve": nc.vector, "act": nc.scalar, "pool": nc.gpsimd, "sp": nc.sync}
        self.sems = {}
        for e in ("pe", "dve", "act", "pool"):
            self.sems[e] = Sem(self.es.enter_context(nc.semaphore("s_" + e)), False)
        self.cnt = {e: 0 for e in self.engs}
        self.waited = {e: {} for e in self.engs}
        self.dsems = []
        self.n_dsem = 0
        self.max_dsem = 80
        self.nins = 0
        self.sbuf_bytes = 0

    def sb(self, name, shape, dt=F32):
        self._uid = getattr(self, "_uid", 0) + 1
        st = self.scopes[-1] if getattr(self, "scopes", None) else self.es
        t = st.enter_context(self.nc.sbuf_tensor("sb%d_%s" % (self._uid, name), list(shape), dt))
        n = 1
        for s in shape[1:]:
            n *= s
        self.sbuf_bytes += n * (4 if dt in (F32, I32) else 2)
        return Tn(t, name)

    @contextlib.contextmanager
    def scope(self):
        if not hasattr(self, "scopes"):
            self.scopes = []
        st = contextlib.ExitStack()
        self.scopes.append(st)
        b0 = self.sbuf_bytes
        try:
            yield
        finally:
            self.barrier()
            self.peak = max(getattr(self, "peak", 0), self.sbuf_bytes)
            self.sbuf_bytes = b0
            self.scopes.pop()
            st.close()

    def barrier(self):
        for e in ("pe", "dve", "act", "pool", "sp"):
            eng = self.engs[e]
            wd = self.waited[e]
            for e2 in ("pe", "dve", "act", "pool"):
                if e2 == e and e == "pe":
                    continue
                s = self.sems[e2]
                v = self.cnt[e2]
                if v > 0 and wd.get(s, 0) < v:
                    eng.wait_ge(s.h, v)
                    wd[s] = v
            for s in self.dsems:
                if s.total > 0 and wd.get(s, 0) < s.total:
                    eng.wait_ge(s.h, s.total)
                    wd[s] = s.total

    def ps(self, name, shape, dt=F32):
        t = self.es.enter_context(self.nc.psum_tensor("ps_" + name, list(shape), dt))
        return Tn(t, name)

    def dram(self, name, shape, dt=F32, kind="Internal"):
        t = self.nc.dram_tensor(name, list(shape), dt, kind=kind).ap()
        return Tn(t, name)

    def new_dsem(self):
        if self.n_dsem < self.max_dsem:
            s = Sem(self.es.enter_context(self.nc.semaphore("d%d" % self.n_dsem)), True)
            self.dsems.append(s)
            self.n_dsem += 1
            return s
        s = self.dsems[self.n_dsem % self.max_dsem]
        self.n_dsem += 1
        return s

    def _waits(self, e, R, W):
        need = {}

        def add(ev):
            if ev is None:
                return
            s, v = ev
            if e == "pe" and s is self.sems["pe"]:
                return
            if need.get(s, 0) < v:
                need[s] = v

        for r in R:
            add(r.w)
        for w in W:
            add(w.w)
            for s, v in w.r.items():
                add((s, v))
        eng = self.engs[e]
        wd = self.waited[e]
        for s, v in need.items():
            if s.is_dma:
                v = s.total
            if wd.get(s, 0) < v:
                eng.wait_ge(s.h, v)
                wd[s] = v
                self.nins += 1

    def _done(self, ev, R, W):
        for r in R:
            r.r[ev[0]] = ev[1]
        for w in W:
            w.w = ev
            w.r = {}

    def op(self, e, fn, R, W):
        R = [x for v in R for x in v.res]
        W = [x for v in W for x in v.res]
        self._waits(e, R, W)
        ins = fn(self.engs[e])
        self.cnt[e] += 1
        self.nins += 1
        ins.then_inc(self.sems[e].h, 1)
        self._done((self.sems[e], self.cnt[e]), R, W)

    def dma(self, q, out, in_, semres=None, **kw):
        R = list(in_.res)
        W = list(out.res)
        self._waits(q, R, W)
        ins = self.engs[q].dma_start(out=out.ap, in_=in_.ap, **kw)
        sr = semres if semres is not None else (out.res[0])
        if sr.dsem is None:
            sr.dsem = self.new_dsem()
        s = sr.dsem
        s.total += 16
        ins.then_inc(s.h, 16)
        self.nins += 1
        self._done((s, s.total), R, W)

    def idma(self, out, in_ap, in_res, idx, element_offset=0):
        import concourse.bass as bass
        R = list(idx.res) + [in_res]
        W = list(out.res)
        self._waits("pool", R, W)
        ins = self.engs["pool"].indirect_dma_start(out=out.ap, out_offset=None, in_=in_ap,
                                                   in_offset=bass.IndirectOffsetOnAxis(ap=idx.ap, axis=0), element_offset=element_offset)
        sr = out.res[0]
        if sr.isem is None:
            sr.isem = Sem(self.es.enter_context(self.nc.semaphore("i%d" % len(self.dsems))), True)
            self.dsems.append(sr.isem)
        s = sr.isem
        s.total += 16
        ins.then_inc(s.h, 16)
        self.nins += 1
        self._done((s, s.total), R, W)

    def finish(self):
        eng = self.engs["sp"]
        for s in self.dsems:
            if s.total > 0 and self.waited["sp"].get(s, 0) < s.total:
                eng.wait_ge(s.h, s.total)
        for e in ("pe", "dve", "act", "pool"):
            if self.cnt[e] > 0:
                eng.wait_ge(self.sems[e].h, self.cnt[e])
        self.es.close()
        return self.nc

    def mm(self, out, lhsT, rhs, start=True, stop=True, sgc=False):
        if sgc:
            self.op("pe", lambda g: g.matmul(out.ap, lhsT.ap, rhs.ap, start=start, stop=stop, skip_group_check=True), [lhsT, rhs], [out])
        else:
            self.op("pe", lambda g: g.matmul(out.ap, lhsT.ap, rhs.ap, start=start, stop=stop), [lhsT, rhs], [out])

    def sp_wait(self, views):
        R = [x for v in views for x in v.res]
        self._waits("sp", R, [])

    def tr(self, out, in_, ident):
        self.op("pe", lambda g: g.transpose(out.ap, in_.ap, ident.ap), [in_, ident], [out])

    def tt(self, e, out, a, b, op):
        self.op(e, lambda g: g.tensor_tensor(out.ap, a.ap, b.ap, op), [a, b], [out])

    def ts(self, e, out, a, s1, s2=None, op0=ALU.mult, op1=None):
        R = [a]
        s1a = s1.ap if isinstance(s1, V) else s1
        s2a = s2.ap if isinstance(s2, V) else s2
        if isinstance(s1, V):
            R.append(s1)
        if isinstance(s2, V):
            R.append(s2)
        if op1 is None:
            self.op(e, lambda g: g.tensor_scalar(out.ap, a.ap, s1a, None, op0), R, [out])
        else:
            self.op(e, lambda g: g.tensor_scalar(out.ap, a.ap, s1a, s2a, op0, op1), R, [out])

    def stt(self, e, out, a, s, b, op0, op1):
        e = "dve"
        R = [a, b]
        sa = s.ap if isinstance(s, V) else s
        if isinstance(s, V):
            R.append(s)
        self.op(e, lambda g: g.scalar_tensor_tensor(out.ap, a.ap, sa, b.ap, op0, op1), R, [out])

    def cp(self, e, out, a):
        if e == "act":
            self.op(e, lambda g: g.copy(out.ap, a.ap), [a], [out])
        else:
            self.op(e, lambda g: g.tensor_copy(out.ap, a.ap), [a], [out])

    def act(self, out, a, func, bias=None, scale=None, accum=None):
        R = [a]
        kw = {}
        if bias is not None:
            if isinstance(bias, V):
                R.append(bias)
                kw["bias"] = bias.ap
            else:
                kw["bias"] = bias
        if scale is not None:
            if isinstance(scale, V):
                R.append(scale)
                kw["scale"] = scale.ap
            else:
                kw["scale"] = scale
        W = [out]
        if accum is not None:
            kw["accum_out"] = accum.ap
            W.append(accum)
        self.op("act", lambda g: g.activation(out.ap, a.ap, func, **kw), R, W)

    def red(self, e, out, a, op=ALU.add, axis=AX.X):
        self.op(e, lambda g: g.tensor_reduce(out.ap, a.ap, axis, op), [a], [out])

    def memset(self, e, out, val):
        self.op(e, lambda g: g.memset(out.ap, val), [], [out])


D = 1024
KC = 8
NEG_C = -0.6065306597126334
GN_EPS = 64e-5
RMS_EPS = 1e-6


def host_consts():
    import ml_dtypes
    bf = ml_dtypes.bfloat16
    c = {}
    c["c_identb"] = np.eye(128, dtype=np.float32).astype(bf)
    c["c_identf"] = np.eye(128, dtype=np.float32)
    l = np.arange(128)[:, None]
    t = np.arange(128)[None, :]
    tri = np.zeros((128, 3, 128), np.float32)
    tri[:, 0, :] = (l <= t) * NEG_C
    tri[:, 1, :] = (l < t) * NEG_C
    tri[:, 2, :] = NEG_C
    c["c_tri"] = tri
    c["c_onescol"] = np.full((128, 1), NEG_C, np.float32)
    mis = np.zeros((128, 256), np.float32)
    mis[:, 0:128] = (l <= t)
    mis[:, 128:256] = (l < t)
    c["c_maskIS"] = mis.astype(bf)
    c["c_maskSL"] = (l > t).astype(np.float32).astype(bf)
    c["c_ones"] = np.ones((128, 128), np.float32)
    return c


class MK:
    def __init__(self, T, NBS, TS=4):
        self.T = T
        self.NBS = NBS
        self.TS = TS
        self.NS = NBS * TS
        self.k = KB()
        k = self.k
        self.din = {}
        self.dout = {}
        self.c_identb = self.inp("c_identb", [128, 128], BF16)
        self.c_identf = self.inp("c_identf", [128, 128], F32)
        self.c_tri = self.inp("c_tri", [128, 3, 128], F32)
        self.c_onescol = self.inp("c_onescol", [128, 1], F32)
        self.c_maskIS = self.inp("c_maskIS", [128, 256], BF16)
        self.c_maskSL = self.inp("c_maskSL", [128, 128], BF16)
        self.c_ones = self.inp("c_ones", [128, 128], F32)
        self.banks = [k.ps("bank%d" % i, [128, 512], F32) for i in range(6)]
        self.bbanks = [k.ps("bbank%d" % i, [128, 1024], BF16) for i in range(2)]
        self.bank_i = 0
        self.bbank_i = 0
        self.identb = k.sb("identb", [128, 128], BF16)
        self.identf = k.sb("identf", [128, 128], F32)
        self.tri = k.sb("tri", [128, 3, 128], F32)
        self.onescol = k.sb("onescol", [128, 1], F32)
        self.maskIS = k.sb("maskIS", [128, 256], BF16)
        self.maskSL = k.sb("maskSL", [128, 128], BF16)
        self.ones = k.sb("ones", [128, 128], F32)
        for sbt, dr in ((self.identb, self.c_identb), (self.identf, self.c_identf), (self.tri, self.c_tri),
                        (self.onescol, self.c_onescol), (self.maskIS, self.c_maskIS), (self.maskSL, self.c_maskSL),
                        (self.ones, self.c_ones)):
            k.dma("sp", sbt[:], dr[:])
        self._stage_i = 0

    def inp(self, name, shape, dt=F32):
        t = self.k.dram(name, shape, dt, kind="ExternalInput")
        self.din[name] = t
        return t

    def outp(self, name, shape, dt=F32):
        t = self.k.dram(name, shape, dt, kind="ExternalOutput")
        self.dout[name] = t
        return t

    def bank(self):
        b = self.banks[self.bank_i % len(self.banks)]
        self.bank_i += 1
        return b

    def bbank(self):
        b = self.bbanks[self.bbank_i % len(self.bbanks)]
        self.bbank_i += 1
        return b

    def load_w_bf16(self, dst, src_ap, src_res, kc, n, stage, engs=("pool", "act")):
        k = self.k
        src3 = src_ap.rearrange("(c p) n -> p c n", p=128)
        per = max(1, 2048 // n)
        i = 0
        c = 0
        while c < kc:
            cc = min(per, kc - c)
            st = stage[self._stage_i % len(stage)]
            self._stage_i += 1
            stv = V(st.t[:, 0:cc * n].rearrange("p (c n) -> p c n", n=n), st.res)
            k.dma("sp", stv, V(src3[:, c:c + cc, :], src_res))
            k.cp(engs[i % len(engs)], dst[:, c:c + cc, :], stv)
            c += cc
            i += 1

    def load_rows_bf16(self, dst, src_ap, src_res, rows, n, stage):
        k = self.k
        st = stage[self._stage_i % len(stage)]
        self._stage_i += 1
        stv = V(st.t[0:rows, 0:n], st.res)
        k.dma("sp", stv, V(src_ap, src_res))
        k.cp("pool", dst[0:rows, :], stv)

    def bcast_load(self, dst, src_ap, src_res):
        self.k.dma("sp", dst[:], V(src_ap.partition_broadcast(128), src_res))

    def rwkv_phase1(self, j, li, xT_p, xT_s, W, scr, O):
        k = self.k
        T, NS, NBS, TS = self.T, self.NS, self.NBS, self.TS
        has_v = j > 0
        stage = [k.sb("stg%d" % i, [128, 2048], F32) for i in range(2)]
        Wr = k.sb("Wr", [128, KC, D], BF16)
        Wk = k.sb("Wk", [128, KC, D], BF16)
        Wv = k.sb("Wv", [128, KC, D], BF16)
        Wz = k.sb("Wz", [128, KC, D], BF16)
        rkvz = W["rw_w_rkvz"]
        for wi, dst in enumerate((Wr, Wk, Wv, Wz)):
            self.load_w_bf16(dst, rkvz.t[j, wi], rkvz.res, KC, D, stage)
        w1 = k.sb("w1", [128, KC, 64], BF16)
        a1 = k.sb("a1", [128, KC, 64], BF16)
        g1 = k.sb("g1", [128, KC, 160], BF16)
        self.load_w_bf16(w1, W["rw_w1"].t[j], W["rw_w1"].res, KC, 64, stage)
        self.load_w_bf16(a1, W["rw_a1"].t[j], W["rw_a1"].res, KC, 64, stage)
        self.load_w_bf16(g1, W["rw_g1"].t[j], W["rw_g1"].res, KC, 160, stage)
        w2 = k.sb("w2", [64, D], BF16)
        a2 = k.sb("a2", [64, D], BF16)
        g2a = k.sb("g2a", [128, D], BF16)
        g2b = k.sb("g2b", [32, D], BF16)
        self.load_rows_bf16(w2, W["rw_w2"].t[j], W["rw_w2"].res, 64, D, stage)
        self.load_rows_bf16(a2, W["rw_a2"].t[j], W["rw_a2"].res, 64, D, stage)
        self.load_rows_bf16(g2a, W["rw_g2"].t[j, 0:128], W["rw_g2"].res, 128, D, stage)
        self.load_rows_bf16(g2b, W["rw_g2"].t[j, 128:160], W["rw_g2"].res, 32, D, stage)
        if has_v:
            v1 = k.sb("v1", [128, KC, 32], BF16)
            v2 = k.sb("v2", [32, D], BF16)
            self.load_w_bf16(v1, W["rw_v1"].t[j - 1], W["rw_v1"].res, KC, 32, stage)
            self.load_rows_bf16(v2, W["rw_v2"].t[j - 1], W["rw_v2"].res, 32, D, stage)
            v0b = k.sb("v0b", [128, D])
            self.bcast_load(v0b, W["rw_v0"].t[j - 1], W["rw_v0"].res)
        w0b = k.sb("w0b", [128, D])
        a0b = k.sb("a0b", [128, D])
        self.bcast_load(w0b, W["rw_w0"].t[j], W["rw_w0"].res)
        self.bcast_load(a0b, W["rw_a0"].t[j], W["rw_a0"].res)
        mu = k.sb("mu", [128, 6, KC])
        k.dma("sp", mu[:], V(W["mu_fm"].t[j], W["mu_fm"].res))
        nw = k.sb("nw", [128, KC])
        k.dma("sp", nw[:], V(W["nw_fm"].t[li], W["nw_fm"].res))

        NTK = 128
        xT = k.sb("xT", [128, KC, NTK])
        sq = k.sb("sq", [128, KC, NTK])
        rstd = k.sb("rstd", [128, NTK])
        hT = k.sb("hT", [128, KC, NTK + 32])
        xx = k.sb("xx", [128, KC, NTK])
        xi = [k.sb("xi%d" % i, [128, KC, NTK], BF16) for i in range(6)]
        lo_w = k.sb("lo_w", [64, NTK], BF16)
        lo_a = k.sb("lo_a", [64, NTK], BF16)
        lo_ga = k.sb("lo_ga", [128, NTK], BF16)
        lo_gb = k.sb("lo_gb", [32, NTK], BF16)
        lo_v = k.sb("lo_v", [32, NTK], BF16) if has_v else None
        st_r = k.sb("st_r", [128, D])
        st_k = k.sb("st_k", [128, D])
        st_v = k.sb("st_v", [128, D])
        st_w = k.sb("st_w", [128, D])
        st_a = k.sb("st_a", [128, D])
        st_g = k.sb("st_g", [128, D])
        st_z = k.sb("st_z", [128, D])
        st_vf = k.sb("st_vf", [128, D]) if has_v else None
        shiftT = k.sb("shiftT", [128, KC, max(NBS, 1)])
        shc = k.sb("shc", [128, KC, max(NBS, 1)])
        shcp = k.sb("shcp", [128, KC])

        def tile(xT_d, col0, ntok, nb, tl, first, pref, row0, shift_out):
            xTv = V(xT.t[:, :, 0:ntok], xT.res)
            k.dma("sp", xTv, V(xT_d.t[:, col0:col0 + ntok].rearrange("(c p) t -> p c t", p=128), xT_d.res))
            sqv = V(sq.t[:, :, 0:ntok], sq.res)
            k.tt("pool", sqv, xTv, xTv, ALU.mult)
            b = self.bank()
            for c in range(KC):
                k.mm(b[:, 0:ntok], self.ones[:], V(sq.t[:, c, 0:ntok], sq.res), start=(c == 0), stop=(c == KC - 1))
            rs = V(rstd.t[:, 0:ntok], rstd.res)
            k.ts("dve", rs, b[:, 0:ntok], 1.0 / D, RMS_EPS, op0=ALU.mult, op1=ALU.add)
            k.act(rs, rs, AF.Sqrt)
            k.op("dve", lambda g: g.reciprocal(rs.ap, rs.ap), [rs], [rs])
            hv4 = hT.t[:, :, 0:nb * (tl + 1)].rearrange("p c (b t) -> p c b t", t=tl + 1)
            if nb == 1:
                if first:
                    k.memset("pool", V(hv4[:, :, :, 0:1], hT.res), 0.0)
                else:
                    k.cp("pool", V(hv4[:, :, :, 0:1], hT.res), V(hv4[:, :, :, tl:tl + 1], hT.res))
            else:
                k.cp("pool", V(hv4[:, :, :, 0], hT.res), V(shiftT.t[:, :, 0:nb], shiftT.res))
            rs3 = V(rstd.t[:, 0:ntok].rearrange("p (b t) -> p b t", t=tl), rstd.res)
            for c in range(KC):
                k.stt("dve", V(hv4[:, c, :, 1:tl + 1], hT.res),
                      V(xT.t[:, c, 0:ntok].rearrange("p (b t) -> p b t", t=tl), xT.res),
                      nw[:, c:c + 1], rs3, ALU.mult, ALU.mult)
            shift_out(hv4)
            xxv4 = xx.t[:, :, 0:ntok].rearrange("p c (b t) -> p c b t", t=tl)
            for c in range(KC):
                k.tt("dve" if c % 2 == 0 else "pool", V(xxv4[:, c], xx.res), V(hv4[:, c, :, 0:tl], hT.res), V(hv4[:, c, :, 1:tl + 1], hT.res), ALU.subtract)
            for i in range(6):
                for c in range(KC):
                    e = "dve" if (i * KC + c) % 2 == 0 else "pool"
                    k.stt(e, V(xi[i].t[:, c, 0:ntok].rearrange("p (b t) -> p b t", t=tl), xi[i].res),
                          V(xxv4[:, c], xx.res), mu[:, i, c:c + 1], V(hv4[:, c, :, 1:tl + 1], hT.res),
                          ALU.mult, ALU.add)
            xr, xw, xk, xv, xa, xg = xi

            def lora1(xin, w, c0, r, dst, func):
                bb = self.bank()
                for c in range(KC):
                    k.mm(bb[0:r, 0:ntok], V(w.t[:, c, c0:c0 + r], w.res), V(xin.t[:, c, 0:ntok], xin.res), start=(c == 0), stop=(c == KC - 1))
                if func is None:
                    k.cp("act", V(dst.t[0:r, 0:ntok], dst.res), bb[0:r, 0:ntok])
                else:
                    k.act(V(dst.t[0:r, 0:ntok], dst.res), bb[0:r, 0:ntok], func)
            lora1(xw, w1, 0, 64, lo_w, AF.Tanh)
            lora1(xa, a1, 0, 64, lo_a, None)
            lora1(xg, g1, 0, 128, lo_ga, AF.Sigmoid)
            lora1(xg, g1, 128, 32, lo_gb, AF.Sigmoid)
            if has_v:
                lora1(xv, v1, 0, 32, lo_v, None)
                vf = scr[pref + "vf"]
                k.dma("sp", V(st_vf.t[0:ntok, :], st_vf.res), V(vf.t[row0:row0 + ntok, :], vf.res))
            for half in range(2):
                cs = slice(half * 512, (half + 1) * 512)

                def sv(tn):
                    return V(tn.t[0:ntok, cs], tn.res)

                def proj(xin, w):
                    bb = self.bank()
                    for c in range(KC):
                        k.mm(bb[0:ntok, :], V(xin.t[:, c, 0:ntok], xin.res), V(w.t[:, c, cs], w.res), start=(c == 0), stop=(c == KC - 1))
                    return bb
                bb = proj(xr, Wr)
                k.cp("act", sv(st_r), bb[0:ntok, :])
                bb = proj(xk, Wk)
                k.cp("act", sv(st_k), bb[0:ntok, :])
                bb = proj(xv, Wv)
                k.cp("act", sv(st_v), bb[0:ntok, :])
                bb = self.bank()
                k.mm(bb[0:ntok, :], V(lo_w.t[0:64, 0:ntok], lo_w.res), V(w2.t[0:64, cs], w2.res))
                k.tt("dve", sv(st_w), bb[0:ntok, :], sv(w0b), ALU.add)
                k.act(sv(st_w), sv(st_w), AF.Sigmoid)
                bb = self.bank()
                k.mm(bb[0:ntok, :], V(lo_a.t[0:64, 0:ntok], lo_a.res), V(a2.t[0:64, cs], a2.res))
                k.tt("dve", sv(st_a), bb[0:ntok, :], sv(a0b), ALU.add)
                k.act(sv(st_a), sv(st_a), AF.Sigmoid)
                if has_v:
                    bb = self.bank()
                    k.mm(bb[0:ntok, :], V(lo_v.t[0:32, 0:ntok], lo_v.res), V(v2.t[0:32, cs], v2.res))
                    k.tt("dve", sv(st_z), bb[0:ntok, :], sv(v0b), ALU.add)
                    k.act(sv(st_z), sv(st_z), AF.Sigmoid)
                    k.tt("pool", sv(st_vf), sv(st_vf), sv(st_v), ALU.subtract)
                    k.tt("pool", sv(st_vf), sv(st_vf), sv(st_z), ALU.mult)
                    k.tt("pool", sv(st_v), sv(st_v), sv(st_vf), ALU.add)
                bb = proj(xg, Wz)
                k.act(sv(st_z), bb[0:ntok, :], AF.Silu)
                bb = self.bank()
                k.mm(bb[0:ntok, :], V(lo_ga.t[:, 0:ntok], lo_ga.res), V(g2a.t[:, cs], g2a.res), start=True, stop=False)
                k.mm(bb[0:ntok, :], V(lo_gb.t[0:32, 0:ntok], lo_gb.res), V(g2b.t[0:32, cs], g2b.res), start=False, stop=True)
                k.tt("dve", sv(st_g), bb[0:ntok, :], sv(st_z), ALU.mult)
            for nm, st in (("r", st_r), ("k", st_k), ("v", st_v), ("w", st_w), ("a", st_a), ("g", st_g)):
                d = scr[pref + nm]
                k.dma("sp", V(d.t[row0:row0 + ntok, :], d.res), V(st.t[0:ntok, :], st.res), semres=st.res)
                if nm == "v" and not has_v:
                    d = scr[pref + "vf"]
                    k.dma("sp", V(d.t[row0:row0 + ntok, :], d.res), V(st.t[0:ntok, :], st.res), semres=st.res)

        nt = T // 128
        for ti in range(nt):
            def so(hv4, last=(ti == nt - 1)):
                if last:
                    k.cp("pool", shcp[:, :], V(hv4[:, :, 0, 128], hT.res))
                    k.dma("sp", V(O["o_p_shift"].t[j], O["o_p_shift"].res), shcp[:, :], semres=shcp.res)
            tile(xT_p, ti * 128, 128, 1, 128, ti == 0, "p_", ti * 128, so)
        import os
        if NS > 0 and "s" not in os.environ.get("SKIP", ""):
            k.dma("sp", shiftT[:, :, 0:NBS], V(W["shift_fm"].t[j], W["shift_fm"].res))

            def so2(hv4):
                k.cp("pool", shc[:, :, 0:NBS], V(hv4[:, :, :, TS], hT.res))
                k.dma("sp", V(O["o_s_shift"].t[j], O["o_s_shift"].res), shc[:, :, 0:NBS], semres=shc.res)
            tile(xT_s, 0, NS, NBS, TS, True, "s_", 0, so2)

    def rwkv_phase2(self, j, li, xT_p, xT_s, W, scr, O):
        k = self.k
        T, NS, NBS, TS = self.T, self.NS, self.NBS, self.TS
        Wo = k.sb("Wo", [128, KC, D], BF16)
        with k.scope():
            stage = [k.sb("stg%d" % i, [128, 2048], F32) for i in range(2)]
            self.load_w_bf16(Wo, W["rw_w_o"].t[j], W["rw_w_o"].res, KC, D, stage)
        kkb = k.sb("kkb", [128, D])
        kab = k.sb("kab", [128, D])
        rkb = k.sb("rkb", [128, D])
        lnw = k.sb("lnw", [128, D])
        lnb = k.sb("lnb", [128, D])
        self.bcast_load(kkb, W["rw_k_k"].t[j], W["rw_k_k"].res)
        self.bcast_load(kab, W["rw_k_a"].t[j], W["rw_k_a"].res)
        self.bcast_load(rkb, W["rw_r_k"].t[j].rearrange("h c -> (h c)"), W["rw_r_k"].res)
        self.bcast_load(lnw, W["rw_ln_w"].t[j], W["rw_ln_w"].res)
        self.bcast_load(lnb, W["rw_ln_b"].t[j], W["rw_ln_b"].res)
        t_r = k.sb("t_r", [128, D])
        t_k = k.sb("t_k", [128, D])
        t_v = k.sb("t_v", [128, D])
        t_w = k.sb("t_w", [128, D])
        t_a = k.sb("t_a", [128, D])
        t_g = k.sb("t_g", [128, D])
        t_kk = k.sb("t_kk", [128, D])
        t_k2 = k.sb("t_k2", [128, D])
        t_bv = k.sb("t_bv", [128, D])
        t_tmp = k.sb("t_tmp", [128, D])
        t_y = k.sb("t_y", [128, D])
        ob = k.sb("ob", [128, D], BF16)
        oT = k.sb("oT", [128, KC, 128], BF16)
        xT = k.sb("xT", [128, KC, 128])
        sm = {n: k.sb("sm_" + n, [128, 16]) for n in ("ss", "rn", "s1", "s2", "mean", "msq", "var", "rk")}

        def hv(tn, n):
            return V(tn.t[0:n, :].rearrange("p (h c) -> p h c", c=64), tn.res)

        def bc(tn, n):
            return V(tn.t[0:n, :].unsqueeze(2).broadcast_to([n, 16, 64]), tn.res)

        def rows(tn, n):
            return V(tn.t[0:n, :], tn.res)

        def prep(pref, row0, n):
            for nm, tn in (("r", t_r), ("k", t_k), ("v", t_v), ("w", t_w), ("a", t_a), ("g", t_g)):
                d = scr[pref + nm]
                k.dma("sp", rows(tn, n), V(d.t[row0:row0 + n, :], d.res))
            k.tt("pool", rows(t_kk, n), rows(t_k, n), rows(kkb, n), ALU.mult)
            k.tt("pool", rows(t_tmp, n), rows(t_kk, n), rows(t_kk, n), ALU.mult)
            ss = V(sm["ss"].t[0:n, :], sm["ss"].res)
            rn = V(sm["rn"].t[0:n, :], sm["rn"].res)
            k.red("dve", ss, hv(t_tmp, n))
            k.ts("dve", ss, ss, 1e-24, op0=ALU.max)
            k.act(ss, ss, AF.Sqrt)
            k.op("dve", lambda g: g.reciprocal(rn.ap, ss.ap), [ss], [rn])
            k.tt("dve", hv(t_kk, n), hv(t_kk, n), bc(sm["rn"], n), ALU.mult)
            k.stt("pool", rows(t_tmp, n), rows(t_a, n), -1.0, rows(kab, n), ALU.add, ALU.mult)
            k.stt("pool", rows(t_k2, n), rows(t_tmp, n), 1.0, rows(t_k, n), ALU.add, ALU.mult)
            k.tt("pool", rows(t_bv, n), rows(t_kk, n), rows(t_a, n), ALU.mult)

        def post(ysrc, n, xT_d, col0, par=False):
            if par:
                ty4 = t_y.t[0:n, :].rearrange("p (c two n) -> p c two n", two=2, n=64)
                k.cp("act", V(ty4[:, :, 0, :], t_y.res), ysrc[0])
                k.cp("act", V(ty4[:, :, 1, :], t_y.res), ysrc[1])
            else:
                k.cp("act", V(t_y.t[0:n, 0:512], t_y.res), ysrc[0])
                k.cp("act", V(t_y.t[0:n, 512:1024], t_y.res), ysrc[1])
            s1 = V(sm["s1"].t[0:n, :], sm["s1"].res)
            s2 = V(sm["s2"].t[0:n, :], sm["s2"].res)
            mean = V(sm["mean"].t[0:n, :], sm["mean"].res)
            msq = V(sm["msq"].t[0:n, :], sm["msq"].res)
            var = V(sm["var"].t[0:n, :], sm["var"].res)
            rk = V(sm["rk"].t[0:n, :], sm["rk"].res)
            k.red("dve", s1, hv(t_y, n))
            k.tt("pool", rows(t_tmp, n), rows(t_y, n), rows(t_y, n), ALU.mult)
            k.red("dve", s2, hv(t_tmp, n))
            k.ts("dve", mean, s1, 1.0 / 64, op0=ALU.mult)
            k.tt("dve", msq, mean, mean, ALU.mult)
            k.stt("dve", var, s2, 1.0 / 64, msq, ALU.mult, ALU.subtract)
            k.ts("dve", var, var, GN_EPS, op0=ALU.add)
            k.act(var, var, AF.Sqrt)
            k.op("dve", lambda g: g.reciprocal(var.ap, var.ap), [var], [var])
            k.tt("dve", hv(t_y, n), hv(t_y, n), bc(sm["mean"], n), ALU.subtract)
            k.tt("dve", hv(t_y, n), hv(t_y, n), bc(sm["var"], n), ALU.mult)
            k.tt("pool", rows(t_y, n), rows(t_y, n), rows(lnw, n), ALU.mult)
            k.tt("pool", rows(t_y, n), rows(t_y, n), rows(lnb, n), ALU.add)
            k.tt("pool", rows(t_tmp, n), rows(t_r, n), rows(t_k2, n), ALU.mult)
            k.tt("pool", rows(t_tmp, n), rows(t_tmp, n), rows(rkb, n), ALU.mult)
            k.red("dve", rk, hv(t_tmp, n))
            k.tt("dve", hv(t_tmp, n), hv(t_v, n), bc(sm["rk"], n), ALU.mult)
            k.tt("pool", rows(t_y, n), rows(t_y, n), rows(t_tmp, n), ALU.add)
            k.tt("dve", V(ob.t[0:n, :], ob.res), rows(t_y, n), rows(t_g, n), ALU.mult)
            bb = self.bbank()
            for c in range(KC):
                k.tr(bb[:, c * 128:c * 128 + n], V(ob.t[0:n, c * 128:(c + 1) * 128], ob.res), V(self.identb.t[0:n, 0:n], self.identb.res))
            k.cp("act", V(oT.t[:, :, 0:n], oT.res), V(bb.t[:, :].rearrange("p (c t) -> p c t", t=128)[:, :, 0:n], bb.res))
            xTv = V(xT.t[:, :, 0:n], xT.res)
            k.dma("sp", xTv, V(xT_d.t[:, col0:col0 + n].rearrange("(c p) t -> p c t", p=128), xT_d.res))
            for hb in range(2):
                b = self.bank()
                for dq in range(4):
                    dc = hb * 4 + dq
                    for c in range(KC):
                        k.mm(b[:, dq * 128:dq * 128 + n], V(Wo.t[:, c, dc * 128:(dc + 1) * 128], Wo.res), V(oT.t[:, c, 0:n], oT.res), start=(c == 0), stop=(c == KC - 1))
                k.tt("dve", V(xT.t[:, hb * 4:hb * 4 + 4, 0:n], xT.res), V(xT.t[:, hb * 4:hb * 4 + 4, 0:n], xT.res),
                     V(b.t[:, :].rearrange("p (c t) -> p c t", t=128)[:, :, 0:n], b.res), ALU.add)
            k.dma("sp", V(xT_d.t[:, col0:col0 + n].rearrange("(c p) t -> p c t", p=128), xT_d.res), xTv, semres=xT.res)

        import os
        for _once in ([] if "P" in os.environ.get("SKIP", "") else [0]):
          with k.scope():
              rt = k.sb("rt", [128, D], BF16)
              at = k.sb("at", [128, D], BF16)
              kt = k.sb("kt", [128, D], BF16)
              bt = k.sb("bt", [128, D], BF16)
              kg = k.sb("kg", [128, D], BF16)
              bg = k.sb("bg", [128, D], BF16)
              vb = k.sb("vb", [128, D], BF16)
              RA = k.sb("RA", [128, KC, 256], BF16)
              KT = k.sb("KT", [128, KC, 128], BF16)
              BT = k.sb("BT", [128, KC, 128], BF16)
              gC = k.sb("gC", [128, KC])
              A1 = [k.sb("A1_%d" % g, [128, 4, 256], BF16) for g in range(4)]
              A2 = [k.sb("A2_%d" % g, [128, 4, 256], BF16) for g in range(4)]
              P = [[k.sb("P%d_%d" % (i, g), [128, 4, 128], BF16) for g in range(4)] for i in range(2)]
              PT = [[k.sb("PT%d_%d" % (i, g), [128, 4, 128], BF16) for g in range(4)] for i in range(2)]
              X = [k.sb("X_%d" % g, [128, 4, 128]) for g in range(4)]
              Zb = [k.sb("Zb_%d" % g, [128, 4, 128], BF16) for g in range(4)]
              AhT = k.sb("AhT", [128, KC, 128], BF16)
              Ub = k.sb("Ub", [128, 1024], BF16)
              S = k.sb("S", [128, KC, 64])
              Sb = k.sb("Sb", [128, KC, 64], BF16)
              k.memset("pool", S[:], 0.0)
              k.memset("pool", Sb[:], 0.0)
              STOP = int(os.environ.get("STOP", "99"))
              SUB = int(os.environ.get("SUB", "99"))
              for ti in range(T // 128):
                  n = 128
                  prep("p_", ti * 128, n)
                  if STOP <= 0:
                      continue
                  Ea, Eb = t_a, t_w
                  cb = []
                  for which in range(3):
                      for half in range(2):
                          b = self.bank()
                          k.mm(b[:, :], V(self.tri.t[:, which, :], self.tri.res), V(t_w.t[:, half * 512:(half + 1) * 512], t_w.res))
                          cb.append(b)
                  def hs(tn, half):
                      return V(tn.t[:, half * 512:(half + 1) * 512], tn.res)
                  for half in range(2):
                      k.act(hs(Ea, half), cb[half][:, :], AF.Exp)
                  k.tt("dve", rt[:], t_r[:], Ea[:], ALU.mult)
                  for half in range(2):
                      k.act(hs(Ea, half), cb[half][:, :], AF.Exp, scale=-1.0)
                  k.tt("pool", kt[:], t_k2[:], Ea[:], ALU.mult)
                  k.tt("pool", bt[:], t_bv[:], Ea[:], ALU.mult)
                  b = self.bank()
                  for c in range(KC):
                      k.mm(b[:, c:c + 1], V(t_w.t[:, c * 128:(c + 1) * 128], t_w.res), self.onescol[:])
                  k.act(gC[:], b[:, 0:KC], AF.Exp)
                  for half in range(2):
                      k.act(hs(Eb, half), cb[2 + half][:, :], AF.Exp)
                  k.stt("dve", at[:], t_kk[:], -1.0, Eb[:], ALU.mult, ALU.mult)
                  for half in range(2):
                      k.act(hs(Eb, half), cb[4 + half][:, :], AF.Exp)
                  k.tt("pool", Eb[:], Eb[:], Ea[:], ALU.mult)
                  k.tt("dve", kg[:], t_k2[:], Eb[:], ALU.mult)
                  k.tt("pool", bg[:], t_bv[:], Eb[:], ALU.mult)
                  k.cp("pool", vb[:], t_v[:])
                  if STOP <= 1:
                      continue
                  for src, dst, off in ((rt, RA, 0), (at, RA, 128), (kt, KT, 0), (bt, BT, 0)):
                      bb = self.bbank()
                      for c in range(KC):
                          k.tr(bb[:, c * 128:(c + 1) * 128], V(src.t[:, c * 128:(c + 1) * 128], src.res), self.identb[:])
                      k.cp("act", V(dst.t[:, :, off:off + 128], dst.res), V(bb.t[:, :].rearrange("p (c t) -> p c t", t=128), bb.res))
                  if STOP <= 2:
                      continue
                  for g in range(4):
                      for par in range(2):
                          hb = 64 * par
                          b1 = self.bank()
                          b2 = self.bank()
                          for qq in range(2):
                              h = 4 * g + par + 2 * qq
                              c8 = h // 2
                              k.mm(b1[:, qq * 256:(qq + 1) * 256], V(KT.t[hb:hb + 64, c8, :], KT.res), V(RA.t[hb:hb + 64, c8, :], RA.res))
                              k.mm(b2[:, qq * 256:(qq + 1) * 256], V(BT.t[hb:hb + 64, c8, :], BT.res), V(RA.t[hb:hb + 64, c8, :], RA.res))
                          mIS = V(self.maskIS.t[:, :].unsqueeze(1).broadcast_to([128, 2, 256]), self.maskIS.res)
                          k.tt("dve", V(A1[g].t[:, par:4:2, :], A1[g].res), V(b1.t[:, :].rearrange("p (q n) -> p q n", n=256), b1.res), mIS, ALU.mult)
                          k.tt("dve", V(A2[g].t[:, par:4:2, :], A2[g].res), V(b2.t[:, :].rearrange("p (q n) -> p q n", n=256), b2.res), mIS, ALU.mult)
                      for par in range(2):
                          hb = 64 * par
                          b3 = self.bank()
                          for qq in range(2):
                              h = 4 * g + par + 2 * qq
                              c8 = h // 2
                              k.mm(b3[:, qq * 128:(qq + 1) * 128], V(RA.t[hb:hb + 64, c8, 128:256], RA.res), V(BT.t[hb:hb + 64, c8, :], BT.res))
                          mSL = V(self.maskSL.t[:, :].unsqueeze(1).broadcast_to([128, 2, 128]), self.maskSL.res)
                          k.tt("dve", V(P[0][g].t[:, par:4:2, :], P[0][g].res), V(b3.t[:, 0:256].rearrange("p (q n) -> p q n", n=128), b3.res), mSL, ALU.mult)
                  for g in range(4):
                      b4 = self.bank()
                      for q in range(4):
                          h = 4 * g + q
                          k.mm(b4[:, q * 64:(q + 1) * 64], V(A1[g].t[:, q, 128:256], A1[g].res), V(vb.t[:, h * 64:(h + 1) * 64], vb.res))
                      k.cp("act", V(X[g].t[:, :, 64:128], X[g].res), V(b4.t[:, 0:256].rearrange("p (q n) -> p q n", n=64), b4.res))
                      k.cp("pool", V(X[g].t[:, :, 0:64], X[g].res), V(at.t[:, g * 256:(g + 1) * 256].rearrange("p (q n) -> p q n", n=64), at.res))
                      k.cp("pool", Zb[g][:], X[g][:])
                  if STOP <= 3:
                      continue
                  for st in range(7):
                      for g in range(4):
                          cur, nxt = st % 2, (st + 1) % 2
                          Pk = P[cur][g]
                          if st == 0:
                              PTk_v = lambda q, g=g: V(A2[g].t[:, q, 128:256], A2[g].res)
                          else:
                              PTk_v = lambda q, g=g, cur=cur: V(PT[cur][g].t[:, q, :], PT[cur][g].res)
                          bz = self.bank()
                          for q in range(4):
                              k.mm(bz[:, q * 128:(q + 1) * 128], PTk_v(q), V(Zb[g].t[:, q, :], Zb[g].res))
                          if st < 6:
                              bp = self.bank()
                              bpt = self.bank()
                              for q in range(4):
                                  k.mm(bp[:, q * 128:(q + 1) * 128], PTk_v(q), V(Pk.t[:, q, :], Pk.res))
                                  k.mm(bpt[:, q * 128:(q + 1) * 128], V(Pk.t[:, q, :], Pk.res), PTk_v(q))
                          k.tt("dve", X[g][:], X[g][:], V(bz.t[:, :].rearrange("p (q n) -> p q n", n=128), bz.res), ALU.add)
                          k.cp("pool", Zb[g][:], X[g][:])
                          if st < 6:
                              k.cp("act", P[nxt][g][:], V(bp.t[:, :].rearrange("p (q n) -> p q n", n=128), bp.res))
                              k.cp("act", PT[nxt][g][:], V(bpt.t[:, :].rearrange("p (q n) -> p q n", n=128), bpt.res))
                  if STOP <= 4:
                      continue
                  bbs = [self.bbank(), self.bbank()]
                  for h in range(16):
                      g, q = h // 4, h % 4
                      zf = Zb[g].t[:, :, :].rearrange("p q n -> p (q n)")
                      lo = q * 128 - (64 if h % 2 == 1 else 0)
                      k.tr(bbs[h // 8][:, (h % 8) * 128:(h % 8 + 1) * 128], V(zf[:, lo:lo + 128], Zb[g].res), self.identb[:])
                  for bi in range(2):
                      bv3 = bbs[bi].t[:, :].rearrange("p (c two t) -> p c two t", two=2, t=128)
                      k.cp("act", V(AhT.t[0:64, 4 * bi:4 * bi + 4, :], AhT.res), V(bv3[0:64, :, 0, :], bbs[bi].res))
                      k.cp("act", V(AhT.t[64:128, 4 * bi:4 * bi + 4, :], AhT.res), V(bv3[64:128, :, 1, :], bbs[bi].res))
                  if STOP <= 5:
                      continue
                  bU = [self.bank(), self.bank()]
                  for h in range(16):
                      c8, par = h // 2, h % 2
                      hb = 64 * par
                      k.mm(bU[par][:, c8 * 64:(c8 + 1) * 64], V(AhT.t[hb:hb + 64, c8, :], AhT.res), V(Sb.t[hb:hb + 64, c8, :], Sb.res))
                  Ub4 = Ub.t[:, :].rearrange("p (c two n) -> p c two n", two=2, n=64)
                  for g in range(4):
                      for par in range(2):
                          k.tt("dve", V(Ub4[:, 2 * g:2 * g + 2, par, :], Ub.res),
                               V(bU[par].t[:, :].rearrange("p (c n) -> p c n", n=64)[:, 2 * g:2 * g + 2, :], bU[par].res),
                               V(X[g].t[:, par:4:2, 64:128], X[g].res), ALU.add)
                  bY = [self.bank(), self.bank()]
                  for h in range(16):
                      c8, par = h // 2, h % 2
                      hb = 64 * par
                      g, q = h // 4, h % 4
                      o = bY[par][:, c8 * 64:(c8 + 1) * 64]
                      k.mm(o, V(RA.t[hb:hb + 64, c8, 0:128], RA.res), V(Sb.t[hb:hb + 64, c8, :], Sb.res), start=True, stop=False)
                      k.mm(o, V(A2[g].t[:, q, 0:128], A2[g].res), V(Ub.t[:, h * 64:(h + 1) * 64], Ub.res), start=False, stop=False)
                      k.mm(o, V(A1[g].t[:, q, 0:128], A1[g].res), V(vb.t[:, h * 64:(h + 1) * 64], vb.res), start=False, stop=True)
                  bS = [self.bank(), self.bank()]
                  for c8 in range(KC):
                      o = bS[c8 // 4][:, (c8 % 4) * 128:(c8 % 4 + 1) * 128]
                      k.mm(o, V(bg.t[:, c8 * 128:(c8 + 1) * 128], bg.res), V(Ub.t[:, c8 * 128:(c8 + 1) * 128], Ub.res), start=True, stop=False)
                      k.mm(o, V(kg.t[:, c8 * 128:(c8 + 1) * 128], kg.res), V(vb.t[:, c8 * 128:(c8 + 1) * 128], vb.res), start=False, stop=True)
                  for hh in range(2):
                      hb = 64 * hh
                      k.tt("pool", V(S.t[hb:hb + 64, :, :], S.res), V(S.t[hb:hb + 64, :, :], S.res),
                           V(gC.t[hb:hb + 64, :].unsqueeze(2).broadcast_to([64, KC, 64]), gC.res), ALU.mult)
                      for bi in range(2):
                          k.tt("dve", V(S.t[hb:hb + 64, 4 * bi:4 * bi + 4, :], S.res), V(S.t[hb:hb + 64, 4 * bi:4 * bi + 4, :], S.res),
                               V(bS[bi].t[hb:hb + 64, :].rearrange("p (c n) -> p c n", n=128)[:, :, hb:hb + 64], bS[bi].res), ALU.add)
                  k.cp("act", Sb[:], S[:])
                  if STOP <= 6:
                      continue
                  post([V(bY[p_].t[:, :].rearrange("p (c n) -> p c n", n=64), bY[p_].res) for p_ in range(2)], 128, xT_p, ti * 128, par=True)
              k.dma("sp", V(O["o_p_wkv"].t[j], O["o_p_wkv"].res), S[:], semres=S.res)

        import os
        for _once in ([] if "S" in os.environ.get("SKIP", "") else [0]):
          with k.scope():
              n = NS
              NP = NBS * 8
              prep("s_", 0, n)
              k.act(rows(t_w, n), rows(t_w, n), AF.Exp, scale=NEG_C)
              k.ts("pool", rows(t_kk, n), rows(t_kk, n), -1.0, op0=ALU.mult)
              sc = scr["s_scan"]
              for qi, tn in enumerate((t_r, t_w, t_k2, t_v, t_kk, t_bv)):
                  k.dma("sp", V(sc.t[qi], sc.res), rows(tn, n), semres=tn.res)
              Ss = k.sb("Ss", [128, 2, 64, 64])
              tmp = k.sb("Stmp", [128, 2, 64, 64])
              qin = k.sb("qin", [128, 6, TS, 128])
              sa = k.sb("sa", [128, 2, 64])
              ys = k.sb("ys", [128, TS, 128])
              st_in = W["state_wkv"]
              for g in range(8):
                  k.dma("sp", V(Ss.t[g * NBS:(g + 1) * NBS], Ss.res),
                        V(st_in.t[j][:, 2 * g:2 * g + 2], st_in.res))
                  for qi in range(6):
                      k.dma("sp", V(qin.t[g * NBS:(g + 1) * NBS, qi], qin.res),
                            V(sc.t[qi][:, g * 128:(g + 1) * 128].rearrange("(b t) c -> b t c", t=TS), sc.res))

              def bi_(qi, t):
                  return V(qin.t[0:NP, qi, t, :].rearrange("p (h c) -> p h c", c=64).unsqueeze(2).broadcast_to([NP, 2, 64, 64]), qin.res)

              def bj_(ap, res):
                  return V(ap.unsqueeze(3).broadcast_to([NP, 2, 64, 64]), res)
              Sv = V(Ss.t[0:NP], Ss.res)
              Tv = V(tmp.t[0:NP], tmp.res)
              sav = V(sa.t[0:NP], sa.res)
              for t in range(TS):
                  k.tt("dve", Tv, Sv, bi_(4, t), ALU.mult)
                  k.red("dve", sav, Tv)
                  k.tt("pool", Sv, Sv, bi_(1, t), ALU.mult)
                  k.tt("dve", Tv, bj_(sa.t[0:NP], sa.res), bi_(5, t), ALU.mult)
                  k.tt("pool", Sv, Sv, Tv, ALU.add)
                  k.tt("dve", Tv, bj_(qin.t[0:NP, 3, t, :].rearrange("p (h c) -> p h c", c=64), qin.res), bi_(2, t), ALU.mult)
                  k.tt("pool", Sv, Sv, Tv, ALU.add)
                  k.tt("dve", Tv, Sv, bi_(0, t), ALU.mult)
                  k.red("dve", V(ys.t[0:NP, t, :].rearrange("p (h c) -> p h c", c=64), ys.res), Tv)
              yd = scr["s_y"]
              for g in range(8):
                  k.dma("sp", V(O["o_s_wkv"].t[j][:, 2 * g:2 * g + 2], O["o_s_wkv"].res), V(Ss.t[g * NBS:(g + 1) * NBS], Ss.res), semres=Ss.res)
                  k.dma("sp", V(yd.t[:, g * 128:(g + 1) * 128].rearrange("(b t) c -> b t c", t=TS), yd.res), V(ys.t[g * NBS:(g + 1) * NBS], ys.res), semres=ys.res)
              k.dma("sp", rows(t_tmp, n), V(yd.t[:, :], yd.res))
              k.cp("pool", rows(t_kk, n), rows(t_tmp, n))
              post([V(t_kk.t[0:n, 0:512], t_kk.res), V(t_kk.t[0:n, 512:1024], t_kk.res)], n, xT_s, 0)


OA = 2176
LA = 6656
RS1 = LA + 128
RS16 = LA + 2048
OW = 128
LW = 1024
RSW = LW + 128
NEGBIG = -30000.0
NIN = 3632


def rel_bucket_np(d):
    n = np.maximum(d, 0)
    nf = np.maximum(n, 1).astype(np.float32)
    large = 16 + (np.log(nf / np.float32(16)) / np.float32(np.log(1024 / 16)) * np.float32(16)).astype(np.int32)
    large = np.minimum(large, 31)
    return np.where(n < 16, n, large)


def nsa_host_consts(T, TS=4, PAST=2048):
    import ml_dtypes
    bf = ml_dtypes.bfloat16
    c = {}
    dA = np.arange(LA) - OA
    oh = np.zeros((33, LA), np.float32)
    bA = rel_bucket_np(dA)
    oh[bA, np.arange(LA)] = (dA >= 0)
    oh[32] = (dA < 0)
    c["c_ohA"] = oh
    dW = np.arange(LW) - OW
    ohw = np.zeros((33, LW), np.float32)
    okw = (dW >= 0) & (dW < 512)
    ohw[rel_bucket_np(dW), np.arange(LW)] = okw
    ohw[32] = ~okw
    c["c_ohW"] = ohw
    ntile = max(T // 128, (PAST + TS + 127) // 128)
    ex = np.zeros((64, ntile, 128), np.float32)
    for kt in range(ntile):
        for cc in range(128):
            s = 2 * kt + cc // 64
            if s < 64:
                ex[s, kt, cc] = 1
    c["c_expand"] = ex.astype(bf)
    exs = np.zeros((64, 2, 128), np.float32)
    for p_ in range(128):
        exs[p_ // 4, 0, p_] = 1
    exs[32, 1, :] = 1
    c["c_expand_s"] = exs.astype(bf)
    c["c_s8"] = (np.arange(128) % 8).astype(np.int32).reshape(128, 1)
    c["_ntile"] = ntile
    nq = T // 128
    fm = np.zeros((max(nq, 1), 128, 64), np.float32)
    for qt in range(nq):
        qpos = qt * 128 + np.arange(128)
        cur = qpos // 64
        blk = np.arange(64)[None, :]
        forced = (blk == 0) | (blk == cur[:, None]) | (blk == cur[:, None] - 1)
        future = (blk * 64) > qpos[:, None]
        fm[qt] = np.where(future, -1e4, np.where(forced, 1e4, 0.0))
    c["c_fm_p"] = fm
    qpos = PAST + np.arange(TS)
    cur = qpos // 64
    blk = np.arange(64)[None, :]
    forced = (blk == 0) | (blk == cur[:, None]) | (blk == cur[:, None] - 1)
    future = (blk * 64) > qpos[:, None]
    c["c_fm_s"] = np.where(future, -1e4, np.where(forced, 1e4, 0.0)).astype(np.float32)
    def ov(n_cmp, n_sel):
        cs = np.arange(n_cmp)[:, None] * 16
        ss = np.arange(n_sel)[None, :] * 64
        o = np.maximum(np.minimum(cs + 32, ss + 64) - np.maximum(cs, ss), 0)
        return o.astype(np.float32) / 32
    ovp = np.zeros((256, 63), np.float32)
    if T >= 64:
        ncp = (T - 32) // 16 + 1
        ovp[:ncp, :T // 64 - 1] = ov(ncp, T // 64)[:, 1:]
    c["c_ov_p"] = ovp.reshape(2, 128, 63).transpose(1, 0, 2).copy()
    L = PAST + TS
    ncs = (L - 32) // 16 + 1
    nss = -(-L // 64)
    ovs = np.zeros((128, 63), np.float32)
    ovs[:ncs, :nss - 1] = ov(ncs, nss)[:, 1:]
    c["c_ov_s"] = ovs
    return c


def _nsa_setup_tables(self, W):
    k = self.k
    import concourse.bass as bass
    self.tz1 = k.dram("tz1", [16, 128, RS1], BF16)
    self.tz16 = k.dram("tz16", [16, 128, RS16], BF16)
    self.tzw = k.dram("tzw", [16, 128, RSW], BF16)
    with k.scope():
        relb = k.sb("relb", [33, 16])
        k.memset("pool", relb[:], NEGBIG)
        k.dma("sp", relb[0:32, :], W["rel_bias"][:, :])
        RB = k.sb("RB", [33, 16, 128])
        k.cp("dve", RB[:], V(relb.t[:, :].unsqueeze(2).broadcast_to([33, 16, 128]), relb.res))
        ohA = k.sb("ohA", [33, LA])
        ohW = k.sb("ohW", [33, LW])
        k.dma("sp", ohA[:], self.din["c_ohA"][:, :])
        k.dma("sp", ohW[:], self.din["c_ohW"][:, :])
        R = [k.sb("Rrow%d" % i, [128, LA], BF16) for i in range(2)]
        for h in range(16):
            Rr = R[h % 2]
            for ci in range(LA // 512):
                b = self.bank()
                k.mm(b[:, :], V(RB.t[:, h, :], RB.res), V(ohA.t[:, ci * 512:(ci + 1) * 512], ohA.res))
                k.cp("act" if ci % 2 == 0 else "dve", V(Rr.t[:, ci * 512:(ci + 1) * 512], Rr.res), b[:, :])
            d1 = bass.AP(self.tz1.t.tensor, h * 128 * RS1, [[RS1 + 1, 128], [1, LA]])
            k.dma("sp", V(d1, self.tz1.res), Rr[:, :], semres=Rr.res)
            d16 = bass.AP(self.tz16.t.tensor, h * 128 * RS16, [[RS16 + 16, 128], [1, LA]])
            k.dma("sp", V(d16, self.tz16.res), Rr[:, :], semres=Rr.res)
        Rw = [k.sb("Rw%d" % i, [128, LW], BF16) for i in range(2)]
        for h in range(16):
            Rr = Rw[h % 2]
            for ci in range(LW // 512):
                b = self.bank()
                k.mm(b[:, :], V(RB.t[:, h, :], RB.res), V(ohW.t[:, ci * 512:(ci + 1) * 512], ohW.res))
                k.cp("act" if ci % 2 == 0 else "dve", V(Rr.t[:, ci * 512:(ci + 1) * 512], Rr.res), b[:, :])
            dw = bass.AP(self.tzw.t.tensor, h * 128 * RSW, [[RSW + 1, 128], [1, LW]])
            k.dma("sp", V(dw, self.tzw.res), Rr[:, :], semres=Rr.res)


def _bias_tile(self, dst, tz, rs, x0, nk, tqn):
    import concourse.bass as bass
    src = bass.AP(tz.t.tensor, x0, [[rs, nk], [128 * rs, 16], [1, tqn]])
    self.k.dma("sp", dst, V(src, tz.res))


MK.nsa_setup_tables = _nsa_setup_tables
MK.bias_tile = _bias_tile


def _nsa_phase1(self, jn, li, xT_p, xT_s, W, scr, O):
    k = self.k
    T, NS = self.T, self.NS
    Win = k.sb("Win", [128, KC, NIN], BF16)
    with k.scope():
        stage = [k.sb("stg%d" % i, [128, 2048], F32) for i in range(2)]
        wsrc = W["nsa_w_in"].t[jn].rearrange("(c p) n -> p c n", p=128)
        i = 0
        for c in range(KC):
            for half in range(2):
                st = stage[i % 2]
                cs = slice(half * 1816, (half + 1) * 1816)
                k.dma("sp", V(st.t[:, 0:1816], st.res), V(wsrc[:, c, cs], W["nsa_w_in"].res))
                k.cp("pool" if i % 2 == 0 else "act", V(Win.t[:, c, cs], Win.res), V(st.t[:, 0:1816], st.res))
                i += 1
    nw = k.sb("nw", [128, KC])
    k.dma("sp", nw[:], V(W["nw_fm"].t[li], W["nw_fm"].res))
    qwb = k.sb("qwb", [128, D])
    self.bcast_load(qwb, W["qn_t"].t[jn], W["qn_t"].res)
    k.ts("pool", qwb[:], qwb[:], 0.125, op0=ALU.mult)
    knb = k.sb("knb", [128, 2, 256])
    for i in range(2):
        k.dma("sp", knb[:, i, :], V(W["kn_t"].t[jn, i].partition_broadcast(128), W["kn_t"].res))
    xT = k.sb("xT", [128, KC, 128])
    sq = k.sb("sq", [128, KC, 128])
    rstd = k.sb("rstd", [128, 128])
    hTb = k.sb("hTb", [128, KC, 128], BF16)
    qf = k.sb("qf", [128, D])
    qnb = k.sb("qnb", [128, D], BF16)
    tmpq = k.sb("tmpq", [128, D])
    kv = [k.sb("kv%d" % i, [128, 512]) for i in range(3)]
    szt = k.sb("szt", [128, D])
    gt = k.sb("gt", [128, 48])
    ss = k.sb("ssq", [128, 16])

    def tile(xT_d, col0, ntok, pref, row0):
        xTv = V(xT.t[:, :, 0:ntok], xT.res)
        k.dma("sp", xTv, V(xT_d.t[:, col0:col0 + ntok].rearrange("(c p) t -> p c t", p=128), xT_d.res))
        sqv = V(sq.t[:, :, 0:ntok], sq.res)
        k.tt("pool", sqv, xTv, xTv, ALU.mult)
        b = self.bank()
        for c in range(KC):
            k.mm(b[:, 0:ntok], self.ones[:], V(sq.t[:, c, 0:ntok], sq.res), start=(c == 0), stop=(c == KC - 1))
        rs = V(rstd.t[:, 0:ntok], rstd.res)
        k.ts("dve", rs, b[:, 0:ntok], 1.0 / D, RMS_EPS, op0=ALU.mult, op1=ALU.add)
        k.act(rs, rs, AF.Sqrt)
        k.op("dve", lambda g: g.reciprocal(rs.ap, rs.ap), [rs], [rs])
        for c in range(KC):
            k.stt("dve", V(hTb.t[:, c, 0:ntok], hTb.res), V(xT.t[:, c, 0:ntok], xT.res), nw[:, c:c + 1], rs, ALU.mult, ALU.mult)
        for blk in range(8):
            c0 = blk * 512
            cw = min(512, NIN - c0)
            b = self.bank()
            for c in range(KC):
                k.mm(b[0:ntok, 0:cw], V(hTb.t[:, c, 0:ntok], hTb.res), V(Win.t[:, c, c0:c0 + cw], Win.res), start=(c == 0), stop=(c == KC - 1))
            if blk < 2:
                k.cp("act", V(qf.t[0:ntok, c0:c0 + 512], qf.res), b[0:ntok, :])
            elif blk < 5:
                k.cp("act", V(kv[blk - 2].t[0:ntok, :], kv[blk - 2].res), b[0:ntok, :])
            elif blk < 7:
                k.act(V(szt.t[0:ntok, (blk - 5) * 512:(blk - 4) * 512], szt.res), b[0:ntok, :], AF.Silu)
            else:
                k.act(V(gt.t[0:ntok, :], gt.res), b[0:ntok, 0:48], AF.Sigmoid)

        def headnorm(src_v, nh, dst_v, wb_v, eng2):
            n = ntok
            t3 = V(tmpq.t[0:n, 0:nh * 64], tmpq.res)
            k.tt("pool", t3, src_v, src_v, ALU.mult)
            ssv = V(ss.t[0:n, 0:nh], ss.res)
            k.red("dve", ssv, V(tmpq.t[0:n, 0:nh * 64].rearrange("p (h c) -> p h c", c=64), tmpq.res))
            k.ts("dve", ssv, ssv, 1.0 / 64, RMS_EPS, op0=ALU.mult, op1=ALU.add)
            k.act(ssv, ssv, AF.Sqrt)
            k.op("dve", lambda g: g.reciprocal(ssv.ap, ssv.ap), [ssv], [ssv])
            s3 = V(src_v.ap.rearrange("p (h c) -> p h c", c=64), src_v.res)
            k.tt("dve", s3, s3, V(ss.t[0:n, 0:nh].unsqueeze(2).broadcast_to([n, nh, 64]), ss.res), ALU.mult)
            k.tt(eng2, dst_v, src_v, wb_v, ALU.mult)
        headnorm(V(qf.t[0:ntok, :], qf.res), 16, V(qnb.t[0:ntok, :], qnb.res), V(qwb.t[0:ntok, :], qwb.res), "pool")
        headnorm(V(kv[1].t[0:ntok, 0:256], kv[1].res), 4, V(kv[1].t[0:ntok, 0:256], kv[1].res), V(knb.t[0:ntok, 0, :], knb.res), "pool")
        headnorm(V(kv[2].t[0:ntok, 0:256], kv[2].res), 4, V(kv[2].t[0:ntok, 0:256], kv[2].res), V(knb.t[0:ntok, 1, :], knb.res), "pool")
        rs_ = slice(row0, row0 + ntok)
        for nm, src in (("kc", V(kv[0].t[0:ntok, 0:256], kv[0].res)), ("vc", V(kv[0].t[0:ntok, 256:512], kv[0].res)),
                        ("ks", V(kv[1].t[0:ntok, 0:256], kv[1].res)), ("vs", V(kv[1].t[0:ntok, 256:512], kv[1].res)),
                        ("kw", V(kv[2].t[0:ntok, 0:256], kv[2].res)), ("vw", V(kv[2].t[0:ntok, 256:512], kv[2].res)),
                        ("qn", V(qnb.t[0:ntok, :], qnb.res)), ("sz", V(szt.t[0:ntok, :], szt.res)), ("gt", V(gt.t[0:ntok, :], gt.res))):
            d = scr[pref + nm]
            k.dma("sp", V(d.t[rs_, :], d.res), src, semres=src.res[0])

    for ti in range(T // 128):
        tile(xT_p, ti * 128, 128, "np_", ti * 128)
    import os
    if NS > 0 and "s" not in os.environ.get("SKIP", ""):
        tile(xT_s, 0, NS, "ns_", 0)


MK.nsa_phase1 = _nsa_phase1


def _nsa_cmp_weights(self, jn, W):
    k = self.k
    cw = {}
    with k.scope():
        stg = k.sb("cstg", [64, 2048])
        pstg = k.sb("pstg", [32, 64])
        pstb = k.sb("pstb", [32, 64], BF16)
        for kvi in range(2):
            W1 = self._cmpW1[kvi]
            src = W["nsa_cmp_w1"].t[jn, kvi].rearrange("(j d) e -> d j e", d=64)
            k.dma("sp", V(stg.t[:, :].rearrange("p (j e) -> p j e", e=64), stg.res), V(src, W["nsa_cmp_w1"].res))
            k.cp("pool", W1[:], V(stg.t[:, :].rearrange("p (j e) -> p j e", e=64), stg.res))
            W2 = self._cmpW2[kvi]
            k.dma("sp", V(stg.t[:, 0:64], stg.res), V(W["nsa_cmp_w2"].t[jn, kvi], W["nsa_cmp_w2"].res))
            k.cp("pool", W2[:], V(stg.t[:, 0:64], stg.res))
            k.dma("sp", pstg[:], V(W["nsa_cmp_pos"].t[jn, kvi], W["nsa_cmp_pos"].res))
            k.cp("pool", pstb[:], pstg[:])
            bb = self.bbank()
            k.tr(bb[0:64, 0:32], pstb[:], V(self.identb.t[0:32, 0:32], self.identb.res))
            posT = self._cmpPos[kvi]
            k.cp("act", posT[:], bb[0:64, 0:32])
            b = self.bank()
            for j in range(32):
                k.mm(b[0:64, 0:1], V(W1.t[:, j, :], W1.res), V(posT.t[:, j:j + 1], posT.res), start=(j == 0), stop=(j == 31))
            k.cp("act", self._cmpBias[kvi][:], b[0:64, 0:1])
        k.dma("sp", self._kn2[:], V(W["kn2_col"].t[jn], W["kn2_col"].res))


def _nsa_alloc_cmp(self):
    k = self.k
    self._cmpW1 = [k.sb("cW1_%d" % i, [64, 32, 64], BF16) for i in range(2)]
    self._cmpW2 = [k.sb("cW2_%d" % i, [64, 64], BF16) for i in range(2)]
    self._cmpPos = [k.sb("cPos_%d" % i, [64, 32], BF16) for i in range(2)]
    self._cmpBias = [k.sb("cBias_%d" % i, [64, 1]) for i in range(2)]
    self._kn2 = k.sb("kn2", [64, 1])
    self._chT = k.sb("chT", [64, 256], BF16)
    self._csq = k.sb("csq", [64, 256])
    self._crs = k.sb("crs", [64, 256])


def _nsa_compress(self, kcT, vcT, NC, kcmpT, vcmpX):
    k = self.k
    hT, sqt, rs = self._chT, self._csq, self._crs
    for kvi, src in enumerate((kcT, vcT)):
        W1, W2, bias = self._cmpW1[kvi], self._cmpW2[kvi], self._cmpBias[kvi]
        for kvh in range(4):
            b = self.bank()
            for j in range(32):
                k.mm(b[0:64, 0:NC], V(W1.t[:, j, :], W1.res), V(src.t[:, kvh, j:j + 16 * (NC - 1) + 1:16], src.res), start=(j == 0), stop=(j == 31))
            k.act(V(hT.t[:, 0:NC], hT.res), b[0:64, 0:NC], AF.Silu, bias=bias[:, 0:1])
            if kvi == 0:
                b2 = self.bank()
                k.mm(b2[0:64, 0:NC], W2[:], V(hT.t[:, 0:NC], hT.res))
                k.act(V(sqt.t[:, 0:NC], sqt.res), b2[0:64, 0:NC], AF.Square)
                b3 = self.bank()
                k.mm(b3[0:64, 0:NC], V(self.ones.t[0:64, 0:64], self.ones.res), V(sqt.t[:, 0:NC], sqt.res))
                rsv = V(rs.t[:, 0:NC], rs.res)
                k.ts("dve", rsv, b3[0:64, 0:NC], 1.0 / 64, RMS_EPS, op0=ALU.mult, op1=ALU.add)
                k.act(rsv, rsv, AF.Sqrt)
                k.op("dve", lambda g: g.reciprocal(rsv.ap, rsv.ap), [rsv], [rsv])
                k.stt("dve", V(kcmpT.t[:, kvh, 0:NC], kcmpT.res), b2[0:64, 0:NC], self._kn2[:, 0:1], rsv, ALU.mult, ALU.mult)
            else:
                for ci in range((NC + 127) // 128):
                    nk = min(128, NC - ci * 128)
                    b2 = self.bank()
                    k.mm(b2[0:nk, 0:64], V(hT.t[:, ci * 128:ci * 128 + nk], hT.res), W2[:])
                    k.cp("act", V(vcmpX.t[0:nk, ci, kvh, 0:64], vcmpX.res), b2[0:nk, 0:64])


MK.nsa_cmp_weights = _nsa_cmp_weights
MK.nsa_alloc_cmp = _nsa_alloc_cmp
MK.nsa_compress = _nsa_compress


def _nsa_alloc_attn(self, tqn):
    k = self.k
    A = {}
    A["tqn"] = tqn
    NQ = 4 * tqn
    A["Eb"] = [k.sb("Eb%d" % i, [128, NQ], BF16) for i in range(3)]
    A["Ei"] = 0
    A["oacc"] = k.sb("oacc", [128, 16, 64])
    A["otmp"] = k.sb("otmp", [128, 4, 64])
    A["den"] = k.sb("den", [128, 4])
    A["coef"] = k.sb("coef", [128, 4])
    A["imp"] = k.sb("imp", [128, 4, 64])
    A["sc"] = k.sb("sc", [128, 64])
    A["scw"] = k.sb("scw", [128, 64])
    A["m8"] = k.sb("m8", [128, 8])
    A["m8b"] = k.sb("m8b", [128, 8])
    A["m30f"] = k.sb("m30f", [128, 64])
    A["m30b"] = k.sb("m30b", [128, 4, 64], BF16)
    A["M30"] = k.sb("M30", [64, 4, 4, tqn], BF16)
    k.memset("pool", A["imp"][:], 0.0)
    return A


def _nsa_attend(self, A, qT, gates, fm, cmp_tiles, sel_tiles, win_tiles):
    k = self.k
    tqn = A["tqn"]
    NQ = 4 * tqn
    acc = self.banks[0:4]
    sbanks = self.banks[4:6]
    oacc = A["oacc"]
    g3 = gates.ap.rearrange("p (h r) -> p h r", r=3)

    def branch(br, tiles, Wd, use_m30):
        for kvh in range(4):
            tl = tiles[kvh]
            for ti, tdesc in enumerate(tl):
                nk = tdesc["nk"]
                sb = sbanks[self._sbi % 2]
                self._sbi += 1
                k.mm(sb[0:nk, 0:NQ], tdesc["kT"], V(qT.ap[:, kvh * NQ:(kvh + 1) * NQ], qT.res), start=True, stop=False)
                last_bias = not use_m30
                k.mm(sb[0:nk, 0:NQ], V(self.identb.t[0:nk, 0:nk], self.identb.res), tdesc["bias"], start=False, stop=last_bias)
                if use_m30:
                    M30 = A["M30"]
                    k.mm(sb[0:nk, 0:NQ], tdesc["expand"], V(M30.t[:, kvh].rearrange("p g t -> p (g t)"), M30.res), start=False, stop=True)
                Eb = A["Eb"][A["Ei"] % 3]
                A["Ei"] += 1
                k.act(V(Eb.t[0:nk, :], Eb.res), sb[0:nk, 0:NQ], AF.Exp)
                for g in range(4):
                    first = (ti == 0 and g == 0)
                    k.mm(acc[kvh][0:tqn, g * 128:g * 128 + Wd], V(Eb.t[0:nk, g * tqn:(g + 1) * tqn], Eb.res), tdesc["vX"],
                         start=first, stop=(ti == len(tl) - 1), sgc=True)
        for kvh in range(4):
            av = acc[kvh].t[0:tqn, :].rearrange("p (g n) -> p g n", n=128)
            den = V(A["den"].t[0:tqn, :], A["den"].res)
            coef = V(A["coef"].t[0:tqn, :], A["coef"].res)
            if len(tiles[kvh]) == 0:
                if br == 0:
                    k.memset("pool", V(oacc.t[0:tqn, 4 * kvh:4 * kvh + 4, :], oacc.res), 0.0)
                continue
            k.ts("dve", den, V(av[:, :, 64], acc[kvh].res), 1e-30, op0=ALU.max)
            k.op("dve", lambda g_: g_.reciprocal(den.ap, den.ap), [den], [den])
            k.tt("dve", coef, den, V(g3[:, 4 * kvh:4 * kvh + 4, br], gates.res), ALU.mult)
            cb = V(A["coef"].t[0:tqn, :].unsqueeze(2).broadcast_to([tqn, 4, 64]), A["coef"].res)
            ov_ = V(oacc.t[0:tqn, 4 * kvh:4 * kvh + 4, :], oacc.res)
            if br == 0:
                k.tt("dve", ov_, V(av[:, :, 0:64], acc[kvh].res), cb, ALU.mult)
            else:
                ot = V(A["otmp"].t[0:tqn], A["otmp"].res)
                k.tt("dve", ot, V(av[:, :, 0:64], acc[kvh].res), cb, ALU.mult)
                k.tt("pool", ov_, ov_, ot, ALU.add)
            if br == 0:
                impv = V(A["imp"].t[0:tqn, kvh, 1:64], A["imp"].res)
                for g in range(4):
                    if g == 0:
                        k.ts("dve", impv, V(av[:, g, 65:128], acc[kvh].res), V(A["den"].t[0:tqn, g:g + 1], A["den"].res), op0=ALU.mult)
                    else:
                        k.stt("dve", impv, V(av[:, g, 65:128], acc[kvh].res), V(A["den"].t[0:tqn, g:g + 1], A["den"].res), impv, ALU.mult, ALU.add)

    self._sbi = getattr(self, "_sbi", 0)
    branch(0, cmp_tiles, 128, False)
    for kvh in range(4):
        sc = V(A["sc"].t[0:tqn, :], A["sc"].res)
        scw = V(A["scw"].t[0:tqn, :], A["scw"].res)
        m8 = V(A["m8"].t[0:tqn, :], A["m8"].res)
        m8b = V(A["m8b"].t[0:tqn, :], A["m8b"].res)
        k.tt("dve", sc, V(A["imp"].t[0:tqn, kvh, :], A["imp"].res), fm, ALU.add)
        k.op("dve", lambda g_: g_.max(m8.ap, sc.ap), [sc], [m8])
        k.op("dve", lambda g_: g_.match_replace(scw.ap, m8.ap, sc.ap, -1e9), [m8, sc], [scw])
        k.op("dve", lambda g_: g_.max(m8b.ap, scw.ap), [scw], [m8b])
        m30f = V(A["m30f"].t[0:tqn, :], A["m30f"].res)
        k.ts("dve", m30f, sc, V(A["m8b"].t[0:tqn, 7:8], A["m8b"].res), op0=ALU.is_ge)
        k.ts("dve", V(A["m30b"].t[0:tqn, kvh, :], A["m30b"].res), m30f, -1.0, -NEGBIG, op0=ALU.add, op1=ALU.mult)
    bb = self.bbank()
    for kvh in range(4):
        k.tr(bb[0:64, kvh * 128:kvh * 128 + tqn], V(A["m30b"].t[0:tqn, kvh, :], A["m30b"].res), V(self.identb.t[0:tqn, 0:tqn], self.identb.res))
    M30 = A["M30"]
    k.cp("dve", M30[:], V(bb.t[0:64, 0:512].rearrange("p (v t) -> p v t", t=128)[:, :, 0:tqn].unsqueeze(2).broadcast_to([64, 4, 4, tqn]), bb.res))
    branch(1, sel_tiles, 65, True)
    branch(2, win_tiles, 65, False)


MK.nsa_alloc_attn = _nsa_alloc_attn
MK.nsa_attend = _nsa_attend


def _outproj(self, ob, n, xT_d, col0, Wo, xT, oT):
    k = self.k
    bb = self.bbank()
    for c in range(KC):
        k.tr(bb[:, c * 128:c * 128 + n], V(ob.t[0:n, c * 128:(c + 1) * 128], ob.res), V(self.identb.t[0:n, 0:n], self.identb.res))
    k.cp("act", V(oT.t[:, :, 0:n], oT.res), V(bb.t[:, :].rearrange("p (c t) -> p c t", t=128)[:, :, 0:n], bb.res))
    xTv = V(xT.t[:, :, 0:n], xT.res)
    k.dma("sp", xTv, V(xT_d.t[:, col0:col0 + n].rearrange("(c p) t -> p c t", p=128), xT_d.res))
    for hb in range(2):
        b = self.bank()
        for dq in range(4):
            dc = hb * 4 + dq
            for c in range(KC):
                k.mm(b[:, dq * 128:dq * 128 + n], V(Wo.t[:, c, dc * 128:(dc + 1) * 128], Wo.res), V(oT.t[:, c, 0:n], oT.res), start=(c == 0), stop=(c == KC - 1))
        k.tt("dve", V(xT.t[:, hb * 4:hb * 4 + 4, 0:n], xT.res), V(xT.t[:, hb * 4:hb * 4 + 4, 0:n], xT.res),
             V(b.t[:, :].rearrange("p (c t) -> p c t", t=128)[:, :, 0:n], b.res), ALU.add)
    k.dma("sp", V(xT_d.t[:, col0:col0 + n].rearrange("(c p) t -> p c t", p=128), xT_d.res), xTv, semres=xT.res)


MK.outproj = _outproj


def _nsa_phase2_prompt(self, jn, li, xT_p, W, scr, O):
    k = self.k
    T = self.T
    NT = T // 128
    NC = (T - 32) // 16 + 1
    Wo = k.sb("Wo", [128, KC, D], BF16)
    with k.scope():
        stage = [k.sb("stg%d" % i, [128, 2048], F32) for i in range(2)]
        self.load_w_bf16(Wo, W["nsa_w_o"].t[jn], W["nsa_w_o"].res, KC, D, stage)
    kcmpT = k.sb("kcmpT", [64, 4, 256], BF16)
    vcmpX = k.sb("vcmpX", [128, 2, 4, 128], BF16)
    k.memset("pool", kcmpT[:], 0.0)
    k.memset("pool", vcmpX[:], 0.0)
    ovp = k.sb("ovp", [128, 2, 63])
    k.dma("sp", ovp[:], self.din["c_ov_p"][:])
    k.memset("pool", V(vcmpX.t[:, :, :, 64:65], vcmpX.res), 1.0)
    for kvh in range(4):
        k.cp("pool", V(vcmpX.t[:, :, kvh, 65:128], vcmpX.res), ovp[:])
    ldf = k.sb("ldf", [128, 512])
    ldb = k.sb("ldb", [128, 512], BF16)

    def build_T(dst, names, ntiles):
        for ti in range(ntiles):
            for i, nm in enumerate(names):
                d = scr[nm]
                k.dma("sp", V(ldf.t[:, i * 256:(i + 1) * 256], ldf.res), V(d.t[ti * 128:(ti + 1) * 128, :], d.res))
            w = 256 * len(names)
            k.cp("pool", V(ldb.t[:, 0:w], ldb.res), V(ldf.t[:, 0:w], ldf.res))
            for i, nm in enumerate(names):
                bb = self.bbank()
                for kvh in range(4):
                    k.tr(bb[0:64, kvh * 128:(kvh + 1) * 128], V(ldb.t[:, i * 256 + kvh * 64:i * 256 + (kvh + 1) * 64], ldb.res), self.identb[:])
                k.cp("act", V(dst[i].t[:, :, ti * 128:(ti + 1) * 128], dst[i].res), V(bb.t[0:64, 0:512].rearrange("p (v t) -> p v t", t=128), bb.res))
    with k.scope():
        self.nsa_alloc_cmp()
        self.nsa_cmp_weights(jn, W)
        kcT = k.sb("kcT", [64, 4, T], BF16)
        vcT = k.sb("vcT", [64, 4, T], BF16)
        build_T([kcT, vcT], ["np_kc", "np_vc"], NT)
        self.nsa_compress(kcT, vcT, NC, kcmpT, vcmpX)
    ksT = k.sb("ksT", [64, 4, T], BF16)
    build_T([ksT], ["np_ks"], NT)
    vsX = k.sb("vsX", [128, NT, 4, 65], BF16)
    k.memset("pool", V(vsX.t[:, :, :, 64:65], vsX.res), 1.0)
    for ti in range(NT):
        d = scr["np_vs"]
        k.dma("sp", V(ldf.t[:, 0:256], ldf.res), V(d.t[ti * 128:(ti + 1) * 128, :], d.res))
        k.cp("pool", V(vsX.t[:, ti, :, 0:64], vsX.res), V(ldf.t[:, 0:256].rearrange("p (v c) -> p v c", c=64), ldf.res))
    kwT = k.sb("kwT", [64, 4, 5 * 128], BF16)
    vwX = k.sb("vwX", [128, 5, 4, 65], BF16)
    k.memset("pool", V(vwX.t[:, :, :, 64:65], vwX.res), 1.0)
    selB = k.sb("selB", [128, 10, 16 * 128], BF16)
    winB = k.sb("winB", [128, 5, 16 * 128], BF16)
    for dlt in range(10):
        self.bias_tile(V(selB.t[:, dlt, :].rearrange("p (h t) -> p h t", t=128), selB.res), self.tz1, RS1, OA + 128 * dlt, 128, 128)
    for dlt in range(5):
        self.bias_tile(V(winB.t[:, dlt, :].rearrange("p (h t) -> p h t", t=128), winB.res), self.tzw, RSW, OW + 128 * dlt, 128, 128)
    cmpB = [k.sb("cmpB%d" % i, [128, 16 * 128], BF16) for i in range(2)]
    nex = self.din["c_expand"].t.shape[1]
    expand = k.sb("expand", [64, nex, 128], BF16)
    k.dma("sp", expand[:], self.din["c_expand"][:])
    fm = k.sb("fm", [128, 64])
    A = self.nsa_alloc_attn(128)
    qn = k.sb("qn", [128, D], BF16)
    qT = k.sb("qT", [64, 16 * 128], BF16)
    gt = k.sb("gt", [128, 48])
    szt = k.sb("szt", [128, D])
    ob = k.sb("ob", [128, D], BF16)
    oT = k.sb("oT", [128, KC, 128], BF16)
    xT = k.sb("xT", [128, KC, 128])
    for qt in range(NT):
        rs_ = slice(qt * 128, (qt + 1) * 128)
        k.dma("sp", qn[:], V(scr["np_qn"].t[rs_, :], scr["np_qn"].res))
        k.dma("sp", gt[:], V(scr["np_gt"].t[rs_, :], scr["np_gt"].res))
        k.dma("sp", szt[:], V(scr["np_sz"].t[rs_, :], scr["np_sz"].res))
        k.dma("sp", fm[:], V(self.din["c_fm_p"].t[qt], self.din["c_fm_p"].res))
        for half in range(2):
            bb = self.bbank()
            for hh in range(8):
                h = half * 8 + hh
                k.tr(bb[0:64, hh * 128:(hh + 1) * 128], V(qn.t[:, h * 64:(h + 1) * 64], qn.res), self.identb[:])
            k.cp("act", V(qT.t[:, half * 1024:(half + 1) * 1024], qT.res), bb[0:64, :])
        slot = qt % 5
        k.dma("sp", V(ldf.t[:, 0:256], ldf.res), V(scr["np_kw"].t[rs_, :], scr["np_kw"].res))
        k.dma("sp", V(ldf.t[:, 256:512], ldf.res), V(scr["np_vw"].t[rs_, :], scr["np_vw"].res))
        k.cp("pool", V(ldb.t[:, 0:256], ldb.res), V(ldf.t[:, 0:256], ldf.res))
        k.cp("pool", V(vwX.t[:, slot, :, 0:64], vwX.res), V(ldf.t[:, 256:512].rearrange("p (v c) -> p v c", c=64), ldf.res))
        bb = self.bbank()
        for kvh in range(4):
            k.tr(bb[0:64, kvh * 128:(kvh + 1) * 128], V(ldb.t[:, kvh * 64:(kvh + 1) * 64], ldb.res), self.identb[:])
        k.cp("act", V(kwT.t[:, :, slot * 128:(slot + 1) * 128], kwT.res), V(bb.t[0:64, 0:512].rearrange("p (v t) -> p v t", t=128), bb.res))
        cmp_tiles = [[] for _ in range(4)]
        sel_tiles = [[] for _ in range(4)]
        win_tiles = [[] for _ in range(4)]
        for c in range((NC + 127) // 128):
            if 8 * qt + 6 < 128 * c:
                continue
            nk = min(128, NC - 128 * c)
            cb = cmpB[c]
            self.bias_tile(V(cb.t[0:nk, :].rearrange("p (h t) -> p h t", t=128), cb.res), self.tz16, RS16, OA + 128 * qt - 31 - 2048 * c, nk, 128)
            for kvh in range(4):
                cmp_tiles[kvh].append(dict(nk=nk, kT=V(kcmpT.t[:, kvh, c * 128:c * 128 + nk], kcmpT.res), vX=V(vcmpX.t[0:nk, c, kvh, :], vcmpX.res),
                                           bias=V(cb.t[0:nk, kvh * 512:(kvh + 1) * 512], cb.res)))
        for kt in range(qt + 1):
            dlt = min(qt - kt, 9)
            for kvh in range(4):
                sel_tiles[kvh].append(dict(nk=128, kT=V(ksT.t[:, kvh, kt * 128:(kt + 1) * 128], ksT.res), vX=V(vsX.t[:, kt, kvh, :], vsX.res),
                                           bias=V(selB.t[:, dlt, kvh * 512:(kvh + 1) * 512], selB.res), expand=V(expand.t[:, kt, :], expand.res)))
        for kt in range(max(0, qt - 4), qt + 1):
            sl = kt % 5
            for kvh in range(4):
                win_tiles[kvh].append(dict(nk=128, kT=V(kwT.t[:, kvh, sl * 128:(sl + 1) * 128], kwT.res), vX=V(vwX.t[:, sl, kvh, :], vwX.res),
                                           bias=V(winB.t[:, qt - kt, kvh * 512:(kvh + 1) * 512], winB.res)))
        self.nsa_attend(A, qT[:, :], gt[:, :], fm[:, :], cmp_tiles, sel_tiles, win_tiles)
        k.tt("dve", ob[:], V(A["oacc"].t[:, :, :].rearrange("p h c -> p (h c)"), A["oacc"].res), szt[:], ALU.mult)
        self.outproj(ob, 128, xT_p, qt * 128, Wo, xT, oT)
    wb = min(512, T)
    k.dma("sp", V(O["o_p_win_k"].t[jn], O["o_p_win_k"].res), V(scr["np_kw"].t[T - wb:T, :], scr["np_kw"].res))
    k.dma("sp", V(O["o_p_win_v"].t[jn], O["o_p_win_v"].res), V(scr["np_vw"].t[T - wb:T, :], scr["np_vw"].res))


MK.nsa_phase2_prompt = _nsa_phase2_prompt


def _nsa_phase2_sample(self, jn, li, xT_s, W, scr, O):
    k = self.k
    nc = k.nc
    import concourse.bass as bass
    NBS, TS, NS = self.NBS, self.TS, self.NS
    PAST = 2048
    NPG = PAST // 128
    L = PAST + TS
    NC = (L - 32) // 16 + 1
    NKT = NPG + 1
    Wo = k.sb("Wo", [128, KC, D], BF16)
    with k.scope():
        stage = [k.sb("stg%d" % i, [128, 2048], F32) for i in range(2)]
        self.load_w_bf16(Wo, W["nsa_w_o"].t[jn], W["nsa_w_o"].res, KC, D, stage)
    self.nsa_alloc_cmp()
    self.nsa_cmp_weights(jn, W)
    kcmpT = k.sb("kcmpT", [64, 4, 256], BF16)
    vcmpX = k.sb("vcmpX", [128, 2, 4, 128], BF16)
    k.memset("pool", kcmpT[:], 0.0)
    k.memset("pool", vcmpX[:], 0.0)
    ovs = k.sb("ovs", [128, 63])
    k.dma("sp", ovs[:], self.din["c_ov_s"][:, :])
    k.memset("pool", V(vcmpX.t[:, :, :, 64:65], vcmpX.res), 1.0)
    for kvh in range(4):
        k.cp("pool", V(vcmpX.t[:, 0, kvh, 65:128], vcmpX.res), ovs[:])
    pts = k.sb("pts", [128, NBS], I32)
    s8s = k.sb("s8s", [128, 1], I32)
    idx = k.sb("idx", [128, NBS], I32)
    idf = k.sb("idf", [128, NBS])
    s8f = k.sb("s8f", [128, 1])
    k.dma("sp", pts[:], W["pt8T"][:, :])
    k.dma("sp", s8s[:], self.din["c_s8"][:, :])
    k.cp("dve", idf[:], pts[:])
    k.cp("dve", s8f[:], s8s[:])
    k.ts("dve", idf[:], idf[:], 8.0, s8f[:, 0:1], op0=ALU.mult, op1=ALU.add)
    k.cp("dve", idx[:], idf[:])
    pg = [k.sb("pgbuf%d" % i, [128, NPG + (0 if i < 2 else 1), 256]) for i in range(4)]
    ldbs = [k.sb("ldb%d" % i, [128, 256], BF16) for i in range(4)]
    ldbi = [0]

    def next_ldb():
        ldbi[0] += 1
        return ldbs[ldbi[0] % 4]
    kcT = k.sb("kcT", [64, 4, PAST], BF16)
    vcT = k.sb("vcT", [64, 4, PAST], BF16)
    ksT = k.sb("ksT", [64, 4, NKT * 128], BF16)
    vsX = k.sb("vsX", [128, NKT, 4, 65], BF16)
    k.memset("pool", V(vsX.t[:, :, :, 64:65], vsX.res), 1.0)
    wbuf = [k.sb("wbuf%d" % i, [128, 5, 256]) for i in range(2)]
    kwT = k.sb("kwT", [64, 4, 5 * 128], BF16)
    vwX = k.sb("vwX", [128, 5, 4, 65], BF16)
    k.memset("pool", V(vwX.t[:, :, :, 64:65], vwX.res), 1.0)
    selB = k.sb("selBs", [128, NKT, 16 * TS], BF16)
    winB = k.sb("winBs", [128, 5, 16 * TS], BF16)
    cmpB = k.sb("cmpBs", [128, 16 * TS], BF16)
    for kt in range(NKT):
        if kt < NPG:
            self.bias_tile(V(selB.t[:, kt, :].rearrange("p (h t) -> p h t", t=TS), selB.res), self.tz16, RS16, OA + PAST - kt, 128, TS)
        else:
            self.bias_tile(V(selB.t[0:TS, kt, :].rearrange("p (h t) -> p h t", t=TS), selB.res), self.tz1, RS1, OA, TS, TS)
    for wt in range(5):
        nk = 128 if wt < 4 else TS
        self.bias_tile(V(winB.t[0:nk, wt, :].rearrange("p (h t) -> p h t", t=TS), winB.res), self.tzw, RSW, OW + 512 - 128 * wt, nk, TS)
    self.bias_tile(V(cmpB.t[0:NC, :].rearrange("p (h t) -> p h t", t=TS), cmpB.res), self.tz16, RS16, OA + PAST - 31, NC, TS)
    expand = k.sb("expand", [64, 2, 128], BF16)
    k.dma("sp", expand[:], self.din["c_expand_s"][:])
    fm = k.sb("fms", [TS, 64])
    k.dma("sp", fm[:], self.din["c_fm_s"][:, :])
    A = self.nsa_alloc_attn(TS)
    qn = k.sb("qn", [TS, D], BF16)
    qT = k.sb("qT", [64, 16 * TS], BF16)
    gt = k.sb("gt", [TS, 48])
    szt = k.sb("szt", [TS, D])
    obs = k.sb("obs", [TS, D], BF16)
    caches = [W["cache_cmp_k"], W["cache_cmp_v"], W["cache_sel_k"], W["cache_sel_v"]]
    newn = ["ns_kc", "ns_vc", "ns_ks", "ns_vs"]
    NPOOL = caches[0].t.shape[1]
    for bl in range(NBS):
        rs_ = slice(bl * TS, (bl + 1) * TS)
        for ci in range(4):
            src = caches[ci].t.rearrange("l n (s t) v c -> (l n s) (t v c)", s=8)
            k.idma(V(pg[ci].t[:, 0:NPG, :].rearrange("p t c -> p (t c)"), pg[ci].res), src, caches[ci].res, idx[:, bl:bl + 1],
                   element_offset=jn * NPOOL * 8 * 4096)
        for ci in range(2, 4):
            d = scr[newn[ci]]
            k.dma("sp", V(pg[ci].t[0:TS, NPG, :], pg[ci].res), V(d.t[rs_, :], d.res))
        for ci, dst in ((0, kcT), (1, vcT), (2, ksT)):
            for pi in range(NKT):
                nr = 128 if pi < NPG else TS
                if pi == NPG and ci < 2:
                    continue
                ldb = next_ldb()
                k.cp("pool" if pi % 2 == 0 else "dve", V(ldb.t[0:nr, :], ldb.res), V(pg[ci].t[0:nr, pi, :], pg[ci].res))
                bb = self.bbank()
                for kvh in range(4):
                    k.tr(bb[0:64, kvh * 128:kvh * 128 + nr], V(ldb.t[0:nr, kvh * 64:(kvh + 1) * 64], ldb.res), V(self.identb.t[0:nr, 0:nr], self.identb.res))
                if ci < 2:
                    dv = V(dst.t[:, :, pi:PAST:16], dst.res)
                else:
                    dv = V(dst.t[:, :, pi * 128:pi * 128 + nr], dst.res)
                k.cp("act", dv, V(bb.t[0:64, 0:512].rearrange("p (v t) -> p v t", t=128)[:, :, 0:nr], bb.res))
        for pi in range(NKT):
            nr = 128 if pi < NPG else TS
            k.cp("pool", V(vsX.t[0:nr, pi, :, 0:64], vsX.res), V(pg[3].t[0:nr, pi, :].rearrange("p (v c) -> p v c", c=64), pg[3].res))
        self.nsa_compress(kcT, vcT, NC, kcmpT, vcmpX)
        for wi, (stn, nn) in enumerate((("state_win_k", "ns_kw"), ("state_win_v", "ns_vw"))):
            k.dma("sp", V(wbuf[wi].t[:, 0:4, :], wbuf[wi].res), V(W[stn].t[jn, bl].rearrange("(t p) c -> p t c", p=128), W[stn].res))
            k.dma("sp", V(wbuf[wi].t[0:TS, 4, :], wbuf[wi].res), V(scr[nn].t[rs_, :], scr[nn].res))
        for wt in range(5):
            nr = 128 if wt < 4 else TS
            ldb = next_ldb()
            k.cp("pool", V(ldb.t[0:nr, :], ldb.res), V(wbuf[0].t[0:nr, wt, :], wbuf[0].res))
            bb = self.bbank()
            for kvh in range(4):
                k.tr(bb[0:64, kvh * 128:kvh * 128 + nr], V(ldb.t[0:nr, kvh * 64:(kvh + 1) * 64], ldb.res), V(self.identb.t[0:nr, 0:nr], self.identb.res))
            k.cp("act", V(kwT.t[:, :, wt * 128:wt * 128 + nr], kwT.res), V(bb.t[0:64, 0:512].rearrange("p (v t) -> p v t", t=128)[:, :, 0:nr], bb.res))
            k.cp("pool", V(vwX.t[0:nr, wt, :, 0:64], vwX.res), V(wbuf[1].t[0:nr, wt, :].rearrange("p (v c) -> p v c", c=64), wbuf[1].res))
        for wi, on in enumerate(("o_s_win_k", "o_s_win_v")):
            od = O[on]
            for wt in range(4):
                lo = wt * 128 - TS
                if wt == 0:
                    k.dma("sp", V(od.t[jn, bl, 0:128 - TS, :], od.res), V(wbuf[wi].t[TS:128, 0, :], wbuf[wi].res), semres=wbuf[wi].res)
                else:
                    k.dma("sp", V(od.t[jn, bl, lo:lo + 128, :], od.res), V(wbuf[wi].t[:, wt, :], wbuf[wi].res), semres=wbuf[wi].res)
            k.dma("sp", V(od.t[jn, bl, 512 - TS:512, :], od.res), V(wbuf[wi].t[0:TS, 4, :], wbuf[wi].res), semres=wbuf[wi].res)
        k.dma("sp", qn[:], V(scr["ns_qn"].t[rs_, :], scr["ns_qn"].res))
        k.dma("sp", gt[:], V(scr["ns_gt"].t[rs_, :], scr["ns_gt"].res))
        k.dma("sp", szt[:], V(scr["ns_sz"].t[rs_, :], scr["ns_sz"].res))
        bb = self.bbank()
        for h in range(16):
            k.tr(bb[0:64, h * TS:(h + 1) * TS], V(qn.t[:, h * 64:(h + 1) * 64], qn.res), V(self.identb.t[0:TS, 0:TS], self.identb.res))
        k.cp("act", qT[:, :], bb[0:64, 0:16 * TS])
        cmp_tiles = [[dict(nk=NC, kT=V(kcmpT.t[:, kvh, 0:NC], kcmpT.res), vX=V(vcmpX.t[0:NC, 0, kvh, :], vcmpX.res),
                           bias=V(cmpB.t[0:NC, kvh * 4 * TS:(kvh + 1) * 4 * TS], cmpB.res))] for kvh in range(4)]
        sel_tiles = [[] for _ in range(4)]
        win_tiles = [[] for _ in range(4)]
        for kt in range(NKT):
            nk = 128 if kt < NPG else TS
            for kvh in range(4):
                sel_tiles[kvh].append(dict(nk=nk, kT=V(ksT.t[:, kvh, kt * 128:kt * 128 + nk], ksT.res), vX=V(vsX.t[0:nk, kt, kvh, :], vsX.res),
                                           bias=V(selB.t[0:nk, kt, kvh * 4 * TS:(kvh + 1) * 4 * TS], selB.res), expand=V(expand.t[:, 0 if kt < NPG else 1, 0:nk], expand.res)))
        for wt in range(5):
            nk = 128 if wt < 4 else TS
            for kvh in range(4):
                win_tiles[kvh].append(dict(nk=nk, kT=V(kwT.t[:, kvh, wt * 128:wt * 128 + nk], kwT.res), vX=V(vwX.t[0:nk, wt, kvh, :], vwX.res),
                                           bias=V(winB.t[0:nk, wt, kvh * 4 * TS:(kvh + 1) * 4 * TS], winB.res)))
        self.nsa_attend(A, qT[:, :], gt[:, :], fm[:, :], cmp_tiles, sel_tiles, win_tiles)
        k.tt("dve", obs[:], V(A["oacc"].t[0:TS, :, :].rearrange("p h c -> p (h c)"), A["oacc"].res), szt[:], ALU.mult)
        k.dma("sp", V(scr["ns_ob"].t[rs_, :], scr["ns_ob"].res), obs[:], semres=obs.res)
    ob = k.sb("ob", [128, D], BF16)
    oT = k.sb("oT", [128, KC, 128], BF16)
    xT = k.sb("xT", [128, KC, 128])
    k.dma("sp", V(ob.t[0:NS, :], ob.res), V(scr["ns_ob"].t[:, :], scr["ns_ob"].res))
    self.outproj(ob, NS, xT_s, 0, Wo, xT, oT)


MK.nsa_phase2_sample = _nsa_phase2_sample


T_FULL = 4096
NBS_FULL = 16
NCORES = 8
RW_NAMES = ["rw_w_rkvz", "rw_w0", "rw_w1", "rw_w2", "rw_a0", "rw_a1", "rw_a2", "rw_v0", "rw_v1", "rw_v2", "rw_g1", "rw_g2",
            "rw_k_k", "rw_k_a", "rw_r_k", "rw_ln_w", "rw_ln_b", "rw_w_o"]
NSA_NAMES = ["nsa_w_in", "nsa_cmp_pos", "nsa_cmp_w1", "nsa_cmp_w2", "nsa_w_o", "rel_bias"]
CACHE_NAMES = ["cache_cmp_k", "cache_cmp_v", "cache_sel_k", "cache_sel_v"]


def build_full(T, NBS, shapes, n_layers=4):
    m = MK(T, NBS)
    k = m.k
    NS = m.NS
    W = {}
    for n in RW_NAMES + NSA_NAMES + CACHE_NAMES:
        W[n] = m.inp(n, list(shapes[n]))
    W["mu_fm"] = m.inp("mu_fm", [2, 128, 6, 8])
    W["nw_fm"] = m.inp("nw_fm", [4, 128, 8])
    W["shift_fm"] = m.inp("shift_fm", [2, 128, 8, NBS])
    W["state_wkv"] = m.inp("state_wkv", [2, NBS, 16, 64, 64])
    W["qn_t"] = m.inp("qn_t", [2, 1024])
    W["kn_t"] = m.inp("kn_t", [2, 3, 256])
    W["kn2_col"] = m.inp("kn2_col", [2, 64, 1])
    W["pt8T"] = m.inp("pt8T", [128, NBS], I32)
    W["state_win_k"] = m.inp("state_win_k", [2, NBS, 512, 256])
    W["state_win_v"] = m.inp("state_win_v", [2, NBS, 512, 256])
    hc = nsa_host_consts(T)
    for n_ in ("c_ohA", "c_ohW", "c_expand", "c_expand_s", "c_s8", "c_fm_p", "c_fm_s", "c_ov_p", "c_ov_s"):
        a = hc[n_]
        dt = I32 if a.dtype == np.int32 else (BF16 if a.dtype != np.float32 else F32)
        m.inp(n_, list(a.shape), dt)
    xin_p = m.inp("xT_p_in", [1024, T])
    xin_s = m.inp("xT_s_in", [1024, NS])
    xo_p = m.outp("xT_p", [1024, T])
    xo_s = m.outp("xT_s", [1024, NS])
    k.dma("sp", xo_p[:, :], xin_p[:, :])
    k.dma("sp", xo_s[:, :], xin_s[:, :])
    wb = min(512, T)
    O = {"o_p_shift": m.outp("o_p_shift", [2, 128, 8]), "o_s_shift": m.outp("o_s_shift", [2, 128, 8, NBS]),
         "o_p_wkv": m.outp("o_p_wkv", [2, 128, 8, 64]), "o_s_wkv": m.outp("o_s_wkv", [2, NBS, 16, 64, 64]),
         "o_p_win_k": m.outp("o_p_win_k", [2, wb, 256]), "o_p_win_v": m.outp("o_p_win_v", [2, wb, 256]),
         "o_s_win_k": m.outp("o_s_win_k", [2, NBS, 512, 256]), "o_s_win_v": m.outp("o_s_win_v", [2, NBS, 512, 256])}
    scr = {}
    for pref, n in (("p_", T), ("s_", NS)):
        for nm in ("r", "k", "v", "w", "a", "g", "vf"):
            scr[pref + nm] = k.dram("scr_" + pref + nm, [n, 1024])
    scr["s_scan"] = k.dram("scr_s_scan", [6, NS, 1024])
    scr["s_y"] = k.dram("scr_s_y", [NS, 1024])
    nscr = []
    for jn in range(2):
        d = {}
        for nm in ("kc", "vc", "ks", "vs"):
            d["np_" + nm] = m.outp("o_p_%s_%d" % (nm, jn), [T, 256])
            d["ns_" + nm] = m.outp("o_s_%s_%d" % (nm, jn), [NS, 256])
        nscr.append(d)
    shared = {}
    for pref, n in (("np_", T), ("ns_", NS)):
        for nm in ("kw", "vw"):
            shared[pref + nm] = k.dram("scr_" + pref + nm, [n, 256])
        shared[pref + "qn"] = k.dram("scr_" + pref + "qn", [n, 1024], BF16)
        shared[pref + "sz"] = k.dram("scr_" + pref + "sz", [n, 1024])
        shared[pref + "gt"] = k.dram("scr_" + pref + "gt", [n, 48])
    shared["ns_ob"] = k.dram("scr_ns_ob", [NS, 1024], BF16)
    m.nsa_setup_tables(W)
    for li in range(n_layers):
        j = li // 2
        if li % 2 == 0:
            with k.scope():
                m.rwkv_phase1(j, li, xo_p, xo_s, W, scr, O)
            with k.scope():
                m.rwkv_phase2(j, li, xo_p, xo_s, W, scr, O)
        else:
            s2 = dict(shared)
            s2.update(nscr[j])
            with k.scope():
                m.nsa_phase1(j, li, xo_p, xo_s, W, s2, O)
            with k.scope():
                m.nsa_phase2_prompt(j, li, xo_p, W, s2, O)
            with k.scope():
                m.nsa_phase2_sample(j, li, xo_s, W, s2, O)
    nc = k.finish()
    return m, nc, hc


def host_prep(inputs, T, NBS, ncores, hc):
    f32 = np.float32
    common = dict(host_consts())
    for n_ in ("c_ohA", "c_ohW", "c_expand", "c_expand_s", "c_s8", "c_fm_p", "c_fm_s", "c_ov_p", "c_ov_s"):
        common[n_] = hc[n_]
    for n in RW_NAMES + NSA_NAMES + CACHE_NAMES:
        common[n] = np.ascontiguousarray(np.asarray(inputs[n], dtype=f32))
    mu = np.asarray(inputs["rw_mu"], f32)
    common["mu_fm"] = np.ascontiguousarray(mu.reshape(2, 6, 8, 128).transpose(0, 3, 1, 2))
    nw = np.asarray(inputs["norm_w"], f32)
    common["nw_fm"] = np.ascontiguousarray(nw.reshape(4, 8, 128).transpose(0, 2, 1))
    qn = np.asarray(inputs["nsa_q_norm"], f32)
    common["qn_t"] = np.ascontiguousarray(np.tile(qn, (1, 16)))
    kn = np.asarray(inputs["nsa_k_norm"], f32)
    common["kn_t"] = np.ascontiguousarray(np.tile(kn, (1, 1, 4)))
    common["kn2_col"] = np.ascontiguousarray(kn[:, 2, :, None])
    xp = np.asarray(inputs["x_prompt"], f32)
    xs = np.asarray(inputs["x_sample"], f32)
    sh = np.asarray(inputs["state_shift"], f32)
    swkv = np.asarray(inputs["state_wkv"], f32)
    pt = np.asarray(inputs["page_table"]).astype(np.int32)
    wk = np.asarray(inputs["state_win_k"], f32)
    wv = np.asarray(inputs["state_win_v"], f32)
    NS = NBS * 4
    maps = []
    for c in range(ncores):
        d = dict(common)
        b = c % xp.shape[0]
        bs = slice(c * NBS, (c + 1) * NBS)
        d["xT_p_in"] = np.ascontiguousarray(xp[b, :T].T)
        d["xT_s_in"] = np.ascontiguousarray(xs[bs].reshape(NS, 1024).T)
        d["shift_fm"] = np.ascontiguousarray(sh[:, bs].reshape(2, NBS, 8, 128).transpose(0, 3, 2, 1))
        d["state_wkv"] = np.ascontiguousarray(swkv[:, bs])
        d["pt8T"] = np.ascontiguousarray(np.repeat(pt[bs], 8, axis=1).T)
        d["state_win_k"] = np.ascontiguousarray(wk[:, bs].reshape(2, NBS, 512, 256))
        d["state_win_v"] = np.ascontiguousarray(wv[:, bs].reshape(2, NBS, 512, 256))
        maps.append(d)
    return maps


def assemble(results, T, NBS, ncores, nb_prompt):
    f32 = np.float32
    NS = NBS * 4
    B = nb_prompt
    DB = ncores * NBS
    y_p = np.zeros((B, T, 1024), f32)
    y_s = np.zeros((DB, 4, 1024), f32)
    p_wkv = np.zeros((2, B, 16, 64, 64), f32)
    p_shift = np.zeros((2, B, 1024), f32)
    p_kv = [np.zeros((2, B, T, 4, 64), f32) for _ in range(4)]
    wb = min(512, T)
    p_win = [np.zeros((2, B, wb, 4, 64), f32) for _ in range(2)]
    s_wkv = np.zeros((2, DB, 16, 64, 64), f32)
    s_shift = np.zeros((2, DB, 1024), f32)
    s_kv = [np.zeros((2, DB, 4, 4, 64), f32) for _ in range(4)]
    s_win = [np.zeros((2, DB, 512, 4, 64), f32) for _ in range(2)]
    for c in range(ncores):
        r = results[c]
        bs = slice(c * NBS, (c + 1) * NBS)
        y_s[bs] = r["xT_s"].T.reshape(NBS, 4, 1024)
        s_wkv[:, bs] = r["o_s_wkv"]
        s_shift[:, bs] = r["o_s_shift"].transpose(0, 3, 2, 1).reshape(2, NBS, 1024)
        for i, nm in enumerate(("kc", "vc", "ks", "vs")):
            for jn in range(2):
                s_kv[i][jn, bs] = r["o_s_%s_%d" % (nm, jn)].reshape(NBS, 4, 4, 64)
        s_win[0][:, bs] = r["o_s_win_k"].reshape(2, NBS, 512, 4, 64)
        s_win[1][:, bs] = r["o_s_win_v"].reshape(2, NBS, 512, 4, 64)
        if c < B:
            b = c
            y_p[b] = r["xT_p"].T
            for j in range(2):
                p_wkv[j, b] = r["o_p_wkv"][j].reshape(2, 64, 8, 64).transpose(2, 0, 3, 1).reshape(16, 64, 64)
                p_shift[j, b] = r["o_p_shift"][j].transpose(1, 0).reshape(1024)
            for i, nm in enumerate(("kc", "vc", "ks", "vs")):
                for jn in range(2):
                    p_kv[i][jn, b] = r["o_p_%s_%d" % (nm, jn)].reshape(T, 4, 64)
            p_win[0][:, b] = r["o_p_win_k"].reshape(2, wb, 4, 64)
            p_win[1][:, b] = r["o_p_win_v"].reshape(2, wb, 4, 64)
    return (y_p, y_s, p_wkv, p_shift, p_kv[0], p_kv[1], p_kv[2], p_kv[3], p_win[0], p_win[1],
            s_wkv, s_shift, s_kv[0], s_kv[1], s_kv[2], s_kv[3], s_win[0], s_win[1])


_CACHE = {}


def kernel(**inputs):
    from concourse.bass_utils import run_bass_kernel_spmd
    T, NBS = T_FULL, NBS_FULL
    shapes = {n: tuple(np.asarray(inputs[n]).shape) for n in RW_NAMES + NSA_NAMES + CACHE_NAMES}
    key = (T, NBS)
    if key not in _CACHE:
        _CACHE[key] = build_full(T, NBS, shapes)
    m, nc, hc = _CACHE[key]
    maps = host_prep(inputs, T, NBS, NCORES, hc)
    res = run_bass_kernel_spmd(nc, maps, core_ids=list(range(NCORES)))
    return assemble(res.results, T, NBS, NCORES, np.asarray(inputs["x_prompt"]).shape[0])
```

```python
import contextlib
import numpy as np
import concourse.bass as bass
import concourse.mybir as mybir

F32 = mybir.dt.float32
BF16 = mybir.dt.bfloat16
I32 = mybir.dt.int32
AF = mybir.ActivationFunctionType
ALU = mybir.AluOpType
AX = mybir.AxisListType


class Sem:
    def __init__(self, h, is_dma):
        self.h = h
        self.is_dma = is_dma
        self.total = 0


class Res:
    __slots__ = ("w", "r", "dsem", "name", "isem")

    def __init__(self, name=""):
        self.w = None
        self.r = {}
        self.dsem = None
        self.isem = None
        self.name = name


class V:
    __slots__ = ("ap", "res")

    def __init__(self, ap, res):
        self.ap = ap
        self.res = res if isinstance(res, (list, tuple)) else [res]


class Tn:
    def __init__(self, t, name, res=None):
        self.t = t
        self.name = name
        self.res = res if res is not None else Res(name)

    def __getitem__(self, k):
        return V(self.t[k], self.res)

    def v(self, ap):
        return V(ap, self.res)


class KB:
    def __init__(self):
        self.nc = bass.Bass("TRN2", target_bir_lowering=False)
        nc = self.nc
        self.es = contextlib.ExitStack()
        self.engs = {"pe": nc.tensor, "dve": nc.vector, "act": nc.scalar, "pool": nc.gpsimd, "sp": nc.sync}
        self.sems = {}
        for e in ("pe", "dve", "act", "pool"):
            self.sems[e] = Sem(self.es.enter_context(nc.semaphore("s_" + e)), False)
        self.cnt = {e: 0 for e in self.engs}
        self.waited = {e: {} for e in self.engs}
        self.dsems = []
        self.n_dsem = 0
        self.max_dsem = 80
        self.nins = 0
        self.sbuf_bytes = 0

    def sb(self, name, shape, dt=F32):
        self._uid = getattr(self, "_uid", 0) + 1
        st = self.scopes[-1] if getattr(self, "scopes", None) else self.es
        t = st.enter_context(self.nc.sbuf_tensor("sb%d_%s" % (self._uid, name), list(shape), dt))
        n = 1
        for s in shape[1:]:
            n *= s
        self.sbuf_bytes += n * (4 if dt in (F32, I32) else 2)
        return Tn(t, name)

    @contextlib.contextmanager
    def scope(self):
        if not hasattr(self, "scopes"):
            self.scopes = []
        st = contextlib.ExitStack()
        self.scopes.append(st)
        b0 = self.sbuf_bytes
        try:
            yield
        finally:
            self.barrier()
            self.peak = max(getattr(self, "peak", 0), self.sbuf_bytes)
            self.sbuf_bytes = b0
            self.scopes.pop()
            st.close()

    def barrier(self):
        for e in ("pe", "dve", "act", "pool", "sp"):
            eng = self.engs[e]
            wd = self.waited[e]
            for e2 in ("pe", "dve", "act", "pool"):
                if e2 == e and e == "pe":
                    continue
                s = self.sems[e2]
                v = self.cnt[e2]
                if v > 0 and wd.get(s, 0) < v:
                    eng.wait_ge(s.h, v)
                    wd[s] = v
            for s in self.dsems:
                if s.total > 0 and wd.get(s, 0) < s.total:
                    eng.wait_ge(s.h, s.total)
                    wd[s] = s.total

    def ps(self, name, shape, dt=F32):
        t = self.es.enter_context(self.nc.psum_tensor("ps_" + name, list(shape), dt))
        return Tn(t, name)

    def dram(self, name, shape, dt=F32, kind="Internal"):
        t = self.nc.dram_tensor(name, list(shape), dt, kind=kind).ap()
        return Tn(t, name)

    def new_dsem(self):
        if self.n_dsem < self.max_dsem:
            s = Sem(self.es.enter_context(self.nc.semaphore("d%d" % self.n_dsem)), True)
            self.dsems.append(s)
            self.n_dsem += 1
            return s
        s = self.dsems[self.n_dsem % self.max_dsem]
        self.n_dsem += 1
        return s

    def _waits(self, e, R, W):
        need = {}

        def add(ev):
            if ev is None:
                return
            s, v = ev
            if e == "pe" and s is self.sems["pe"]:
                return
            if need.get(s, 0) < v:
                need[s] = v

        for r in R:
            add(r.w)
        for w in W:
            add(w.w)
            for s, v in w.r.items():
                add((s, v))
        eng = self.engs[e]
        wd = self.waited[e]
        for s, v in need.items():
            if s.is_dma:
                v = s.total
            if wd.get(s, 0) < v:
                eng.wait_ge(s.h, v)
                wd[s] = v
                self.nins += 1

    def _done(self, ev, R, W):
        for r in R:
            r.r[ev[0]] = ev[1]
        for w in W:
            w.w = ev
            w.r = {}

    def op(self, e, fn, R, W):
        R = [x for v in R for x in v.res]
        W = [x for v in W for x in v.res]
        self._waits(e, R, W)
        ins = fn(self.engs[e])
        self.cnt[e] += 1
        self.nins += 1
        ins.then_inc(self.sems[e].h, 1)
        self._done((self.sems[e], self.cnt[e]), R, W)

    def dma(self, q, out, in_, semres=None, **kw):
        R = list(in_.res)
        W = list(out.res)
        self._waits(q, R, W)
        ins = self.engs[q].dma_start(out=out.ap, in_=in_.ap, **kw)
        sr = semres if semres is not None else (out.res[0])
        if sr.dsem is None:
            sr.dsem = self.new_dsem()
        s = sr.dsem
        s.total += 16
        ins.then_inc(s.h, 16)
        self.nins += 1
        self._done((s, s.total), R, W)

    def idma(self, out, in_ap, in_res, idx, element_offset=0):
        import concourse.bass as bass
        R = list(idx.res) + [in_res]
        W = list(out.res)
        self._waits("pool", R, W)
        ins = self.engs["pool"].indirect_dma_start(out=out.ap, out_offset=None, in_=in_ap,
                                                   in_offset=bass.IndirectOffsetOnAxis(ap=idx.ap, axis=0), element_offset=element_offset)
        sr = out.res[0]
        if sr.isem is None:
            sr.isem = Sem(self.es.enter_context(self.nc.semaphore("i%d" % len(self.dsems))), True)
            self.dsems.append(sr.isem)
        s = sr.isem
        s.total += 16
        ins.then_inc(s.h, 16)
        self.nins += 1
        self._done((s, s.total), R, W)

    def finish(self):
        eng = self.engs["sp"]
        for s in self.dsems:
            if s.total > 0 and self.waited["sp"].get(s, 0) < s.total:
                eng.wait_ge(s.h, s.total)
        for e in ("pe", "dve", "act", "pool"):
            if self.cnt[e] > 0:
                eng.wait_ge(self.sems[e].h, self.cnt[e])
        self.es.close()
        return self.nc

    def mm(self, out, lhsT, rhs, start=True, stop=True, sgc=False):
        if sgc:
            self.op("pe", lambda g: g.matmul(out.ap, lhsT.ap, rhs.ap, start=start, stop=stop, skip_group_check=True), [lhsT, rhs], [out])
        else:
            self.op("pe", lambda g: g.matmul(out.ap, lhsT.ap, rhs.ap, start=start, stop=stop), [lhsT, rhs], [out])

    def sp_wait(self, views):
        R = [x for v in views for x in v.res]
        self._waits("sp", R, [])

    def tr(self, out, in_, ident):
        self.op("pe", lambda g: g.transpose(out.ap, in_.ap, ident.ap), [in_, ident], [out])

    def tt(self, e, out, a, b, op):
        self.op(e, lambda g: g.tensor_tensor(out.ap, a.ap, b.ap, op), [a, b], [out])

    def ts(self, e, out, a, s1, s2=None, op0=ALU.mult, op1=None):
        R = [a]
        s1a = s1.ap if isinstance(s1, V) else s1
        s2a = s2.ap if isinstance(s2, V) else s2
        if isinstance(s1, V):
            R.append(s1)
        if isinstance(s2, V):
            R.append(s2)
        if op1 is None:
            self.op(e, lambda g: g.tensor_scalar(out.ap, a.ap, s1a, None, op0), R, [out])
        else:
            self.op(e, lambda g: g.tensor_scalar(out.ap, a.ap, s1a, s2a, op0, op1), R, [out])

    def stt(self, e, out, a, s, b, op0, op1):
        e = "dve"
        R = [a, b]
        sa = s.ap if isinstance(s, V) else s
        if isinstance(s, V):
            R.append(s)
        self.op(e, lambda g: g.scalar_tensor_tensor(out.ap, a.ap, sa, b.ap, op0, op1), R, [out])

    def cp(self, e, out, a):
        if e == "act":
            self.op(e, lambda g: g.copy(out.ap, a.ap), [a], [out])
        else:
            self.op(e, lambda g: g.tensor_copy(out.ap, a.ap), [a], [out])

    def act(self, out, a, func, bias=None, scale=None, accum=None):
        R = [a]
        kw = {}
        if bias is not None:
            if isinstance(bias, V):
                R.append(bias)
                kw["bias"] = bias.ap
            else:
                kw["bias"] = bias
        if scale is not None:
            if isinstance(scale, V):
                R.append(scale)
                kw["scale"] = scale.ap
            else:
                kw["scale"] = scale
        W = [out]
        if accum is not None:
            kw["accum_out"] = accum.ap
            W.append(accum)
        self.op("act", lambda g: g.activation(out.ap, a.ap, func, **kw), R, W)

    def red(self, e, out, a, op=ALU.add, axis=AX.X):
        self.op(e, lambda g: g.tensor_reduce(out.ap, a.ap, axis, op), [a], [out])

    def memset(self, e, out, val):
        self.op(e, lambda g: g.memset(out.ap, val), [], [out])


D = 1024
KC = 8
NEG_C = -0.6065306597126334
GN_EPS = 64e-5
RMS_EPS = 1e-6


def host_consts():
    import ml_dtypes
    bf = ml_dtypes.bfloat16
    c = {}
    c["c_identb"] = np.eye(128, dtype=np.float32).astype(bf)
    c["c_identf"] = np.eye(128, dtype=np.float32)
    l = np.arange(128)[:, None]
    t = np.arange(128)[None, :]
    tri = np.zeros((128, 3, 128), np.float32)
    tri[:, 0, :] = (l <= t) * NEG_C
    tri[:, 1, :] = (l < t) * NEG_C
    tri[:, 2, :] = NEG_C
    c["c_tri"] = tri
    c["c_onescol"] = np.full((128, 1), NEG_C, np.float32)
    mis = np.zeros((128, 256), np.float32)
    mis[:, 0:128] = (l <= t)
    mis[:, 128:256] = (l < t)
    c["c_maskIS"] = mis.astype(bf)
    c["c_maskSL"] = (l > t).astype(np.float32).astype(bf)
    c["c_ones"] = np.ones((128, 128), np.float32)
    return c


class MK:
    def __init__(self, T, NBS, TS=4):
        self.T = T
        self.NBS = NBS
        self.TS = TS
        self.NS = NBS * TS
        self.k = KB()
        k = self.k
        self.din = {}
        self.dout = {}
        self.c_identb = self.inp("c_identb", [128, 128], BF16)
        self.c_identf = self.inp("c_identf", [128, 128], F32)
        self.c_tri = self.inp("c_tri", [128, 3, 128], F32)
        self.c_onescol = self.inp("c_onescol", [128, 1], F32)
        self.c_maskIS = self.inp("c_maskIS", [128, 256], BF16)
        self.c_maskSL = self.inp("c_maskSL", [128, 128], BF16)
        self.c_ones = self.inp("c_ones", [128, 128], F32)
        self.banks = [k.ps("bank%d" % i, [128, 512], F32) for i in range(6)]
        self.bbanks = [k.ps("bbank%d" % i, [128, 1024], BF16) for i in range(2)]
        self.bank_i = 0
        self.bbank_i = 0
        self.identb = k.sb("identb", [128, 128], BF16)
        self.identf = k.sb("identf", [128, 128], F32)
        self.tri = k.sb("tri", [128, 3, 128], F32)
        self.onescol = k.sb("onescol", [128, 1], F32)
        self.maskIS = k.sb("maskIS", [128, 256], BF16)
        self.maskSL = k.sb("maskSL", [128, 128], BF16)
        self.ones = k.sb("ones", [128, 128], F32)
        for sbt, dr in ((self.identb, self.c_identb), (self.identf, self.c_identf), (self.tri, self.c_tri),
                        (self.onescol, self.c_onescol), (self.maskIS, self.c_maskIS), (self.maskSL, self.c_maskSL),
                        (self.ones, self.c_ones)):
            k.dma("sp", sbt[:], dr[:])
        self._stage_i = 0

    def inp(self, name, shape, dt=F32):
        t = self.k.dram(name, shape, dt, kind="ExternalInput")
        self.din[name] = t
        return t

    def outp(self, name, shape, dt=F32):
        t = self.k.dram(name, shape, dt, kind="ExternalOutput")
        self.dout[name] = t
        return t

    def bank(self):
        b = self.banks[self.bank_i % len(self.banks)]
        self.bank_i += 1
        return b

    def bbank(self):
        b = self.bbanks[self.bbank_i % len(self.bbanks)]
        self.bbank_i += 1
        return b

    def load_w_bf16(self, dst, src_ap, src_res, kc, n, stage, engs=("pool", "act")):
        k = self.k
        src3 = src_ap.rearrange("(c p) n -> p c n", p=128)
        per = max(1, 2048 // n)
        i = 0
        c = 0
        while c < kc:
            cc = min(per, kc - c)
            st = stage[self._stage_i % len(stage)]
            self._stage_i += 1
            stv = V(st.t[:, 0:cc * n].rearrange("p (c n) -> p c n", n=n), st.res)
            k.dma("sp", stv, V(src3[:, c:c + cc, :], src_res))
            k.cp(engs[i % len(engs)], dst[:, c:c + cc, :], stv)
            c += cc
            i += 1

    def load_rows_bf16(self, dst, src_ap, src_res, rows, n, stage):
        k = self.k
        st = stage[self._stage_i % len(stage)]
        self._stage_i += 1
        stv = V(st.t[0:rows, 0:n], st.res)
        k.dma("sp", stv, V(src_ap, src_res))
        k.cp("pool", dst[0:rows, :], stv)

    def bcast_load(self, dst, src_ap, src_res):
        self.k.dma("sp", dst[:], V(src_ap.partition_broadcast(128), src_res))

    def rwkv_phase1(self, j, li, xT_p, xT_s, W, scr, O):
        k = self.k
        T, NS, NBS, TS = self.T, self.NS, self.NBS, self.TS
        has_v = j > 0
        stage = [k.sb("stg%d" % i, [128, 2048], F32) for i in range(2)]
        Wr = k.sb("Wr", [128, KC, D], BF16)
        Wk = k.sb("Wk", [128, KC, D], BF16)
        Wv = k.sb("Wv", [128, KC, D], BF16)
        Wz = k.sb("Wz", [128, KC, D], BF16)
        rkvz = W["rw_w_rkvz"]
        for wi, dst in enumerate((Wr, Wk, Wv, Wz)):
            self.load_w_bf16(dst, rkvz.t[j, wi], rkvz.res, KC, D, stage)
        w1 = k.sb("w1", [128, KC, 64], BF16)
        a1 = k.sb("a1", [128, KC, 64], BF16)
        g1 = k.sb("g1", [128, KC, 160], BF16)
        self.load_w_bf16(w1, W["rw_w1"].t[j], W["rw_w1"].res, KC, 64, stage)
        self.load_w_bf16(a1, W["rw_a1"].t[j], W["rw_a1"].res, KC, 64, stage)
        self.load_w_bf16(g1, W["rw_g1"].t[j], W["rw_g1"].res, KC, 160, stage)
        w2 = k.sb("w2", [64, D], BF16)
        a2 = k.sb("a2", [64, D], BF16)
        g2a = k.sb("g2a", [128, D], BF16)
        g2b = k.sb("g2b", [32, D], BF16)
        self.load_rows_bf16(w2, W["rw_w2"].t[j], W["rw_w2"].res, 64, D, stage)
        self.load_rows_bf16(a2, W["rw_a2"].t[j], W["rw_a2"].res, 64, D, stage)
        self.load_rows_bf16(g2a, W["rw_g2"].t[j, 0:128], W["rw_g2"].res, 128, D, stage)
        self.load_rows_bf16(g2b, W["rw_g2"].t[j, 128:160], W["rw_g2"].res, 32, D, stage)
        if has_v:
            v1 = k.sb("v1", [128, KC, 32], BF16)
            v2 = k.sb("v2", [32, D], BF16)
            self.load_w_bf16(v1, W["rw_v1"].t[j - 1], W["rw_v1"].res, KC, 32, stage)
            self.load_rows_bf16(v2, W["rw_v2"].t[j - 1], W["rw_v2"].res, 32, D, stage)
            v0b = k.sb("v0b", [128, D])
            self.bcast_load(v0b, W["rw_v0"].t[j - 1], W["rw_v0"].res)
        w0b = k.sb("w0b", [128, D])
        a0b = k.sb("a0b", [128, D])
        self.bcast_load(w0b, W["rw_w0"].t[j], W["rw_w0"].res)
        self.bcast_load(a0b, W["rw_a0"].t[j], W["rw_a0"].res)
        mu = k.sb("mu", [128, 6, KC])
        k.dma("sp", mu[:], V(W["mu_fm"].t[j], W["mu_fm"].res))
        nw = k.sb("nw", [128, KC])
        k.dma("sp", nw[:], V(W["nw_fm"].t[li], W["nw_fm"].res))

        NTK = 128
        xT = k.sb("xT", [128, KC, NTK])
        sq = k.sb("sq", [128, KC, NTK])
        rstd = k.sb("rstd", [128, NTK])
        hT = k.sb("hT", [128, KC, NTK + 32])
        xx = k.sb("xx", [128, KC, NTK])
        xi = [k.sb("xi%d" % i, [128, KC, NTK], BF16) for i in range(6)]
        lo_w = k.sb("lo_w", [64, NTK], BF16)
        lo_a = k.sb("lo_a", [64, NTK], BF16)
        lo_ga = k.sb("lo_ga", [128, NTK], BF16)
        lo_gb = k.sb("lo_gb", [32, NTK], BF16)
        lo_v = k.sb("lo_v", [32, NTK], BF16) if has_v else None
        st_r = k.sb("st_r", [128, D])
        st_k = k.sb("st_k", [128, D])
        st_v = k.sb("st_v", [128, D])
        st_w = k.sb("st_w", [128, D])
        st_a = k.sb("st_a", [128, D])
        st_g = k.sb("st_g", [128, D])
        st_z = k.sb("st_z", [128, D])
        st_vf = k.sb("st_vf", [128, D]) if has_v else None
        shiftT = k.sb("shiftT", [128, KC, max(NBS, 1)])
        shc = k.sb("shc", [128, KC, max(NBS, 1)])
        shcp = k.sb("shcp", [128, KC])

        def tile(xT_d, col0, ntok, nb, tl, first, pref, row0, shift_out):
            xTv = V(xT.t[:, :, 0:ntok], xT.res)
            k.dma("sp", xTv, V(xT_d.t[:, col0:col0 + ntok].rearrange("(c p) t -> p c t", p=128), xT_d.res))
            sqv = V(sq.t[:, :, 0:ntok], sq.res)
            k.tt("pool", sqv, xTv, xTv, ALU.mult)
            b = self.bank()
            for c in range(KC):
                k.mm(b[:, 0:ntok], self.ones[:], V(sq.t[:, c, 0:ntok], sq.res), start=(c == 0), stop=(c == KC - 1))
            rs = V(rstd.t[:, 0:ntok], rstd.res)
            k.ts("dve", rs, b[:, 0:ntok], 1.0 / D, RMS_EPS, op0=ALU.mult, op1=ALU.add)
            k.act(rs, rs, AF.Sqrt)
            k.op("dve", lambda g: g.reciprocal(rs.ap, rs.ap), [rs], [rs])
            hv4 = hT.t[:, :, 0:nb * (tl + 1)].rearrange("p c (b t) -> p c b t", t=tl + 1)
            if nb == 1:
                if first:
                    k.memset("pool", V(hv4[:, :, :, 0:1], hT.res), 0.0)
                else:
                    k.cp("pool", V(hv4[:, :, :, 0:1], hT.res), V(hv4[:, :, :, tl:tl + 1], hT.res))
            else:
                k.cp("pool", V(hv4[:, :, :, 0], hT.res), V(shiftT.t[:, :, 0:nb], shiftT.res))
            rs3 = V(rstd.t[:, 0:ntok].rearrange("p (b t) -> p b t", t=tl), rstd.res)
            for c in range(KC):
                k.stt("dve", V(hv4[:, c, :, 1:tl + 1], hT.res),
                      V(xT.t[:, c, 0:ntok].rearrange("p (b t) -> p b t", t=tl), xT.res),
                      nw[:, c:c + 1], rs3, ALU.mult, ALU.mult)
            shift_out(hv4)
            xxv4 = xx.t[:, :, 0:ntok].rearrange("p c (b t) -> p c b t", t=tl)
            for c in range(KC):
                k.tt("dve" if c % 2 == 0 else "pool", V(xxv4[:, c], xx.res), V(hv4[:, c, :, 0:tl], hT.res), V(hv4[:, c, :, 1:tl + 1], hT.res), ALU.subtract)
            for i in range(6):
                for c in range(KC):
                    e = "dve" if (i * KC + c) % 2 == 0 else "pool"
                    k.stt(e, V(xi[i].t[:, c, 0:ntok].rearrange("p (b t) -> p b t", t=tl), xi[i].res),
                          V(xxv4[:, c], xx.res), mu[:, i, c:c + 1], V(hv4[:, c, :, 1:tl + 1], hT.res),
                          ALU.mult, ALU.add)
            xr, xw, xk, xv, xa, xg = xi

            def lora1(xin, w, c0, r, dst, func):
                bb = self.bank()
                for c in range(KC):
                    k.mm(bb[0:r, 0:ntok], V(w.t[:, c, c0:c0 + r], w.res), V(xin.t[:, c, 0:ntok], xin.res), start=(c == 0), stop=(c == KC - 1))
                if func is None:
                    k.cp("act", V(dst.t[0:r, 0:ntok], dst.res), bb[0:r, 0:ntok])
                else:
                    k.act(V(dst.t[0:r, 0:ntok], dst.res), bb[0:r, 0:ntok], func)
            lora1(xw, w1, 0, 64, lo_w, AF.Tanh)
            lora1(xa, a1, 0, 64, lo_a, None)
            lora1(xg, g1, 0, 128, lo_ga, AF.Sigmoid)
            lora1(xg, g1, 128, 32, lo_gb, AF.Sigmoid)
            if has_v:
                lora1(xv, v1, 0, 32, lo_v, None)
                vf = scr[pref + "vf"]
                k.dma("sp", V(st_vf.t[0:ntok, :], st_vf.res), V(vf.t[row0:row0 + ntok, :], vf.res))
            for half in range(2):
                cs = slice(half * 512, (half + 1) * 512)

                def sv(tn):
                    return V(tn.t[0:ntok, cs], tn.res)

                def proj(xin, w):
                    bb = self.bank()
                    for c in range(KC):
                        k.mm(bb[0:ntok, :], V(xin.t[:, c, 0:ntok], xin.res), V(w.t[:, c, cs], w.res), start=(c == 0), stop=(c == KC - 1))
                    return bb
                bb = proj(xr, Wr)
                k.cp("act", sv(st_r), bb[0:ntok, :])
                bb = proj(xk, Wk)
                k.cp("act", sv(st_k), bb[0:ntok, :])
                bb = proj(xv, Wv)
                k.cp("act", sv(st_v), bb[0:ntok, :])
                bb = self.bank()
                k.mm(bb[0:ntok, :], V(lo_w.t[0:64, 0:ntok], lo_w.res), V(w2.t[0:64, cs], w2.res))
                k.tt("dve", sv(st_w), bb[0:ntok, :], sv(w0b), ALU.add)
                k.act(sv(st_w), sv(st_w), AF.Sigmoid)
                bb = self.bank()
                k.mm(bb[0:ntok, :], V(lo_a.t[0:64, 0:ntok], lo_a.res), V(a2.t[0:64, cs], a2.res))
                k.tt("dve", sv(st_a), bb[0:ntok, :], sv(a0b), ALU.add)
                k.act(sv(st_a), sv(st_a), AF.Sigmoid)
                if has_v:
                    bb = self.bank()
                    k.mm(bb[0:ntok, :], V(lo_v.t[0:32, 0:ntok], lo_v.res), V(v2.t[0:32, cs], v2.res))
                    k.tt("dve", sv(st_z), bb[0:ntok, :], sv(v0b), ALU.add)
                    k.act(sv(st_z), sv(st_z), AF.Sigmoid)
                    k.tt("pool", sv(st_vf), sv(st_vf), sv(st_v), ALU.subtract)
                    k.tt("pool", sv(st_vf), sv(st_vf), sv(st_z), ALU.mult)
                    k.tt("pool", sv(st_v), sv(st_v), sv(st_vf), ALU.add)
                bb = proj(xg, Wz)
                k.act(sv(st_z), bb[0:ntok, :], AF.Silu)
                bb = self.bank()
                k.mm(bb[0:ntok, :], V(lo_ga.t[:, 0:ntok], lo_ga.res), V(g2a.t[:, cs], g2a.res), start=True, stop=False)
                k.mm(bb[0:ntok, :], V(lo_gb.t[0:32, 0:ntok], lo_gb.res), V(g2b.t[0:32, cs], g2b.res), start=False, stop=True)
                k.tt("dve", sv(st_g), bb[0:ntok, :], sv(st_z), ALU.mult)
            for nm, st in (("r", st_r), ("k", st_k), ("v", st_v), ("w", st_w), ("a", st_a), ("g", st_g)):
                d = scr[pref + nm]
                k.dma("sp", V(d.t[row0:row0 + ntok, :], d.res), V(st.t[0:ntok, :], st.res), semres=st.res)
                if nm == "v" and not has_v:
                    d = scr[pref + "vf"]
                    k.dma("sp", V(d.t[row0:row0 + ntok, :], d.res), V(st.t[0:ntok, :], st.res), semres=st.res)

        nt = T // 128
        for ti in range(nt):
            def so(hv4, last=(ti == nt - 1)):
                if last:
                    k.cp("pool", shcp[:, :], V(hv4[:, :, 0, 128], hT.res))
                    k.dma("sp", V(O["o_p_shift"].t[j], O["o_p_shift"].res), shcp[:, :], semres=shcp.res)
            tile(xT_p, ti * 128, 128, 1, 128, ti == 0, "p_", ti * 128, so)
        import os
        if NS > 0 and "s" not in os.environ.get("SKIP", ""):
            k.dma("sp", shiftT[:, :, 0:NBS], V(W["shift_fm"].t[j], W["shift_fm"].res))

            def so2(hv4):
                k.cp("pool", shc[:, :, 0:NBS], V(hv4[:, :, :, TS], hT.res))
                k.dma("sp", V(O["o_s_shift"].t[j], O["o_s_shift"].res), shc[:, :, 0:NBS], semres=shc.res)
            tile(xT_s, 0, NS, NBS, TS, True, "s_", 0, so2)

    def rwkv_phase2(self, j, li, xT_p, xT_s, W, scr, O):
        k = self.k
        T, NS, NBS, TS = self.T, self.NS, self.NBS, self.TS
        Wo = k.sb("Wo", [128, KC, D], BF16)
        with k.scope():
            stage = [k.sb("stg%d" % i, [128, 2048], F32) for i in range(2)]
            self.load_w_bf16(Wo, W["rw_w_o"].t[j], W["rw_w_o"].res, KC, D, stage)
        kkb = k.sb("kkb", [128, D])
        kab = k.sb("kab", [128, D])
        rkb = k.sb("rkb", [128, D])
        lnw = k.sb("lnw", [128, D])
        lnb = k.sb("lnb", [128, D])
        self.bcast_load(kkb, W["rw_k_k"].t[j], W["rw_k_k"].res)
        self.bcast_load(kab, W["rw_k_a"].t[j], W["rw_k_a"].res)
        self.bcast_load(rkb, W["rw_r_k"].t[j].rearrange("h c -> (h c)"), W["rw_r_k"].res)
        self.bcast_load(lnw, W["rw_ln_w"].t[j], W["rw_ln_w"].res)
        self.bcast_load(lnb, W["rw_ln_b"].t[j], W["rw_ln_b"].res)
        t_r = k.sb("t_r", [128, D])
        t_k = k.sb("t_k", [128, D])
        t_v = k.sb("t_v", [128, D])
        t_w = k.sb("t_w", [128, D])
        t_a = k.sb("t_a", [128, D])
        t_g = k.sb("t_g", [128, D])
        t_kk = k.sb("t_kk", [128, D])
        t_k2 = k.sb("t_k2", [128, D])
        t_bv = k.sb("t_bv", [128, D])
        t_tmp = k.sb("t_tmp", [128, D])
        t_y = k.sb("t_y", [128, D])
        ob = k.sb("ob", [128, D], BF16)
        oT = k.sb("oT", [128, KC, 128], BF16)
        xT = k.sb("xT", [128, KC, 128])
        sm = {n: k.sb("sm_" + n, [128, 16]) for n in ("ss", "rn", "s1", "s2", "mean", "msq", "var", "rk")}

        def hv(tn, n):
            return V(tn.t[0:n, :].rearrange("p (h c) -> p h c", c=64), tn.res)

        def bc(tn, n):
            return V(tn.t[0:n, :].unsqueeze(2).broadcast_to([n, 16, 64]), tn.res)

        def rows(tn, n):
            return V(tn.t[0:n, :], tn.res)

        def prep(pref, row0, n):
            for nm, tn in (("r", t_r), ("k", t_k), ("v", t_v), ("w", t_w), ("a", t_a), ("g", t_g)):
                d = scr[pref + nm]
                k.dma("sp", rows(tn, n), V(d.t[row0:row0 + n, :], d.res))
            k.tt("pool", rows(t_kk, n), rows(t_k, n), rows(kkb, n), ALU.mult)
            k.tt("pool", rows(t_tmp, n), rows(t_kk, n), rows(t_kk, n), ALU.mult)
            ss = V(sm["ss"].t[0:n, :], sm["ss"].res)
            rn = V(sm["rn"].t[0:n, :], sm["rn"].res)
            k.red("dve", ss, hv(t_tmp, n))
            k.ts("dve", ss, ss, 1e-24, op0=ALU.max)
            k.act(ss, ss, AF.Sqrt)
            k.op("dve", lambda g: g.reciprocal(rn.ap, ss.ap), [ss], [rn])
            k.tt("dve", hv(t_kk, n), hv(t_kk, n), bc(sm["rn"], n), ALU.mult)
            k.stt("pool", rows(t_tmp, n), rows(t_a, n), -1.0, rows(kab, n), ALU.add, ALU.mult)
            k.stt("pool", rows(t_k2, n), rows(t_tmp, n), 1.0, rows(t_k, n), ALU.add, ALU.mult)
            k.tt("pool", rows(t_bv, n), rows(t_kk, n), rows(t_a, n), ALU.mult)

        def post(ysrc, n, xT_d, col0, par=False):
            if par:
                ty4 = t_y.t[0:n, :].rearrange("p (c two n) -> p c two n", two=2, n=64)
                k.cp("act", V(ty4[:, :, 0, :], t_y.res), ysrc[0])
                k.cp("act", V(ty4[:, :, 1, :], t_y.res), ysrc[1])
            else:
                k.cp("act", V(t_y.t[0:n, 0:512], t_y.res), ysrc[0])
                k.cp("act", V(t_y.t[0:n, 512:1024], t_y.res), ysrc[1])
            s1 = V(sm["s1"].t[0:n, :], sm["s1"].res)
            s2 = V(sm["s2"].t[0:n, :], sm["s2"].res)
            mean = V(sm["mean"].t[0:n, :], sm["mean"].res)
            msq = V(sm["msq"].t[0:n, :], sm["msq"].res)
            var = V(sm["var"].t[0:n, :], sm["var"].res)
            rk = V(sm["rk"].t[0:n, :], sm["rk"].res)
            k.red("dve", s1, hv(t_y, n))
            k.tt("pool", rows(t_tmp, n), rows(t_y, n), rows(t_y, n), ALU.mult)
            k.red("dve", s2, hv(t_tmp, n))
            k.ts("dve", mean, s1, 1.0 / 64, op0=ALU.mult)
            k.tt("dve", msq, mean, mean, ALU.mult)
            k.stt("dve", var, s2, 1.0 / 64, msq, ALU.mult, ALU.subtract)
            k.ts("dve", var, var, GN_EPS, op0=ALU.add)
            k.act(var, var, AF.Sqrt)
            k.op("dve", lambda g: g.reciprocal(var.ap, var.ap), [var], [var])
            k.tt("dve", hv(t_y, n), hv(t_y, n), bc(sm["mean"], n), ALU.subtract)
            k.tt("dve", hv(t_y, n), hv(t_y, n), bc(sm["var"], n), ALU.mult)
            k.tt("pool", rows(t_y, n), rows(t_y, n), rows(lnw, n), ALU.mult)
            k.tt("pool", rows(t_y, n), rows(t_y, n), rows(lnb, n), ALU.add)
            k.tt("pool", rows(t_tmp, n), rows(t_r, n), rows(t_k2, n), ALU.mult)
            k.tt("pool", rows(t_tmp, n), rows(t_tmp, n), rows(rkb, n), ALU.mult)
            k.red("dve", rk, hv(t_tmp, n))
            k.tt("dve", hv(t_tmp, n), hv(t_v, n), bc(sm["rk"], n), ALU.mult)
            k.tt("pool", rows(t_y, n), rows(t_y, n), rows(t_tmp, n), ALU.add)
            k.tt("dve", V(ob.t[0:n, :], ob.res), rows(t_y, n), rows(t_g, n), ALU.mult)
            bb = self.bbank()
            for c in range(KC):
                k.tr(bb[:, c * 128:c * 128 + n], V(ob.t[0:n, c * 128:(c + 1) * 128], ob.res), V(self.identb.t[0:n, 0:n], self.identb.res))
            k.cp("act", V(oT.t[:, :, 0:n], oT.res), V(bb.t[:, :].rearrange("p (c t) -> p c t", t=128)[:, :, 0:n], bb.res))
            xTv = V(xT.t[:, :, 0:n], xT.res)
            k.dma("sp", xTv, V(xT_d.t[:, col0:col0 + n].rearrange("(c p) t -> p c t", p=128), xT_d.res))
            for hb in range(2):
                b = self.bank()
                for dq in range(4):
                    dc = hb * 4 + dq
                    for c in range(KC):
                        k.mm(b[:, dq * 128:dq * 128 + n], V(Wo.t[:, c, dc * 128:(dc + 1) * 128], Wo.res), V(oT.t[:, c, 0:n], oT.res), start=(c == 0), stop=(c == KC - 1))
                k.tt("dve", V(xT.t[:, hb * 4:hb * 4 + 4, 0:n], xT.res), V(xT.t[:, hb * 4:hb * 4 + 4, 0:n], xT.res),
                     V(b.t[:, :].rearrange("p (c t) -> p c t", t=128)[:, :, 0:n], b.res), ALU.add)
            k.dma("sp", V(xT_d.t[:, col0:col0 + n].rearrange("(c p) t -> p c t", p=128), xT_d.res), xTv, semres=xT.res)

        import os
        for _once in ([] if "P" in os.environ.get("SKIP", "") else [0]):
          with k.scope():
              rt = k.sb("rt", [128, D], BF16)
              at = k.sb("at", [128, D], BF16)
              kt = k.sb("kt", [128, D], BF16)
              bt = k.sb("bt", [128, D], BF16)
              kg = k.sb("kg", [128, D], BF16)
              bg = k.sb("bg", [128, D], BF16)
              vb = k.sb("vb", [128, D], BF16)
              RA = k.sb("RA", [128, KC, 256], BF16)
              KT = k.sb("KT", [128, KC, 128], BF16)
              BT = k.sb("BT", [128, KC, 128], BF16)
              gC = k.sb("gC", [128, KC])
              A1 = [k.sb("A1_%d" % g, [128, 4, 256], BF16) for g in range(4)]
              A2 = [k.sb("A2_%d" % g, [128, 4, 256], BF16) for g in range(4)]
              P = [[k.sb("P%d_%d" % (i, g), [128, 4, 128], BF16) for g in range(4)] for i in range(2)]
              PT = [[k.sb("PT%d_%d" % (i, g), [128, 4, 128], BF16) for g in range(4)] for i in range(2)]
              X = [k.sb("X_%d" % g, [128, 4, 128]) for g in range(4)]
              Zb = [k.sb("Zb_%d" % g, [128, 4, 128], BF16) for g in range(4)]
              AhT = k.sb("AhT", [128, KC, 128], BF16)
              Ub = k.sb("Ub", [128, 1024], BF16)
              S = k.sb("S", [128, KC, 64])
              Sb = k.sb("Sb", [128, KC, 64], BF16)
              k.memset("pool", S[:], 0.0)
              k.memset("pool", Sb[:], 0.0)
              STOP = int(os.environ.get("STOP", "99"))
              SUB = int(os.environ.get("SUB", "99"))
              for ti in range(T // 128):
                  n = 128
                  prep("p_", ti * 128, n)
                  if STOP <= 0:
                      continue
                  Ea, Eb = t_a, t_w
                  cb = []
                  for which in range(3):
                      for half in range(2):
                          b = self.bank()
                          k.mm(b[:, :], V(self.tri.t[:, which, :], self.tri.res), V(t_w.t[:, half * 512:(half + 1) * 512], t_w.res))
                          cb.append(b)
                  def hs(tn, half):
                      return V(tn.t[:, half * 512:(half + 1) * 512], tn.res)
                  for half in range(2):
                      k.act(hs(Ea, half), cb[half][:, :], AF.Exp)
                  k.tt("dve", rt[:], t_r[:], Ea[:], ALU.mult)
                  for half in range(2):
                      k.act(hs(Ea, half), cb[half][:, :], AF.Exp, scale=-1.0)
                  k.tt("pool", kt[:], t_k2[:], Ea[:], ALU.mult)
                  k.tt("pool", bt[:], t_bv[:], Ea[:], ALU.mult)
                  b = self.bank()
                  for c in range(KC):
                      k.mm(b[:, c:c + 1], V(t_w.t[:, c * 128:(c + 1) * 128], t_w.res), self.onescol[:])
                  k.act(gC[:], b[:, 0:KC], AF.Exp)
                  for half in range(2):
                      k.act(hs(Eb, half), cb[2 + half][:, :], AF.Exp)
                  k.stt("dve", at[:], t_kk[:], -1.0, Eb[:], ALU.mult, ALU.mult)
                  for half in range(2):
                      k.act(hs(Eb, half), cb[4 + half][:, :], AF.Exp)
                  k.tt("pool", Eb[:], Eb[:], Ea[:], ALU.mult)
                  k.tt("dve", kg[:], t_k2[:], Eb[:], ALU.mult)
                  k.tt("pool", bg[:], t_bv[:], Eb[:], ALU.mult)
                  k.cp("pool", vb[:], t_v[:])
                  if STOP <= 1:
                      continue
                  for src, dst, off in ((rt, RA, 0), (at, RA, 128), (kt, KT, 0), (bt, BT, 0)):
                      bb = self.bbank()
                      for c in range(KC):
                          k.tr(bb[:, c * 128:(c + 1) * 128], V(src.t[:, c * 128:(c + 1) * 128], src.res), self.identb[:])
                      k.cp("act", V(dst.t[:, :, off:off + 128], dst.res), V(bb.t[:, :].rearrange("p (c t) -> p c t", t=128), bb.res))
                  if STOP <= 2:
                      continue
                  for g in range(4):
                      for par in range(2):
                          hb = 64 * par
                          b1 = self.bank()
                          b2 = self.bank()
                          for qq in range(2):
                              h = 4 * g + par + 2 * qq
                              c8 = h // 2
                              k.mm(b1[:, qq * 256:(qq + 1) * 256], V(KT.t[hb:hb + 64, c8, :], KT.res), V(RA.t[hb:hb + 64, c8, :], RA.res))
                              k.mm(b2[:, qq * 256:(qq + 1) * 256], V(BT.t[hb:hb + 64, c8, :], BT.res), V(RA.t[hb:hb + 64, c8, :], RA.res))
                          mIS = V(self.maskIS.t[:, :].unsqueeze(1).broadcast_to([128, 2, 256]), self.maskIS.res)
                          k.tt("dve", V(A1[g].t[:, par:4:2, :], A1[g].res), V(b1.t[:, :].rearrange("p (q n) -> p q n", n=256), b1.res), mIS, ALU.mult)
                          k.tt("dve", V(A2[g].t[:, par:4:2, :], A2[g].res), V(b2.t[:, :].rearrange("p (q n) -> p q n", n=256), b2.res), mIS, ALU.mult)
                      for par in range(2):
                          hb = 64 * par
                          b3 = self.bank()
                          for qq in range(2):
                              h = 4 * g + par + 2 * qq
                              c8 = h // 2
                              k.mm(b3[:, qq * 128:(qq + 1) * 128], V(RA.t[hb:hb + 64, c8, 128:256], RA.res), V(BT.t[hb:hb + 64, c8, :], BT.res))
                          mSL = V(self.maskSL.t[:, :].unsqueeze(1).broadcast_to([128, 2, 128]), self.maskSL.res)
                          k.tt("dve", V(P[0][g].t[:, par:4:2, :], P[0][g].res), V(b3.t[:, 0:256].rearrange("p (q n) -> p q n", n=128), b3.res), mSL, ALU.mult)
                  for g in range(4):
                      b4 = self.bank()
                      for q in range(4):
                          h = 4 * g + q
                          k.mm(b4[:, q * 64:(q + 1) * 64], V(A1[g].t[:, q, 128:256], A1[g].res), V(vb.t[:, h * 64:(h + 1) * 64], vb.res))
                      k.cp("act", V(X[g].t[:, :, 64:128], X[g].res), V(b4.t[:, 0:256].rearrange("p (q n) -> p q n", n=64), b4.res))
                      k.cp("pool", V(X[g].t[:, :, 0:64], X[g].res), V(at.t[:, g * 256:(g + 1) * 256].rearrange("p (q n) -> p q n", n=64), at.res))
                      k.cp("pool", Zb[g][:], X[g][:])
                  if STOP <= 3:
                      continue
                  for st in range(7):
                      for g in range(4):
                          cur, nxt = st % 2, (st + 1) % 2
                          Pk = P[cur][g]
                          if st == 0:
                              PTk_v = lambda q, g=g: V(A2[g].t[:, q, 128:256], A2[g].res)
                          else:
                              PTk_v = lambda q, g=g, cur=cur: V(PT[cur][g].t[:, q, :], PT[cur][g].res)
                          bz = self.bank()
                          for q in range(4):
                              k.mm(bz[:, q * 128:(q + 1) * 128], PTk_v(q), V(Zb[g].t[:, q, :], Zb[g].res))
                          if st < 6:
                              bp = self.bank()
                              bpt = self.bank()
                              for q in range(4):
                                  k.mm(bp[:, q * 128:(q + 1) * 128], PTk_v(q), V(Pk.t[:, q, :], Pk.res))
                                  k.mm(bpt[:, q * 128:(q + 1) * 128], V(Pk.t[:, q, :], Pk.res), PTk_v(q))
                          k.tt("dve", X[g][:], X[g][:], V(bz.t[:, :].rearrange("p (q n) -> p q n", n=128), bz.res), ALU.add)
                          k.cp("pool", Zb[g][:], X[g][:])
                          if st < 6:
                              k.cp("act", P[nxt][g][:], V(bp.t[:, :].rearrange("p (q n) -> p q n", n=128), bp.res))
                              k.cp("act", PT[nxt][g][:], V(bpt.t[:, :].rearrange("p (q n) -> p q n", n=128), bpt.res))
                  if STOP <= 4:
                      continue
                  bbs = [self.bbank(), self.bbank()]
                  for h in range(16):
                      g, q = h // 4, h % 4
                      zf = Zb[g].t[:, :, :].rearrange("p q n -> p (q n)")
                      lo = q * 128 - (64 if h % 2 == 1 else 0)
                      k.tr(bbs[h // 8][:, (h % 8) * 128:(h % 8 + 1) * 128], V(zf[:, lo:lo + 128], Zb[g].res), self.identb[:])
                  for bi in range(2):
                      bv3 = bbs[bi].t[:, :].rearrange("p (c two t) -> p c two t", two=2, t=128)
                      k.cp("act", V(AhT.t[0:64, 4 * bi:4 * bi + 4, :], AhT.res), V(bv3[0:64, :, 0, :], bbs[bi].res))
                      k.cp("act", V(AhT.t[64:128, 4 * bi:4 * bi + 4, :], AhT.res), V(bv3[64:128, :, 1, :], bbs[bi].res))
                  if STOP <= 5:
                      continue
                  bU = [self.bank(), self.bank()]
                  for h in range(16):
                      c8, par = h // 2, h % 2
                      hb = 64 * par
                      k.mm(bU[par][:, c8 * 64:(c8 + 1) * 64], V(AhT.t[hb:hb + 64, c8, :], AhT.res), V(Sb.t[hb:hb + 64, c8, :], Sb.res))
                  Ub4 = Ub.t[:, :].rearrange("p (c two n) -> p c two n", two=2, n=64)
                  for g in range(4):
                      for par in range(2):
                          k.tt("dve", V(Ub4[:, 2 * g:2 * g + 2, par, :], Ub.res),
                               V(bU[par].t[:, :].rearrange("p (c n) -> p c n", n=64)[:, 2 * g:2 * g + 2, :], bU[par].res),
                               V(X[g].t[:, par:4:2, 64:128], X[g].res), ALU.add)
                  bY = [self.bank(), self.bank()]
                  for h in range(16):
                      c8, par = h // 2, h % 2
                      hb = 64 * par
                      g, q = h // 4, h % 4
                      o = bY[par][:, c8 * 64:(c8 + 1) * 64]
                      k.mm(o, V(RA.t[hb:hb + 64, c8, 0:128], RA.res), V(Sb.t[hb:hb + 64, c8, :], Sb.res), start=True, stop=False)
                      k.mm(o, V(A2[g].t[:, q, 0:128], A2[g].res), V(Ub.t[:, h * 64:(h + 1) * 64], Ub.res), start=False, stop=False)
                      k.mm(o, V(A1[g].t[:, q, 0:128], A1[g].res), V(vb.t[:, h * 64:(h + 1) * 64], vb.res), start=False, stop=True)
                  bS = [self.bank(), self.bank()]
                  for c8 in range(KC):
                      o = bS[c8 // 4][:, (c8 % 4) * 128:(c8 % 4 + 1) * 128]
                      k.mm(o, V(bg.t[:, c8 * 128:(c8 + 1) * 128], bg.res), V(Ub.t[:, c8 * 128:(c8 + 1) * 128], Ub.res), start=True, stop=False)
                      k.mm(o, V(kg.t[:, c8 * 128:(c8 + 1) * 128], kg.res), V(vb.t[:, c8 * 128:(c8 + 1) * 128], vb.res), start=False, stop=True)
                  for hh in range(2):
                      hb = 64 * hh
                      k.tt("pool", V(S.t[hb:hb + 64, :, :], S.res), V(S.t[hb:hb + 64, :, :], S.res),
                           V(gC.t[hb:hb + 64, :].unsqueeze(2).broadcast_to([64, KC, 64]), gC.res), ALU.mult)
                      for bi in range(2):
                          k.tt("dve", V(S.t[hb:hb + 64, 4 * bi:4 * bi + 4, :], S.res), V(S.t[hb:hb + 64, 4 * bi:4 * bi + 4, :], S.res),
                               V(bS[bi].t[hb:hb + 64, :].rearrange("p (c n) -> p c n", n=128)[:, :, hb:hb + 64], bS[bi].res), ALU.add)
                  k.cp("act", Sb[:], S[:])
                  if STOP <= 6:
                      continue
                  post([V(bY[p_].t[:, :].rearrange("p (c n) -> p c n", n=64), bY[p_].res) for p_ in range(2)], 128, xT_p, ti * 128, par=True)
              k.dma("sp", V(O["o_p_wkv"].t[j], O["o_p_wkv"].res), S[:], semres=S.res)

        import os
        for _once in ([] if "S" in os.environ.get("SKIP", "") else [0]):
          with k.scope():
              n = NS
              NP = NBS * 8
              prep("s_", 0, n)
              k.act(rows(t_w, n), rows(t_w, n), AF.Exp, scale=NEG_C)
              k.ts("pool", rows(t_kk, n), rows(t_kk, n), -1.0, op0=ALU.mult)
              sc = scr["s_scan"]
              for qi, tn in enumerate((t_r, t_w, t_k2, t_v, t_kk, t_bv)):
                  k.dma("sp", V(sc.t[qi], sc.res), rows(tn, n), semres=tn.res)
              Ss = k.sb("Ss", [128, 2, 64, 64])
              tmp = k.sb("Stmp", [128, 2, 64, 64])
              qin = k.sb("qin", [128, 6, TS, 128])
              sa = k.sb("sa", [128, 2, 64])
              ys = k.sb("ys", [128, TS, 128])
              st_in = W["state_wkv"]
              for g in range(8):
                  k.dma("sp", V(Ss.t[g * NBS:(g + 1) * NBS], Ss.res),
                        V(st_in.t[j][:, 2 * g:2 * g + 2], st_in.res))
                  for qi in range(6):
                      k.dma("sp", V(qin.t[g * NBS:(g + 1) * NBS, qi], qin.res),
                            V(sc.t[qi][:, g * 128:(g + 1) * 128].rearrange("(b t) c -> b t c", t=TS), sc.res))

              def bi_(qi, t):
                  return V(qin.t[0:NP, qi, t, :].rearrange("p (h c) -> p h c", c=64).unsqueeze(2).broadcast_to([NP, 2, 64, 64]), qin.res)

              def bj_(ap, res):
                  return V(ap.unsqueeze(3).broadcast_to([NP, 2, 64, 64]), res)
              Sv = V(Ss.t[0:NP], Ss.res)
              Tv = V(tmp.t[0:NP], tmp.res)
              sav = V(sa.t[0:NP], sa.res)
              for t in range(TS):
                  k.tt("dve", Tv, Sv, bi_(4, t), ALU.mult)
                  k.red("dve", sav, Tv)
                  k.tt("pool", Sv, Sv, bi_(1, t), ALU.mult)
                  k.tt("dve", Tv, bj_(sa.t[0:NP], sa.res), bi_(5, t), ALU.mult)
                  k.tt("pool", Sv, Sv, Tv, ALU.add)
                  k.tt("dve", Tv, bj_(qin.t[0:NP, 3, t, :].rearrange("p (h c) -> p h c", c=64), qin.res), bi_(2, t), ALU.mult)
                  k.tt("pool", Sv, Sv, Tv, ALU.add)
                  k.tt("dve", Tv, Sv, bi_(0, t), ALU.mult)
                  k.red("dve", V(ys.t[0:NP, t, :].rearrange("p (h c) -> p h c", c=64), ys.res), Tv)
              yd = scr["s_y"]
              for g in range(8):
                  k.dma("sp", V(O["o_s_wkv"].t[j][:, 2 * g:2 * g + 2], O["o_s_wkv"].res), V(Ss.t[g * NBS:(g + 1) * NBS], Ss.res), semres=Ss.res)
                  k.dma("sp", V(yd.t[:, g * 128:(g + 1) * 128].rearrange("(b t) c -> b t c", t=TS), yd.res), V(ys.t[g * NBS:(g + 1) * NBS], ys.res), semres=ys.res)
              k.dma("sp", rows(t_tmp, n), V(yd.t[:, :], yd.res))
              k.cp("pool", rows(t_kk, n), rows(t_tmp, n))
              post([V(t_kk.t[0:n, 0:512], t_kk.res), V(t_kk.t[0:n, 512:1024], t_kk.res)], n, xT_s, 0)


OA = 2176
LA = 6656
RS1 = LA + 128
RS16 = LA + 2048
OW = 128
LW = 1024
RSW = LW + 128
NEGBIG = -30000.0
NIN = 3632


def rel_bucket_np(d):
    n = np.maximum(d, 0)
    nf = np.maximum(n, 1).astype(np.float32)
    large = 16 + (np.log(nf / np.float32(16)) / np.float32(np.log(1024 / 16)) * np.float32(16)).astype(np.int32)
    large = np.minimum(large, 31)
    return np.where(n < 16, n, large)


def nsa_host_consts(T, TS=4, PAST=2048):
    import ml_dtypes
    bf = ml_dtypes.bfloat16
    c = {}
    dA = np.arange(LA) - OA
    oh = np.zeros((33, LA), np.float32)
    bA = rel_bucket_np(dA)
    oh[bA, np.arange(LA)] = (dA >= 0)
    oh[32] = (dA < 0)
    c["c_ohA"] = oh
    dW = np.arange(LW) - OW
    ohw = np.zeros((33, LW), np.float32)
    okw = (dW >= 0) & (dW < 512)
    ohw[rel_bucket_np(dW), np.arange(LW)] = okw
    ohw[32] = ~okw
    c["c_ohW"] = ohw
    ntile = max(T // 128, (PAST + TS + 127) // 128)
    ex = np.zeros((64, ntile, 128), np.float32)
    for kt in range(ntile):
        for cc in range(128):
            s = 2 * kt + cc // 64
            if s < 64:
                ex[s, kt, cc] = 1
    c["c_expand"] = ex.astype(bf)
    exs = np.zeros((64, 2, 128), np.float32)
    for p_ in range(128):
        exs[p_ // 4, 0, p_] = 1
    exs[32, 1, :] = 1
    c["c_expand_s"] = exs.astype(bf)
    c["c_s8"] = (np.arange(128) % 8).astype(np.int32).reshape(128, 1)
    c["_ntile"] = ntile
    nq = T // 128
    fm = np.zeros((max(nq, 1), 128, 64), np.float32)
    for qt in range(nq):
        qpos = qt * 128 + np.arange(128)
        cur = qpos // 64
        blk = np.arange(64)[None, :]
        forced = (blk == 0) | (blk == cur[:, None]) | (blk == cur[:, None] - 1)
        future = (blk * 64) > qpos[:, None]
        fm[qt] = np.where(future, -1e4, np.where(forced, 1e4, 0.0))
    c["c_fm_p"] = fm
    qpos = PAST + np.arange(TS)
    cur = qpos // 64
    blk = np.arange(64)[None, :]
    forced = (blk == 0) | (blk == cur[:, None]) | (blk == cur[:, None] - 1)
    future = (blk * 64) > qpos[:, None]
    c["c_fm_s"] = np.where(future, -1e4, np.where(forced, 1e4, 0.0)).astype(np.float32)
    def ov(n_cmp, n_sel):
        cs = np.arange(n_cmp)[:, None] * 16
        ss = np.arange(n_sel)[None, :] * 64
        o = np.maximum(np.minimum(cs + 32, ss + 64) - np.maximum(cs, ss), 0)
        return o.astype(np.float32) / 32
    ovp = np.zeros((256, 63), np.float32)
    if T >= 64:
        ncp = (T - 32) // 16 + 1
        ovp[:ncp, :T // 64 - 1] = ov(ncp, T // 64)[:, 1:]
    c["c_ov_p"] = ovp.reshape(2, 128, 63).transpose(1, 0, 2).copy()
    L = PAST + TS
    ncs = (L - 32) // 16 + 1
    nss = -(-L // 64)
    ovs = np.zeros((128, 63), np.float32)
    ovs[:ncs, :nss - 1] = ov(ncs, nss)[:, 1:]
    c["c_ov_s"] = ovs
    return c


def _nsa_setup_tables(self, W):
    k = self.k
    import concourse.bass as bass
    self.tz1 = k.dram("tz1", [16, 128, RS1], BF16)
    self.tz16 = k.dram("tz16", [16, 128, RS16], BF16)
    self.tzw = k.dram("tzw", [16, 128, RSW], BF16)
    with k.scope():
        relb = k.sb("relb", [33, 16])
        k.memset("pool", relb[:], NEGBIG)
        k.dma("sp", relb[0:32, :], W["rel_bias"][:, :])
        RB = k.sb("RB", [33, 16, 128])
        k.cp("dve", RB[:], V(relb.t[:, :].unsqueeze(2).broadcast_to([33, 16, 128]), relb.res))
        ohA = k.sb("ohA", [33, LA])
        ohW = k.sb("ohW", [33, LW])
        k.dma("sp", ohA[:], self.din["c_ohA"][:, :])
        k.dma("sp", ohW[:], self.din["c_ohW"][:, :])
        R = [k.sb("Rrow%d" % i, [128, LA], BF16) for i in range(2)]
        for h in range(16):
            Rr = R[h % 2]
            for ci in range(LA // 512):
                b = self.bank()
                k.mm(b[:, :], V(RB.t[:, h, :], RB.res), V(ohA.t[:, ci * 512:(ci + 1) * 512], ohA.res))
                k.cp("act" if ci % 2 == 0 else "dve", V(Rr.t[:, ci * 512:(ci + 1) * 512], Rr.res), b[:, :])
            d1 = bass.AP(self.tz1.t.tensor, h * 128 * RS1, [[RS1 + 1, 128], [1, LA]])
            k.dma("sp", V(d1, self.tz1.res), Rr[:, :], semres=Rr.res)
            d16 = bass.AP(self.tz16.t.tensor, h * 128 * RS16, [[RS16 + 16, 128], [1, LA]])
            k.dma("sp", V(d16, self.tz16.res), Rr[:, :], semres=Rr.res)
        Rw = [k.sb("Rw%d" % i, [128, LW], BF16) for i in range(2)]
        for h in range(16):
            Rr = Rw[h % 2]
            for ci in range(LW // 512):
                b = self.bank()
                k.mm(b[:, :], V(RB.t[:, h, :], RB.res), V(ohW.t[:, ci * 512:(ci + 1) * 512], ohW.res))
                k.cp("act" if ci % 2 == 0 else "dve", V(Rr.t[:, ci * 512:(ci + 1) * 512], Rr.res), b[:, :])
            dw = bass.AP(self.tzw.t.tensor, h * 128 * RSW, [[RSW + 1, 128], [1, LW]])
            k.dma("sp", V(dw, self.tzw.res), Rr[:, :], semres=Rr.res)


def _bias_tile(self, dst, tz, rs, x0, nk, tqn):
    import concourse.bass as bass
    src = bass.AP(tz.t.tensor, x0, [[rs, nk], [128 * rs, 16], [1, tqn]])
    self.k.dma("sp", dst, V(src, tz.res))


MK.nsa_setup_tables = _nsa_setup_tables
MK.bias_tile = _bias_tile


def _nsa_phase1(self, jn, li, xT_p, xT_s, W, scr, O):
    k = self.k
    T, NS = self.T, self.NS
    Win = k.sb("Win", [128, KC, NIN], BF16)
    with k.scope():
        stage = [k.sb("stg%d" % i, [128, 2048], F32) for i in range(2)]
        wsrc = W["nsa_w_in"].t[jn].rearrange("(c p) n -> p c n", p=128)
        i = 0
        for c in range(KC):
            for half in range(2):
                st = stage[i % 2]
                cs = slice(half * 1816, (half + 1) * 1816)
                k.dma("sp", V(st.t[:, 0:1816], st.res), V(wsrc[:, c, cs], W["nsa_w_in"].res))
                k.cp("pool" if i % 2 == 0 else "act", V(Win.t[:, c, cs], Win.res), V(st.t[:, 0:1816], st.res))
                i += 1
    nw = k.sb("nw", [128, KC])
    k.dma("sp", nw[:], V(W["nw_fm"].t[li], W["nw_fm"].res))
    qwb = k.sb("qwb", [128, D])
    self.bcast_load(qwb, W["qn_t"].t[jn], W["qn_t"].res)
    k.ts("pool", qwb[:], qwb[:], 0.125, op0=ALU.mult)
    knb = k.sb("knb", [128, 2, 256])
    for i in range(2):
        k.dma("sp", knb[:, i, :], V(W["kn_t"].t[jn, i].partition_broadcast(128), W["kn_t"].res))
    xT = k.sb("xT", [128, KC, 128])
    sq = k.sb("sq", [128, KC, 128])
    rstd = k.sb("rstd", [128, 128])
    hTb = k.sb("hTb", [128, KC, 128], BF16)
    qf = k.sb("qf", [128, D])
    qnb = k.sb("qnb", [128, D], BF16)
    tmpq = k.sb("tmpq", [128, D])
    kv = [k.sb("kv%d" % i, [128, 512]) for i in range(3)]
    szt = k.sb("szt", [128, D])
    gt = k.sb("gt", [128, 48])
    ss = k.sb("ssq", [128, 16])

    def tile(xT_d, col0, ntok, pref, row0):
        xTv = V(xT.t[:, :, 0:ntok], xT.res)
        k.dma("sp", xTv, V(xT_d.t[:, col0:col0 + ntok].rearrange("(c p) t -> p c t", p=128), xT_d.res))
        sqv = V(sq.t[:, :, 0:ntok], sq.res)
        k.tt("pool", sqv, xTv, xTv, ALU.mult)
        b = self.bank()
        for c in range(KC):
            k.mm(b[:, 0:ntok], self.ones[:], V(sq.t[:, c, 0:ntok], sq.res), start=(c == 0), stop=(c == KC - 1))
        rs = V(rstd.t[:, 0:ntok], rstd.res)
        k.ts("dve", rs, b[:, 0:ntok], 1.0 / D, RMS_EPS, op0=ALU.mult, op1=ALU.add)
        k.act(rs, rs, AF.Sqrt)
        k.op("dve", lambda g: g.reciprocal(rs.ap, rs.ap), [rs], [rs])
        for c in range(KC):
            k.stt("dve", V(hTb.t[:, c, 0:ntok], hTb.res), V(xT.t[:, c, 0:ntok], xT.res), nw[:, c:c + 1], rs, ALU.mult, ALU.mult)
        for blk in range(8):
            c0 = blk * 512
            cw = min(512, NIN - c0)
            b = self.bank()
            for c in range(KC):
                k.mm(b[0:ntok, 0:cw], V(hTb.t[:, c, 0:ntok], hTb.res), V(Win.t[:, c, c0:c0 + cw], Win.res), start=(c == 0), stop=(c == KC - 1))
            if blk < 2:
                k.cp("act", V(qf.t[0:ntok, c0:c0 + 512], qf.res), b[0:ntok, :])
            elif blk < 5:
                k.cp("act", V(kv[blk - 2].t[0:ntok, :], kv[blk - 2].res), b[0:ntok, :])
            elif blk < 7:
                k.act(V(szt.t[0:ntok, (blk - 5) * 512:(blk - 4) * 512], szt.res), b[0:ntok, :], AF.Silu)
            else:
                k.act(V(gt.t[0:ntok, :], gt.res), b[0:ntok, 0:48], AF.Sigmoid)

        def headnorm(src_v, nh, dst_v, wb_v, eng2):
            n = ntok
            t3 = V(tmpq.t[0:n, 0:nh * 64], tmpq.res)
            k.tt("pool", t3, src_v, src_v, ALU.mult)
            ssv = V(ss.t[0:n, 0:nh], ss.res)
            k.red("dve", ssv, V(tmpq.t[0:n, 0:nh * 64].rearrange("p (h c) -> p h c", c=64), tmpq.res))
            k.ts("dve", ssv, ssv, 1.0 / 64, RMS_EPS, op0=ALU.mult, op1=ALU.add)
            k.act(ssv, ssv, AF.Sqrt)
            k.op("dve", lambda g: g.reciprocal(ssv.ap, ssv.ap), [ssv], [ssv])
            s3 = V(src_v.ap.rearrange("p (h c) -> p h c", c=64), src_v.res)
            k.tt("dve", s3, s3, V(ss.t[0:n, 0:nh].unsqueeze(2).broadcast_to([n, nh, 64]), ss.res), ALU.mult)
            k.tt(eng2, dst_v, src_v, wb_v, ALU.mult)
        headnorm(V(qf.t[0:ntok, :], qf.res), 16, V(qnb.t[0:ntok, :], qnb.res), V(qwb.t[0:ntok, :], qwb.res), "pool")
        headnorm(V(kv[1].t[0:ntok, 0:256], kv[1].res), 4, V(kv[1].t[0:ntok, 0:256], kv[1].res), V(knb.t[0:ntok, 0, :], knb.res), "pool")
        headnorm(V(kv[2].t[0:ntok, 0:256], kv[2].res), 4, V(kv[2].t[0:ntok, 0:256], kv[2].res), V(knb.t[0:ntok, 1, :], knb.res), "pool")
        rs_ = slice(row0, row0 + ntok)
        for nm, src in (("kc", V(kv[0].t[0:ntok, 0:256], kv[0].res)), ("vc", V(kv[0].t[0:ntok, 256:512], kv[0].res)),
                        ("ks", V(kv[1].t[0:ntok, 0:256], kv[1].res)), ("vs", V(kv[1].t[0:ntok, 256:512], kv[1].res)),
                        ("kw", V(kv[2].t[0:ntok, 0:256], kv[2].res)), ("vw", V(kv[2].t[0:ntok, 256:512], kv[2].res)),
                        ("qn", V(qnb.t[0:ntok, :], qnb.res)), ("sz", V(szt.t[0:ntok, :], szt.res)), ("gt", V(gt.t[0:ntok, :], gt.res))):
            d = scr[pref + nm]
            k.dma("sp", V(d.t[rs_, :], d.res), src, semres=src.res[0])

    for ti in range(T // 128):
        tile(xT_p, ti * 128, 128, "np_", ti * 128)
    import os
    if NS > 0 and "s" not in os.environ.get("SKIP", ""):
        tile(xT_s, 0, NS, "ns_", 0)


MK.nsa_phase1 = _nsa_phase1


def _nsa_cmp_weights(self, jn, W):
    k = self.k
    cw = {}
    with k.scope():
        stg = k.sb("cstg", [64, 2048])
        pstg = k.sb("pstg", [32, 64])
        pstb = k.sb("pstb", [32, 64], BF16)
        for kvi in range(2):
            W1 = self._cmpW1[kvi]
            src = W["nsa_cmp_w1"].t[jn, kvi].rearrange("(j d) e -> d j e", d=64)
            k.dma("sp", V(stg.t[:, :].rearrange("p (j e) -> p j e", e=64), stg.res), V(src, W["nsa_cmp_w1"].res))
            k.cp("pool", W1[:], V(stg.t[:, :].rearrange("p (j e) -> p j e", e=64), stg.res))
            W2 = self._cmpW2[kvi]
            k.dma("sp", V(stg.t[:, 0:64], stg.res), V(W["nsa_cmp_w2"].t[jn, kvi], W["nsa_cmp_w2"].res))
            k.cp("pool", W2[:], V(stg.t[:, 0:64], stg.res))
            k.dma("sp", pstg[:], V(W["nsa_cmp_pos"].t[jn, kvi], W["nsa_cmp_pos"].res))
            k.cp("pool", pstb[:], pstg[:])
            bb = self.bbank()
            k.tr(bb[0:64, 0:32], pstb[:], V(self.identb.t[0:32, 0:32], self.identb.res))
            posT = self._cmpPos[kvi]
            k.cp("act", posT[:], bb[0:64, 0:32])
            b = self.bank()
            for j in range(32):
                k.mm(b[0:64, 0:1], V(W1.t[:, j, :], W1.res), V(posT.t[:, j:j + 1], posT.res), start=(j == 0), stop=(j == 31))
            k.cp("act", self._cmpBias[kvi][:], b[0:64, 0:1])
        k.dma("sp", self._kn2[:], V(W["kn2_col"].t[jn], W["kn2_col"].res))


def _nsa_alloc_cmp(self):
    k = self.k
    self._cmpW1 = [k.sb("cW1_%d" % i, [64, 32, 64], BF16) for i in range(2)]
    self._cmpW2 = [k.sb("cW2_%d" % i, [64, 64], BF16) for i in range(2)]
    self._cmpPos = [k.sb("cPos_%d" % i, [64, 32], BF16) for i in range(2)]
    self._cmpBias = [k.sb("cBias_%d" % i, [64, 1]) for i in range(2)]
    self._kn2 = k.sb("kn2", [64, 1])
    self._chT = k.sb("chT", [64, 256], BF16)
    self._csq = k.sb("csq", [64, 256])
    self._crs = k.sb("crs", [64, 256])


def _nsa_compress(self, kcT, vcT, NC, kcmpT, vcmpX):
    k = self.k
    hT, sqt, rs = self._chT, self._csq, self._crs
    for kvi, src in enumerate((kcT, vcT)):
        W1, W2, bias = self._cmpW1[kvi], self._cmpW2[kvi], self._cmpBias[kvi]
        for kvh in range(4):
            b = self.bank()
            for j in range(32):
                k.mm(b[0:64, 0:NC], V(W1.t[:, j, :], W1.res), V(src.t[:, kvh, j:j + 16 * (NC - 1) + 1:16], src.res), start=(j == 0), stop=(j == 31))
            k.act(V(hT.t[:, 0:NC], hT.res), b[0:64, 0:NC], AF.Silu, bias=bias[:, 0:1])
            if kvi == 0:
                b2 = self.bank()
                k.mm(b2[0:64, 0:NC], W2[:], V(hT.t[:, 0:NC], hT.res))
                k.act(V(sqt.t[:, 0:NC], sqt.res), b2[0:64, 0:NC], AF.Square)
                b3 = self.bank()
                k.mm(b3[0:64, 0:NC], V(self.ones.t[0:64, 0:64], self.ones.res), V(sqt.t[:, 0:NC], sqt.res))
                rsv = V(rs.t[:, 0:NC], rs.res)
                k.ts("dve", rsv, b3[0:64, 0:NC], 1.0 / 64, RMS_EPS, op0=ALU.mult, op1=ALU.add)
                k.act(rsv, rsv, AF.Sqrt)
                k.op("dve", lambda g: g.reciprocal(rsv.ap, rsv.ap), [rsv], [rsv])
                k.stt("dve", V(kcmpT.t[:, kvh, 0:NC], kcmpT.res), b2[0:64, 0:NC], self._kn2[:, 0:1], rsv, ALU.mult, ALU.mult)
            else:
                for ci in range((NC + 127) // 128):
                    nk = min(128, NC - ci * 128)
                    b2 = self.bank()
                    k.mm(b2[0:nk, 0:64], V(hT.t[:, ci * 128:ci * 128 + nk], hT.res), W2[:])
                    k.cp("act", V(vcmpX.t[0:nk, ci, kvh, 0:64], vcmpX.res), b2[0:nk, 0:64])


MK.nsa_cmp_weights = _nsa_cmp_weights
MK.nsa_alloc_cmp = _nsa_alloc_cmp
MK.nsa_compress = _nsa_compress


def _nsa_alloc_attn(self, tqn):
    k = self.k
    A = {}
    A["tqn"] = tqn
    NQ = 4 * tqn
    A["Eb"] = [k.sb("Eb%d" % i, [128, NQ], BF16) for i in range(3)]
    A["Ei"] = 0
    A["oacc"] = k.sb("oacc", [128, 16, 64])
    A["accS"] = [k.sb("accS%d" % i, [128, 4, 512]) for i in range(2)]
    A["otmp"] = k.sb("otmp", [128, 4, 64])
    A["den"] = k.sb("den", [128, 4])
    A["coef"] = k.sb("coef", [128, 4])
    A["imp"] = k.sb("imp", [128, 4, 64])
    A["sc"] = k.sb("sc", [128, 64])
    A["scw"] = k.sb("scw", [128, 64])
    A["m8"] = k.sb("m8", [128, 8])
    A["m8b"] = k.sb("m8b", [128, 8])
    A["m30f"] = k.sb("m30f", [128, 64])
    A["m30b"] = k.sb("m30b", [128, 4, 64], BF16)
    A["M30"] = k.sb("M30", [64, 4, 4, tqn], BF16)
    k.memset("pool", A["imp"][:], 0.0)
    return A


def _nsa_attend(self, A, qT, gates, fm, cmp_tiles, sel_tiles, win_tiles):
    k = self.k
    tqn = A["tqn"]
    NQ = 4 * tqn
    acc = self.banks[0:4]
    sbanks = self.banks[4:6]
    oacc = A["oacc"]
    g3 = gates.ap.rearrange("p (h r) -> p h r", r=3)

    def branch(br, tiles, Wd, use_m30):
        units = [(kvh, ti, tdesc, len(tiles[kvh])) for kvh in range(4) for ti, tdesc in enumerate(tiles[kvh])]

        def stage1(u):
            kvh, ti, tdesc, ntl = u
            nk = tdesc["nk"]
            sb = sbanks[self._sbi % 2]
            self._sbi += 1
            k.mm(sb[0:nk, 0:NQ], tdesc["kT"], V(qT.ap[:, kvh * NQ:(kvh + 1) * NQ], qT.res), start=True, stop=False)
            last_bias = not use_m30
            k.mm(sb[0:nk, 0:NQ], V(self.identb.t[0:nk, 0:nk], self.identb.res), tdesc["bias"], start=False, stop=last_bias)
            if use_m30:
                M30 = A["M30"]
                k.mm(sb[0:nk, 0:NQ], tdesc["expand"], V(M30.t[:, kvh].rearrange("p g t -> p (g t)"), M30.res), start=False, stop=True)
            Eb = A["Eb"][A["Ei"] % 3]
            A["Ei"] += 1
            k.act(V(Eb.t[0:nk, :], Eb.res), sb[0:nk, 0:NQ], AF.Exp)
            return Eb

        def stage2(u, Eb):
            kvh, ti, tdesc, ntl = u
            nk = tdesc["nk"]
            for g in range(4):
                first = (ti == 0 and g == 0)
                k.mm(acc[kvh][0:tqn, g * 128:g * 128 + Wd], V(Eb.t[0:nk, g * tqn:(g + 1) * tqn], Eb.res), tdesc["vX"],
                     start=first, stop=(ti == ntl - 1), sgc=True)
        prev = None
        for u in units:
            Eb = stage1(u)
            if prev is not None:
                stage2(*prev)
            prev = (u, Eb)
        if prev is not None:
            stage2(*prev)
        aS = A["accS"][br % 2]
        for kvh in range(4):
            if len(tiles[kvh]) > 0:
                k.cp("act", V(aS.t[0:tqn, kvh, :], aS.res), acc[kvh][0:tqn, :])

    def fin(br, tiles):
        aS = A["accS"][br % 2]
        for kvh in range(4):
            av = aS.t[0:tqn, kvh, :].rearrange("p (g n) -> p g n", n=128)
            den = V(A["den"].t[0:tqn, :], A["den"].res)
            coef = V(A["coef"].t[0:tqn, :], A["coef"].res)
            if len(tiles[kvh]) == 0:
                if br == 0:
                    k.memset("pool", V(oacc.t[0:tqn, 4 * kvh:4 * kvh + 4, :], oacc.res), 0.0)
                continue
            k.ts("dve", den, V(av[:, :, 64], aS.res), 1e-30, op0=ALU.max)
            k.op("dve", lambda g_: g_.reciprocal(den.ap, den.ap), [den], [den])
            k.tt("dve", coef, den, V(g3[:, 4 * kvh:4 * kvh + 4, br], gates.res), ALU.mult)
            cb = V(A["coef"].t[0:tqn, :].unsqueeze(2).broadcast_to([tqn, 4, 64]), A["coef"].res)
            ov_ = V(oacc.t[0:tqn, 4 * kvh:4 * kvh + 4, :], oacc.res)
            if br == 0:
                k.tt("dve", ov_, V(av[:, :, 0:64], aS.res), cb, ALU.mult)
            else:
                ot = V(A["otmp"].t[0:tqn], A["otmp"].res)
                k.tt("dve", ot, V(av[:, :, 0:64], aS.res), cb, ALU.mult)
                k.tt("pool", ov_, ov_, ot, ALU.add)
            if br == 0:
                impv = V(A["imp"].t[0:tqn, kvh, 1:64], A["imp"].res)
                for g in range(4):
                    if g == 0:
                        k.ts("dve", impv, V(av[:, g, 65:128], aS.res), V(A["den"].t[0:tqn, g:g + 1], A["den"].res), op0=ALU.mult)
                    else:
                        k.stt("dve", impv, V(av[:, g, 65:128], aS.res), V(A["den"].t[0:tqn, g:g + 1], A["den"].res), impv, ALU.mult, ALU.add)

    self._sbi = getattr(self, "_sbi", 0)
    branch(0, cmp_tiles, 128, False)
    fin(0, cmp_tiles)
    for kvh in range(4):
        sc = V(A["sc"].t[0:tqn, :], A["sc"].res)
        scw = V(A["scw"].t[0:tqn, :], A["scw"].res)
        m8 = V(A["m8"].t[0:tqn, :], A["m8"].res)
        m8b = V(A["m8b"].t[0:tqn, :], A["m8b"].res)
        k.tt("dve", sc, V(A["imp"].t[0:tqn, kvh, :], A["imp"].res), fm, ALU.add)
        k.op("dve", lambda g_: g_.max(m8.ap, sc.ap), [sc], [m8])
        k.op("dve", lambda g_: g_.match_replace(scw.ap, m8.ap, sc.ap, -1e9), [m8, sc], [scw])
        k.op("dve", lambda g_: g_.max(m8b.ap, scw.ap), [scw], [m8b])
        m30f = V(A["m30f"].t[0:tqn, :], A["m30f"].res)
        k.ts("dve", m30f, sc, V(A["m8b"].t[0:tqn, 7:8], A["m8b"].res), op0=ALU.is_ge)
        k.ts("dve", V(A["m30b"].t[0:tqn, kvh, :], A["m30b"].res), m30f, -1.0, -NEGBIG, op0=ALU.add, op1=ALU.mult)
    branch(2, win_tiles, 65, False)
    bb = self.bbank()
    for kvh in range(4):
        k.tr(bb[0:64, kvh * 128:kvh * 128 + tqn], V(A["m30b"].t[0:tqn, kvh, :], A["m30b"].res), V(self.identb.t[0:tqn, 0:tqn], self.identb.res))
    M30 = A["M30"]
    k.cp("dve", M30[:], V(bb.t[0:64, 0:512].rearrange("p (v t) -> p v t", t=128)[:, :, 0:tqn].unsqueeze(2).broadcast_to([64, 4, 4, tqn]), bb.res))
    fin(2, win_tiles)
    branch(1, sel_tiles, 65, True)
    fin(1, sel_tiles)


MK.nsa_alloc_attn = _nsa_alloc_attn
MK.nsa_attend = _nsa_attend


def _outproj(self, ob, n, xT_d, col0, Wo, xT, oT):
    k = self.k
    bb = self.bbank()
    for c in range(KC):
        k.tr(bb[:, c * 128:c * 128 + n], V(ob.t[0:n, c * 128:(c + 1) * 128], ob.res), V(self.identb.t[0:n, 0:n], self.identb.res))
    k.cp("act", V(oT.t[:, :, 0:n], oT.res), V(bb.t[:, :].rearrange("p (c t) -> p c t", t=128)[:, :, 0:n], bb.res))
    xTv = V(xT.t[:, :, 0:n], xT.res)
    k.dma("sp", xTv, V(xT_d.t[:, col0:col0 + n].rearrange("(c p) t -> p c t", p=128), xT_d.res))
    for hb in range(2):
        b = self.bank()
        for dq in range(4):
            dc = hb * 4 + dq
            for c in range(KC):
                k.mm(b[:, dq * 128:dq * 128 + n], V(Wo.t[:, c, dc * 128:(dc + 1) * 128], Wo.res), V(oT.t[:, c, 0:n], oT.res), start=(c == 0), stop=(c == KC - 1))
        k.tt("dve", V(xT.t[:, hb * 4:hb * 4 + 4, 0:n], xT.res), V(xT.t[:, hb * 4:hb * 4 + 4, 0:n], xT.res),
             V(b.t[:, :].rearrange("p (c t) -> p c t", t=128)[:, :, 0:n], b.res), ALU.add)
    k.dma("sp", V(xT_d.t[:, col0:col0 + n].rearrange("(c p) t -> p c t", p=128), xT_d.res), xTv, semres=xT.res)


MK.outproj = _outproj


def _nsa_phase2_prompt(self, jn, li, xT_p, W, scr, O):
    k = self.k
    T = self.T
    NT = T // 128
    NC = (T - 32) // 16 + 1
    Wo = k.sb("Wo", [128, KC, D], BF16)
    with k.scope():
        stage = [k.sb("stg%d" % i, [128, 2048], F32) for i in range(2)]
        self.load_w_bf16(Wo, W["nsa_w_o"].t[jn], W["nsa_w_o"].res, KC, D, stage)
    kcmpT = k.sb("kcmpT", [64, 4, 256], BF16)
    vcmpX = k.sb("vcmpX", [128, 2, 4, 128], BF16)
    k.memset("pool", kcmpT[:], 0.0)
    k.memset("pool", vcmpX[:], 0.0)
    ovp = k.sb("ovp", [128, 2, 63])
    k.dma("sp", ovp[:], self.din["c_ov_p"][:])
    k.memset("pool", V(vcmpX.t[:, :, :, 64:65], vcmpX.res), 1.0)
    for kvh in range(4):
        k.cp("pool", V(vcmpX.t[:, :, kvh, 65:128], vcmpX.res), ovp[:])
    ldf = k.sb("ldf", [128, 512])
    ldb = k.sb("ldb", [128, 512], BF16)

    def build_T(dst, names, ntiles):
        for ti in range(ntiles):
            for i, nm in enumerate(names):
                d = scr[nm]
                k.dma("sp", V(ldf.t[:, i * 256:(i + 1) * 256], ldf.res), V(d.t[ti * 128:(ti + 1) * 128, :], d.res))
            w = 256 * len(names)
            k.cp("pool", V(ldb.t[:, 0:w], ldb.res), V(ldf.t[:, 0:w], ldf.res))
            for i, nm in enumerate(names):
                bb = self.bbank()
                for kvh in range(4):
                    k.tr(bb[0:64, kvh * 128:(kvh + 1) * 128], V(ldb.t[:, i * 256 + kvh * 64:i * 256 + (kvh + 1) * 64], ldb.res), self.identb[:])
                k.cp("act", V(dst[i].t[:, :, ti * 128:(ti + 1) * 128], dst[i].res), V(bb.t[0:64, 0:512].rearrange("p (v t) -> p v t", t=128), bb.res))
    with k.scope():
        self.nsa_alloc_cmp()
        self.nsa_cmp_weights(jn, W)
        kcT = k.sb("kcT", [64, 4, T], BF16)
        vcT = k.sb("vcT", [64, 4, T], BF16)
        build_T([kcT, vcT], ["np_kc", "np_vc"], NT)
        self.nsa_compress(kcT, vcT, NC, kcmpT, vcmpX)
    ksT = k.sb("ksT", [64, 4, T], BF16)
    build_T([ksT], ["np_ks"], NT)
    vsX = k.sb("vsX", [128, NT, 4, 65], BF16)
    k.memset("pool", V(vsX.t[:, :, :, 64:65], vsX.res), 1.0)
    for ti in range(NT):
        d = scr["np_vs"]
        k.dma("sp", V(ldf.t[:, 0:256], ldf.res), V(d.t[ti * 128:(ti + 1) * 128, :], d.res))
        k.cp("pool", V(vsX.t[:, ti, :, 0:64], vsX.res), V(ldf.t[:, 0:256].rearrange("p (v c) -> p v c", c=64), ldf.res))
    kwT = k.sb("kwT", [64, 4, 5 * 128], BF16)
    vwX = k.sb("vwX", [128, 5, 4, 65], BF16)
    k.memset("pool", V(vwX.t[:, :, :, 64:65], vwX.res), 1.0)
    selB = k.sb("selB", [128, 10, 16 * 128], BF16)
    winB = k.sb("winB", [128, 5, 16 * 128], BF16)
    for dlt in range(10):
        self.bias_tile(V(selB.t[:, dlt, :].rearrange("p (h t) -> p h t", t=128), selB.res), self.tz1, RS1, OA + 128 * dlt, 128, 128)
    for dlt in range(5):
        self.bias_tile(V(winB.t[:, dlt, :].rearrange("p (h t) -> p h t", t=128), winB.res), self.tzw, RSW, OW + 128 * dlt, 128, 128)
    cmpB = [k.sb("cmpB%d" % i, [128, 16 * 128], BF16) for i in range(2)]
    nex = self.din["c_expand"].t.shape[1]
    expand = k.sb("expand", [64, nex, 128], BF16)
    k.dma("sp", expand[:], self.din["c_expand"][:])
    fm = k.sb("fm", [128, 64])
    A = self.nsa_alloc_attn(128)
    qn = k.sb("qn", [128, D], BF16)
    qT = k.sb("qT", [64, 16 * 128], BF16)
    gt = k.sb("gt", [128, 48])
    szt = k.sb("szt", [128, D])
    ob = k.sb("ob", [128, D], BF16)
    oT = k.sb("oT", [128, KC, 128], BF16)
    xT = k.sb("xT", [128, KC, 128])
    for qt in range(NT):
        rs_ = slice(qt * 128, (qt + 1) * 128)
        k.dma("sp", qn[:], V(scr["np_qn"].t[rs_, :], scr["np_qn"].res))
        k.dma("sp", gt[:], V(scr["np_gt"].t[rs_, :], scr["np_gt"].res))
        k.dma("sp", szt[:], V(scr["np_sz"].t[rs_, :], scr["np_sz"].res))
        k.dma("sp", fm[:], V(self.din["c_fm_p"].t[qt], self.din["c_fm_p"].res))
        for half in range(2):
            bb = self.bbank()
            for hh in range(8):
                h = half * 8 + hh
                k.tr(bb[0:64, hh * 128:(hh + 1) * 128], V(qn.t[:, h * 64:(h + 1) * 64], qn.res), self.identb[:])
            k.cp("act", V(qT.t[:, half * 1024:(half + 1) * 1024], qT.res), bb[0:64, :])
        slot = qt % 5
        k.dma("sp", V(ldf.t[:, 0:256], ldf.res), V(scr["np_kw"].t[rs_, :], scr["np_kw"].res))
        k.dma("sp", V(ldf.t[:, 256:512], ldf.res), V(scr["np_vw"].t[rs_, :], scr["np_vw"].res))
        k.cp("pool", V(ldb.t[:, 0:256], ldb.res), V(ldf.t[:, 0:256], ldf.res))
        k.cp("pool", V(vwX.t[:, slot, :, 0:64], vwX.res), V(ldf.t[:, 256:512].rearrange("p (v c) -> p v c", c=64), ldf.res))
        bb = self.bbank()
        for kvh in range(4):
            k.tr(bb[0:64, kvh * 128:(kvh + 1) * 128], V(ldb.t[:, kvh * 64:(kvh + 1) * 64], ldb.res), self.identb[:])
        k.cp("act", V(kwT.t[:, :, slot * 128:(slot + 1) * 128], kwT.res), V(bb.t[0:64, 0:512].rearrange("p (v t) -> p v t", t=128), bb.res))
        cmp_tiles = [[] for _ in range(4)]
        sel_tiles = [[] for _ in range(4)]
        win_tiles = [[] for _ in range(4)]
        for c in range((NC + 127) // 128):
            if 8 * qt + 6 < 128 * c:
                continue
            nk = min(128, NC - 128 * c)
            cb = cmpB[c]
            self.bias_tile(V(cb.t[0:nk, :].rearrange("p (h t) -> p h t", t=128), cb.res), self.tz16, RS16, OA + 128 * qt - 31 - 2048 * c, nk, 128)
            for kvh in range(4):
                cmp_tiles[kvh].append(dict(nk=nk, kT=V(kcmpT.t[:, kvh, c * 128:c * 128 + nk], kcmpT.res), vX=V(vcmpX.t[0:nk, c, kvh, :], vcmpX.res),
                                           bias=V(cb.t[0:nk, kvh * 512:(kvh + 1) * 512], cb.res)))
        for kt in range(qt + 1):
            dlt = min(qt - kt, 9)
            for kvh in range(4):
                sel_tiles[kvh].append(dict(nk=128, kT=V(ksT.t[:, kvh, kt * 128:(kt + 1) * 128], ksT.res), vX=V(vsX.t[:, kt, kvh, :], vsX.res),
                                           bias=V(selB.t[:, dlt, kvh * 512:(kvh + 1) * 512], selB.res), expand=V(expand.t[:, kt, :], expand.res)))
        for kt in range(max(0, qt - 4), qt + 1):
            sl = kt % 5
            for kvh in range(4):
                win_tiles[kvh].append(dict(nk=128, kT=V(kwT.t[:, kvh, sl * 128:(sl + 1) * 128], kwT.res), vX=V(vwX.t[:, sl, kvh, :], vwX.res),
                                           bias=V(winB.t[:, qt - kt, kvh * 512:(kvh + 1) * 512], winB.res)))
        self.nsa_attend(A, qT[:, :], gt[:, :], fm[:, :], cmp_tiles, sel_tiles, win_tiles)
        k.tt("dve", ob[:], V(A["oacc"].t[:, :, :].rearrange("p h c -> p (h c)"), A["oacc"].res), szt[:], ALU.mult)
        self.outproj(ob, 128, xT_p, qt * 128, Wo, xT, oT)
    wb = min(512, T)
    k.dma("sp", V(O["o_p_win_k"].t[jn], O["o_p_win_k"].res), V(scr["np_kw"].t[T - wb:T, :], scr["np_kw"].res))
    k.dma("sp", V(O["o_p_win_v"].t[jn], O["o_p_win_v"].res), V(scr["np_vw"].t[T - wb:T, :], scr["np_vw"].res))


MK.nsa_phase2_prompt = _nsa_phase2_prompt


def _nsa_phase2_sample(self, jn, li, xT_s, W, scr, O):
    k = self.k
    nc = k.nc
    import concourse.bass as bass
    NBS, TS, NS = self.NBS, self.TS, self.NS
    PAST = 2048
    NPG = PAST // 128
    L = PAST + TS
    NC = (L - 32) // 16 + 1
    NKT = NPG + 1
    Wo = k.sb("Wo", [128, KC, D], BF16)
    with k.scope():
        stage = [k.sb("stg%d" % i, [128, 2048], F32) for i in range(2)]
        self.load_w_bf16(Wo, W["nsa_w_o"].t[jn], W["nsa_w_o"].res, KC, D, stage)
    self.nsa_alloc_cmp()
    self.nsa_cmp_weights(jn, W)
    kcmpT = k.sb("kcmpT", [64, 4, 256], BF16)
    vcmpX = k.sb("vcmpX", [128, 2, 4, 128], BF16)
    k.memset("pool", kcmpT[:], 0.0)
    k.memset("pool", vcmpX[:], 0.0)
    ovs = k.sb("ovs", [128, 63])
    k.dma("sp", ovs[:], self.din["c_ov_s"][:, :])
    k.memset("pool", V(vcmpX.t[:, :, :, 64:65], vcmpX.res), 1.0)
    for kvh in range(4):
        k.cp("pool", V(vcmpX.t[:, 0, kvh, 65:128], vcmpX.res), ovs[:])
    pts = k.sb("pts", [128, NBS], I32)
    s8s = k.sb("s8s", [128, 1], I32)
    idx = k.sb("idx", [128, NBS], I32)
    idf = k.sb("idf", [128, NBS])
    s8f = k.sb("s8f", [128, 1])
    k.dma("sp", pts[:], W["pt8T"][:, :])
    k.dma("sp", s8s[:], self.din["c_s8"][:, :])
    k.cp("dve", idf[:], pts[:])
    k.cp("dve", s8f[:], s8s[:])
    k.ts("dve", idf[:], idf[:], 8.0, s8f[:, 0:1], op0=ALU.mult, op1=ALU.add)
    k.cp("dve", idx[:], idf[:])
    pg2 = [k.sb("pgbuf%d" % i, [128, NPG + 1, 256]) for i in range(2)]
    pg = [pg2[0], pg2[1], pg2[0], pg2[1]]
    ldbs = [k.sb("ldb%d" % i, [128, 256], BF16) for i in range(4)]
    ldbi = [0]

    def next_ldb():
        ldbi[0] += 1
        return ldbs[ldbi[0] % 4]
    kcT = k.sb("kcT", [64, 4, PAST], BF16)
    vcT = k.sb("vcT", [64, 4, PAST], BF16)
    ksT = k.sb("ksT", [64, 4, NKT * 128], BF16)
    vsX = k.sb("vsX", [128, NKT, 4, 65], BF16)
    k.memset("pool", V(vsX.t[:, :, :, 64:65], vsX.res), 1.0)
    wbuf = [k.sb("wbuf%d" % i, [128, 5, 256]) for i in range(2)]
    kwT = k.sb("kwT", [64, 4, 5 * 128], BF16)
    vwX = k.sb("vwX", [128, 5, 4, 65], BF16)
    k.memset("pool", V(vwX.t[:, :, :, 64:65], vwX.res), 1.0)
    selB = k.sb("selBs", [128, NKT, 16 * TS], BF16)
    winB = k.sb("winBs", [128, 5, 16 * TS], BF16)
    cmpB = k.sb("cmpBs", [128, 16 * TS], BF16)
    for kt in range(NKT):
        if kt < NPG:
            self.bias_tile(V(selB.t[:, kt, :].rearrange("p (h t) -> p h t", t=TS), selB.res), self.tz16, RS16, OA + PAST - kt, 128, TS)
        else:
            self.bias_tile(V(selB.t[0:TS, kt, :].rearrange("p (h t) -> p h t", t=TS), selB.res), self.tz1, RS1, OA, TS, TS)
    for wt in range(5):
        nk = 128 if wt < 4 else TS
        self.bias_tile(V(winB.t[0:nk, wt, :].rearrange("p (h t) -> p h t", t=TS), winB.res), self.tzw, RSW, OW + 512 - 128 * wt, nk, TS)
    self.bias_tile(V(cmpB.t[0:NC, :].rearrange("p (h t) -> p h t", t=TS), cmpB.res), self.tz16, RS16, OA + PAST - 31, NC, TS)
    expand = k.sb("expand", [64, 2, 128], BF16)
    k.dma("sp", expand[:], self.din["c_expand_s"][:])
    fm = k.sb("fms", [TS, 64])
    k.dma("sp", fm[:], self.din["c_fm_s"][:, :])
    A = self.nsa_alloc_attn(TS)
    qn = k.sb("qn", [TS, D], BF16)
    qT = k.sb("qT", [64, 16 * TS], BF16)
    gt = k.sb("gt", [TS, 48])
    szt = k.sb("szt", [TS, D])
    obs = k.sb("obs", [TS, D], BF16)
    caches = [W["cache_cmp_k"], W["cache_cmp_v"], W["cache_sel_k"], W["cache_sel_v"]]
    newn = ["ns_kc", "ns_vc", "ns_ks", "ns_vs"]
    NPOOL = caches[0].t.shape[1]
    for bl in range(NBS):
        rs_ = slice(bl * TS, (bl + 1) * TS)
        for ci, dst in ((0, kcT), (1, vcT), (2, ksT), (3, None)):
            src = caches[ci].t.rearrange("l n (s t) v c -> (l n s) (t v c)", s=8)
            k.idma(V(pg[ci].t[:, 0:NPG, :].rearrange("p t c -> p (t c)"), pg[ci].res), src, caches[ci].res, idx[:, bl:bl + 1],
                   element_offset=jn * NPOOL * 8 * 4096)
            if ci >= 2:
                d = scr[newn[ci]]
                k.dma("sp", V(pg[ci].t[0:TS, NPG, :], pg[ci].res), V(d.t[rs_, :], d.res))
            if ci < 3:
                for pi in range(NKT):
                    nr = 128 if pi < NPG else TS
                    if pi == NPG and ci < 2:
                        continue
                    ldb = next_ldb()
                    k.cp("pool" if pi % 2 == 0 else "dve", V(ldb.t[0:nr, :], ldb.res), V(pg[ci].t[0:nr, pi, :], pg[ci].res))
                    bb = self.bbank()
                    for kvh in range(4):
                        k.tr(bb[0:64, kvh * 128:kvh * 128 + nr], V(ldb.t[0:nr, kvh * 64:(kvh + 1) * 64], ldb.res), V(self.identb.t[0:nr, 0:nr], self.identb.res))
                    if ci < 2:
                        dv = V(dst.t[:, :, pi:PAST:16], dst.res)
                    else:
                        dv = V(dst.t[:, :, pi * 128:pi * 128 + nr], dst.res)
                    k.cp("act", dv, V(bb.t[0:64, 0:512].rearrange("p (v t) -> p v t", t=128)[:, :, 0:nr], bb.res))
            else:
                for pi in range(NKT):
                    nr = 128 if pi < NPG else TS
                    k.cp("pool", V(vsX.t[0:nr, pi, :, 0:64], vsX.res), V(pg[3].t[0:nr, pi, :].rearrange("p (v c) -> p v c", c=64), pg[3].res))
        self.nsa_compress(kcT, vcT, NC, kcmpT, vcmpX)
        for wi, (stn, nn) in enumerate((("state_win_k", "ns_kw"), ("state_win_v", "ns_vw"))):
            k.dma("sp", V(wbuf[wi].t[:, 0:4, :], wbuf[wi].res), V(W[stn].t[jn, bl].rearrange("(t p) c -> p t c", p=128), W[stn].res))
            k.dma("sp", V(wbuf[wi].t[0:TS, 4, :], wbuf[wi].res), V(scr[nn].t[rs_, :], scr[nn].res))
        for wt in range(5):
            nr = 128 if wt < 4 else TS
            ldb = next_ldb()
            k.cp("pool", V(ldb.t[0:nr, :], ldb.res), V(wbuf[0].t[0:nr, wt, :], wbuf[0].res))
            bb = self.bbank()
            for kvh in range(4):
                k.tr(bb[0:64, kvh * 128:kvh * 128 + nr], V(ldb.t[0:nr, kvh * 64:(kvh + 1) * 64], ldb.res), V(self.identb.t[0:nr, 0:nr], self.identb.res))
            k.cp("act", V(kwT.t[:, :, wt * 128:wt * 128 + nr], kwT.res), V(bb.t[0:64, 0:512].rearrange("p (v t) -> p v t", t=128)[:, :, 0:nr], bb.res))
            k.cp("pool", V(vwX.t[0:nr, wt, :, 0:64], vwX.res), V(wbuf[1].t[0:nr, wt, :].rearrange("p (v c) -> p v c", c=64), wbuf[1].res))
        for wi, on in enumerate(("o_s_win_k", "o_s_win_v")):
            od = O[on]
            for wt in range(4):
                lo = wt * 128 - TS
                if wt == 0:
                    k.dma("sp", V(od.t[jn, bl, 0:128 - TS, :], od.res), V(wbuf[wi].t[TS:128, 0, :], wbuf[wi].res), semres=wbuf[wi].res)
                else:
                    k.dma("sp", V(od.t[jn, bl, lo:lo + 128, :], od.res), V(wbuf[wi].t[:, wt, :], wbuf[wi].res), semres=wbuf[wi].res)
            k.dma("sp", V(od.t[jn, bl, 512 - TS:512, :], od.res), V(wbuf[wi].t[0:TS, 4, :], wbuf[wi].res), semres=wbuf[wi].res)
        k.dma("sp", qn[:], V(scr["ns_qn"].t[rs_, :], scr["ns_qn"].res))
        k.dma("sp", gt[:], V(scr["ns_gt"].t[rs_, :], scr["ns_gt"].res))
        k.dma("sp", szt[:], V(scr["ns_sz"].t[rs_, :], scr["ns_sz"].res))
        bb = self.bbank()
        for h in range(16):
            k.tr(bb[0:64, h * TS:(h + 1) * TS], V(qn.t[:, h * 64:(h + 1) * 64], qn.res), V(self.identb.t[0:TS, 0:TS], self.identb.res))
        k.cp("act", qT[:, :], bb[0:64, 0:16 * TS])
        cmp_tiles = [[dict(nk=NC, kT=V(kcmpT.t[:, kvh, 0:NC], kcmpT.res), vX=V(vcmpX.t[0:NC, 0, kvh, :], vcmpX.res),
                           bias=V(cmpB.t[0:NC, kvh * 4 * TS:(kvh + 1) * 4 * TS], cmpB.res))] for kvh in range(4)]
        sel_tiles = [[] for _ in range(4)]
        win_tiles = [[] for _ in range(4)]
        for kt in range(NKT):
            nk = 128 if kt < NPG else TS
            for kvh in range(4):
                sel_tiles[kvh].append(dict(nk=nk, kT=V(ksT.t[:, kvh, kt * 128:kt * 128 + nk], ksT.res), vX=V(vsX.t[0:nk, kt, kvh, :], vsX.res),
                                           bias=V(selB.t[0:nk, kt, kvh * 4 * TS:(kvh + 1) * 4 * TS], selB.res), expand=V(expand.t[:, 0 if kt < NPG else 1, 0:nk], expand.res)))
        for wt in range(5):
            nk = 128 if wt < 4 else TS
            for kvh in range(4):
                win_tiles[kvh].append(dict(nk=nk, kT=V(kwT.t[:, kvh, wt * 128:wt * 128 + nk], kwT.res), vX=V(vwX.t[0:nk, wt, kvh, :], vwX.res),
                                           bias=V(winB.t[0:nk, wt, kvh * 4 * TS:(kvh + 1) * 4 * TS], winB.res)))
        self.nsa_attend(A, qT[:, :], gt[:, :], fm[:, :], cmp_tiles, sel_tiles, win_tiles)
        k.tt("dve", obs[:], V(A["oacc"].t[0:TS, :, :].rearrange("p h c -> p (h c)"), A["oacc"].res), szt[:], ALU.mult)
        k.dma("sp", V(scr["ns_ob"].t[rs_, :], scr["ns_ob"].res), obs[:], semres=obs.res)
    ob = k.sb("ob", [128, D], BF16)
    oT = k.sb("oT", [128, KC, 128], BF16)
    xT = k.sb("xT", [128, KC, 128])
    k.dma("sp", V(ob.t[0:NS, :], ob.res), V(scr["ns_ob"].t[:, :], scr["ns_ob"].res))
    self.outproj(ob, NS, xT_s, 0, Wo, xT, oT)


MK.nsa_phase2_sample = _nsa_phase2_sample


T_FULL = 4096
NBS_FULL = 16
NCORES = 8
RW_NAMES = ["rw_w_rkvz", "rw_w0", "rw_w1", "rw_w2", "rw_a0", "rw_a1", "rw_a2", "rw_v0", "rw_v1", "rw_v2", "rw_g1", "rw_g2",
            "rw_k_k", "rw_k_a", "rw_r_k", "rw_ln_w", "rw_ln_b", "rw_w_o"]
NSA_NAMES = ["nsa_w_in", "nsa_cmp_pos", "nsa_cmp_w1", "nsa_cmp_w2", "nsa_w_o", "rel_bias"]
CACHE_NAMES = ["cache_cmp_k", "cache_cmp_v", "cache_sel_k", "cache_sel_v"]


def build_full(T, NBS, shapes, n_layers=4):
    m = MK(T, NBS)
    k = m.k
    NS = m.NS
    W = {}
    for n in RW_NAMES + NSA_NAMES + CACHE_NAMES:
        W[n] = m.inp(n, list(shapes[n]))
    W["mu_fm"] = m.inp("mu_fm", [2, 128, 6, 8])
    W["nw_fm"] = m.inp("nw_fm", [4, 128, 8])
    W["shift_fm"] = m.inp("shift_fm", [2, 128, 8, NBS])
    W["state_wkv"] = m.inp("state_wkv", [2, NBS, 16, 64, 64])
    W["qn_t"] = m.inp("qn_t", [2, 1024])
    W["kn_t"] = m.inp("kn_t", [2, 3, 256])
    W["kn2_col"] = m.inp("kn2_col", [2, 64, 1])
    W["pt8T"] = m.inp("pt8T", [128, NBS], I32)
    W["state_win_k"] = m.inp("state_win_k", [2, NBS, 512, 256])
    W["state_win_v"] = m.inp("state_win_v", [2, NBS, 512, 256])
    hc = nsa_host_consts(T)
    for n_ in ("c_ohA", "c_ohW", "c_expand", "c_expand_s", "c_s8", "c_fm_p", "c_fm_s", "c_ov_p", "c_ov_s"):
        a = hc[n_]
        dt = I32 if a.dtype == np.int32 else (BF16 if a.dtype != np.float32 else F32)
        m.inp(n_, list(a.shape), dt)
    xin_p = m.inp("xT_p_in", [1024, T])
    xin_s = m.inp("xT_s_in", [1024, NS])
    xo_p = m.outp("xT_p", [1024, T])
    xo_s = m.outp("xT_s", [1024, NS])
    k.dma("sp", xo_p[:, :], xin_p[:, :])
    k.dma("sp", xo_s[:, :], xin_s[:, :])
    wb = min(512, T)
    O = {"o_p_shift": m.outp("o_p_shift", [2, 128, 8]), "o_s_shift": m.outp("o_s_shift", [2, 128, 8, NBS]),
         "o_p_wkv": m.outp("o_p_wkv", [2, 128, 8, 64]), "o_s_wkv": m.outp("o_s_wkv", [2, NBS, 16, 64, 64]),
         "o_p_win_k": m.outp("o_p_win_k", [2, wb, 256]), "o_p_win_v": m.outp("o_p_win_v", [2, wb, 256]),
         "o_s_win_k": m.outp("o_s_win_k", [2, NBS, 512, 256]), "o_s_win_v": m.outp("o_s_win_v", [2, NBS, 512, 256])}
    scr = {}
    for pref, n in (("p_", T), ("s_", NS)):
        for nm in ("r", "k", "v", "w", "a", "g", "vf"):
            scr[pref + nm] = k.dram("scr_" + pref + nm, [n, 1024])
    scr["s_scan"] = k.dram("scr_s_scan", [6, NS, 1024])
    scr["s_y"] = k.dram("scr_s_y", [NS, 1024])
    nscr = []
    for jn in range(2):
        d = {}
        for nm in ("kc", "vc", "ks", "vs"):
            d["np_" + nm] = m.outp("o_p_%s_%d" % (nm, jn), [T, 256])
            d["ns_" + nm] = m.outp("o_s_%s_%d" % (nm, jn), [NS, 256])
        nscr.append(d)
    shared = {}
    for pref, n in (("np_", T), ("ns_", NS)):
        for nm in ("kw", "vw"):
            shared[pref + nm] = k.dram("scr_" + pref + nm, [n, 256])
        shared[pref + "qn"] = k.dram("scr_" + pref + "qn", [n, 1024], BF16)
        shared[pref + "sz"] = k.dram("scr_" + pref + "sz", [n, 1024])
        shared[pref + "gt"] = k.dram("scr_" + pref + "gt", [n, 48])
    shared["ns_ob"] = k.dram("scr_ns_ob", [NS, 1024], BF16)
    m.nsa_setup_tables(W)
    for li in range(n_layers):
        j = li // 2
        if li % 2 == 0:
            with k.scope():
                m.rwkv_phase1(j, li, xo_p, xo_s, W, scr, O)
            with k.scope():
                m.rwkv_phase2(j, li, xo_p, xo_s, W, scr, O)
        else:
            s2 = dict(shared)
            s2.update(nscr[j])
            with k.scope():
                m.nsa_phase1(j, li, xo_p, xo_s, W, s2, O)
            with k.scope():
                m.nsa_phase2_prompt(j, li, xo_p, W, s2, O)
            with k.scope():
                m.nsa_phase2_sample(j, li, xo_s, W, s2, O)
    nc = k.finish()
    return m, nc, hc


def host_prep(inputs, T, NBS, ncores, hc):
    f32 = np.float32
    common = dict(host_consts())
    for n_ in ("c_ohA", "c_ohW", "c_expand", "c_expand_s", "c_s8", "c_fm_p", "c_fm_s", "c_ov_p", "c_ov_s"):
        common[n_] = hc[n_]
    for n in RW_NAMES + NSA_NAMES + CACHE_NAMES:
        common[n] = np.ascontiguousarray(np.asarray(inputs[n], dtype=f32))
    mu = np.asarray(inputs["rw_mu"], f32)
    common["mu_fm"] = np.ascontiguousarray(mu.reshape(2, 6, 8, 128).transpose(0, 3, 1, 2))
    nw = np.asarray(inputs["norm_w"], f32)
    common["nw_fm"] = np.ascontiguousarray(nw.reshape(4, 8, 128).transpose(0, 2, 1))
    qn = np.asarray(inputs["nsa_q_norm"], f32)
    common["qn_t"] = np.ascontiguousarray(np.tile(qn, (1, 16)))
    kn = np.asarray(inputs["nsa_k_norm"], f32)
    common["kn_t"] = np.ascontiguousarray(np.tile(kn, (1, 1, 4)))
    common["kn2_col"] = np.ascontiguousarray(kn[:, 2, :, None])
    xp = np.asarray(inputs["x_prompt"], f32)
    xs = np.asarray(inputs["x_sample"], f32)
    sh = np.asarray(inputs["state_shift"], f32)
    swkv = np.asarray(inputs["state_wkv"], f32)
    pt = np.asarray(inputs["page_table"]).astype(np.int32)
    wk = np.asarray(inputs["state_win_k"], f32)
    wv = np.asarray(inputs["state_win_v"], f32)
    NS = NBS * 4
    maps = []
    for c in range(ncores):
        d = dict(common)
        b = c % xp.shape[0]
        bs = slice(c * NBS, (c + 1) * NBS)
        d["xT_p_in"] = np.ascontiguousarray(xp[b, :T].T)
        d["xT_s_in"] = np.ascontiguousarray(xs[bs].reshape(NS, 1024).T)
        d["shift_fm"] = np.ascontiguousarray(sh[:, bs].reshape(2, NBS, 8, 128).transpose(0, 3, 2, 1))
        d["state_wkv"] = np.ascontiguousarray(swkv[:, bs])
        d["pt8T"] = np.ascontiguousarray(np.repeat(pt[bs], 8, axis=1).T)
        d["state_win_k"] = np.ascontiguousarray(wk[:, bs].reshape(2, NBS, 512, 256))
        d["state_win_v"] = np.ascontiguousarray(wv[:, bs].reshape(2, NBS, 512, 256))
        maps.append(d)
    return maps


def assemble(results, T, NBS, ncores, nb_prompt):
    f32 = np.float32
    NS = NBS * 4
    B = nb_prompt
    DB = ncores * NBS
    y_p = np.zeros((B, T, 1024), f32)
    y_s = np.zeros((DB, 4, 1024), f32)
    p_wkv = np.zeros((2, B, 16, 64, 64), f32)
    p_shift = np.zeros((2, B, 1024), f32)
    p_kv = [np.zeros((2, B, T, 4, 64), f32) for _ in range(4)]
    wb = min(512, T)
    p_win = [np.zeros((2, B, wb, 4, 64), f32) for _ in range(2)]
    s_wkv = np.zeros((2, DB, 16, 64, 64), f32)
    s_shift = np.zeros((2, DB, 1024), f32)
    s_kv = [np.zeros((2, DB, 4, 4, 64), f32) for _ in range(4)]
    s_win = [np.zeros((2, DB, 512, 4, 64), f32) for _ in range(2)]
    for c in range(ncores):
        r = results[c]
        bs = slice(c * NBS, (c + 1) * NBS)
        y_s[bs] = r["xT_s"].T.reshape(NBS, 4, 1024)
        s_wkv[:, bs] = r["o_s_wkv"]
        s_shift[:, bs] = r["o_s_shift"].transpose(0, 3, 2, 1).reshape(2, NBS, 1024)
        for i, nm in enumerate(("kc", "vc", "ks", "vs")):
            for jn in range(2):
                s_kv[i][jn, bs] = r["o_s_%s_%d" % (nm, jn)].reshape(NBS, 4, 4, 64)
        s_win[0][:, bs] = r["o_s_win_k"].reshape(2, NBS, 512, 4, 64)
        s_win[1][:, bs] = r["o_s_win_v"].reshape(2, NBS, 512, 4, 64)
        if c < B:
            b = c
            y_p[b] = r["xT_p"].T
            for j in range(2):
                p_wkv[j, b] = r["o_p_wkv"][j].reshape(2, 64, 8, 64).transpose(2, 0, 3, 1).reshape(16, 64, 64)
                p_shift[j, b] = r["o_p_shift"][j].transpose(1, 0).reshape(1024)
            for i, nm in enumerate(("kc", "vc", "ks", "vs")):
                for jn in range(2):
                    p_kv[i][jn, b] = r["o_p_%s_%d" % (nm, jn)].reshape(T, 4, 64)
            p_win[0][:, b] = r["o_p_win_k"].reshape(2, wb, 4, 64)
            p_win[1][:, b] = r["o_p_win_v"].reshape(2, wb, 4, 64)
    return (y_p, y_s, p_wkv, p_shift, p_kv[0], p_kv[1], p_kv[2], p_kv[3], p_win[0], p_win[1],
            s_wkv, s_shift, s_kv[0], s_kv[1], s_kv[2], s_kv[3], s_win[0], s_win[1])


_CACHE = {}


def kernel(**inputs):
    from concourse.bass_utils import run_bass_kernel_spmd
    T, NBS = T_FULL, NBS_FULL
    shapes = {n: tuple(np.asarray(inputs[n]).shape) for n in RW_NAMES + NSA_NAMES + CACHE_NAMES}
    key = (T, NBS)
    if key not in _CACHE:
        _CACHE[key] = build_full(T, NBS, shapes)
    m, nc, hc = _CACHE[key]
    maps = host_prep(inputs, T, NBS, NCORES, hc)
    res = run_bass_kernel_spmd(nc, maps, core_ids=list(range(NCORES)))
    return assemble(res.results, T, NBS, NCORES, np.asarray(inputs["x_prompt"]).shape[0])
```

```python
import contextlib
import numpy as np
import concourse.bass as bass
import concourse.mybir as mybir

F32 = mybir.dt.float32
BF16 = mybir.dt.bfloat16
I32 = mybir.dt.int32
AF = mybir.ActivationFunctionType
ALU = mybir.AluOpType
AX = mybir.AxisListType


class Sem:
    def __init__(self, h, is_dma):
        self.h = h
        self.is_dma = is_dma
        self.total = 0


class Res:
    __slots__ = ("w", "r", "dsem", "name", "isem")

    def __init__(self, name=""):
        self.w = None
        self.r = {}
        self.dsem = None
        self.isem = None
        self.name = name


class V:
    __slots__ = ("ap", "res")

    def __init__(self, ap, res):
        self.ap = ap
        self.res = res if isinstance(res, (list, tuple)) else [res]


class Tn:
    def __init__(self, t, name, res=None):
        self.t = t
        self.name = name
        self.res = res if res is not None else Res(name)

    def __getitem__(self, k):
        return V(self.t[k], self.res)

    def v(self, ap):
        return V(ap, self.res)


class KB:
    def __init__(self):
        self.nc = bass.Bass("TRN2", target_bir_lowering=False)
        nc = self.nc
        self.es = contextlib.ExitStack()
        self.engs = {"pe": nc.tensor, "dve": nc.vector, "act": nc.scalar, "pool": nc.gpsimd, "sp": nc.sync}
        self.sems = {}
        for e in ("pe", "dve", "act", "pool"):
            self.sems[e] = Sem(self.es.enter_context(nc.semaphore("s_" + e)), False)
        self.cnt = {e: 0 for e in self.engs}
        self.waited = {e: {} for e in self.engs}
        self.dsems = []
        self.n_dsem = 0
        self.max_dsem = 80
        self.nins = 0
        self.sbuf_bytes = 0

    def sb(self, name, shape, dt=F32):
        self._uid = getattr(self, "_uid", 0) + 1
        st = self.scopes[-1] if getattr(self, "scopes", None) else self.es
        t = st.enter_context(self.nc.sbuf_tensor("sb%d_%s" % (self._uid, name), list(shape), dt))
        n = 1
        for s in shape[1:]:
            n *= s
        self.sbuf_bytes += n * (4 if dt in (F32, I32) else 2)
        return Tn(t, name)

    @contextlib.contextmanager
    def scope(self):
        if not hasattr(self, "scopes"):
            self.scopes = []
        st = contextlib.ExitStack()
        self.scopes.append(st)
        b0 = self.sbuf_bytes
        try:
            yield
        finally:
            self.barrier()
            self.peak = max(getattr(self, "peak", 0), self.sbuf_bytes)
            self.sbuf_bytes = b0
            self.scopes.pop()
            st.close()

    def barrier(self):
        for e in ("pe", "dve", "act", "pool", "sp"):
            eng = self.engs[e]
            wd = self.waited[e]
            for e2 in ("pe", "dve", "act", "pool"):
                if e2 == e and e == "pe":
                    continue
                s = self.sems[e2]
                v = self.cnt[e2]
                if v > 0 and wd.get(s, 0) < v:
                    eng.wait_ge(s.h, v)
                    wd[s] = v
            for s in self.dsems:
                if s.total > 0 and wd.get(s, 0) < s.total:
                    eng.wait_ge(s.h, s.total)
                    wd[s] = s.total

    def ps(self, name, shape, dt=F32):
        t = self.es.enter_context(self.nc.psum_tensor("ps_" + name, list(shape), dt))
        return Tn(t, name)

    def dram(self, name, shape, dt=F32, kind="Internal"):
        t = self.nc.dram_tensor(name, list(shape), dt, kind=kind).ap()
        return Tn(t, name)

    def new_dsem(self):
        if self.n_dsem < self.max_dsem:
            s = Sem(self.es.enter_context(self.nc.semaphore("d%d" % self.n_dsem)), True)
            self.dsems.append(s)
            self.n_dsem += 1
            return s
        s = self.dsems[self.n_dsem % self.max_dsem]
        self.n_dsem += 1
        return s

    def _waits(self, e, R, W):
        need = {}

        def add(ev):
            if ev is None:
                return
            s, v = ev
            if e == "pe" and s is self.sems["pe"]:
                return
            if need.get(s, 0) < v:
                need[s] = v

        for r in R:
            add(r.w)
        for w in W:
            add(w.w)
            for s, v in w.r.items():
                add((s, v))
        eng = self.engs[e]
        wd = self.waited[e]
        for s, v in need.items():
            if s.is_dma:
                v = s.total
            if wd.get(s, 0) < v:
                eng.wait_ge(s.h, v)
                wd[s] = v
                self.nins += 1

    def _done(self, ev, R, W):
        for r in R:
            r.r[ev[0]] = ev[1]
        for w in W:
            w.w = ev
            w.r = {}

    def op(self, e, fn, R, W):
        R = [x for v in R for x in v.res]
        W = [x for v in W for x in v.res]
        self._waits(e, R, W)
        ins = fn(self.engs[e])
        self.cnt[e] += 1
        self.nins += 1
        ins.then_inc(self.sems[e].h, 1)
        self._done((self.sems[e], self.cnt[e]), R, W)

    def dma(self, q, out, in_, semres=None, **kw):
        R = list(in_.res)
        W = list(out.res)
        self._waits(q, R, W)
        ins = self.engs[q].dma_start(out=out.ap, in_=in_.ap, **kw)
        sr = semres if semres is not None else (out.res[0])
        if sr.dsem is None:
            sr.dsem = self.new_dsem()
        s = sr.dsem
        s.total += 16
        ins.then_inc(s.h, 16)
        self.nins += 1
        self._done((s, s.total), R, W)

    def idma(self, out, in_ap, in_res, idx, element_offset=0):
        import concourse.bass as bass
        R = list(idx.res) + [in_res]
        W = list(out.res)
        self._waits("pool", R, W)
        ins = self.engs["pool"].indirect_dma_start(out=out.ap, out_offset=None, in_=in_ap,
                                                   in_offset=bass.IndirectOffsetOnAxis(ap=idx.ap, axis=0), element_offset=element_offset)
        sr = out.res[0]
        if sr.isem is None:
            sr.isem = Sem(self.es.enter_context(self.nc.semaphore("i%d" % len(self.dsems))), True)
            self.dsems.append(sr.isem)
        s = sr.isem
        s.total += 16
        ins.then_inc(s.h, 16)
        self.nins += 1
        self._done((s, s.total), R, W)

    def finish(self):
        eng = self.engs["sp"]
        for s in self.dsems:
            if s.total > 0 and self.waited["sp"].get(s, 0) < s.total:
                eng.wait_ge(s.h, s.total)
        for e in ("pe", "dve", "act", "pool"):
            if self.cnt[e] > 0:
                eng.wait_ge(self.sems[e].h, self.cnt[e])
        self.es.close()
        return self.nc

    def mm(self, out, lhsT, rhs, start=True, stop=True, sgc=False):
        if sgc:
            self.op("pe", lambda g: g.matmul(out.ap, lhsT.ap, rhs.ap, start=start, stop=stop, skip_group_check=True), [lhsT, rhs], [out])
        else:
            self.op("pe", lambda g: g.matmul(out.ap, lhsT.ap, rhs.ap, start=start, stop=stop), [lhsT, rhs], [out])

    def sp_wait(self, views):
        R = [x for v in views for x in v.res]
        self._waits("sp", R, [])

    def tr(self, out, in_, ident):
        self.op("pe", lambda g: g.transpose(out.ap, in_.ap, ident.ap), [in_, ident], [out])

    def tt(self, e, out, a, b, op):
        self.op(e, lambda g: g.tensor_tensor(out.ap, a.ap, b.ap, op), [a, b], [out])

    def ts(self, e, out, a, s1, s2=None, op0=ALU.mult, op1=None):
        R = [a]
        s1a = s1.ap if isinstance(s1, V) else s1
        s2a = s2.ap if isinstance(s2, V) else s2
        if isinstance(s1, V):
            R.append(s1)
        if isinstance(s2, V):
            R.append(s2)
        if op1 is None:
            self.op(e, lambda g: g.tensor_scalar(out.ap, a.ap, s1a, None, op0), R, [out])
        else:
            self.op(e, lambda g: g.tensor_scalar(out.ap, a.ap, s1a, s2a, op0, op1), R, [out])

    def stt(self, e, out, a, s, b, op0, op1):
        e = "dve"
        R = [a, b]
        sa = s.ap if isinstance(s, V) else s
        if isinstance(s, V):
            R.append(s)
        self.op(e, lambda g: g.scalar_tensor_tensor(out.ap, a.ap, sa, b.ap, op0, op1), R, [out])

    def cp(self, e, out, a):
        if e == "act":
            self.op(e, lambda g: g.copy(out.ap, a.ap), [a], [out])
        else:
            self.op(e, lambda g: g.tensor_copy(out.ap, a.ap), [a], [out])

    def act(self, out, a, func, bias=None, scale=None, accum=None):
        R = [a]
        kw = {}
        if bias is not None:
            if isinstance(bias, V):
                R.append(bias)
                kw["bias"] = bias.ap
            else:
                kw["bias"] = bias
        if scale is not None:
            if isinstance(scale, V):
                R.append(scale)
                kw["scale"] = scale.ap
            else:
                kw["scale"] = scale
        W = [out]
        if accum is not None:
            kw["accum_out"] = accum.ap
            W.append(accum)
        self.op("act", lambda g: g.activation(out.ap, a.ap, func, **kw), R, W)

    def red(self, e, out, a, op=ALU.add, axis=AX.X):
        self.op(e, lambda g: g.tensor_reduce(out.ap, a.ap, axis, op), [a], [out])

    def memset(self, e, out, val):
        self.op(e, lambda g: g.memset(out.ap, val), [], [out])


D = 1024
KC = 8
NEG_C = -0.6065306597126334
GN_EPS = 64e-5
RMS_EPS = 1e-6


def host_consts():
    import ml_dtypes
    bf = ml_dtypes.bfloat16
    c = {}
    c["c_identb"] = np.eye(128, dtype=np.float32).astype(bf)
    c["c_identf"] = np.eye(128, dtype=np.float32)
    l = np.arange(128)[:, None]
    t = np.arange(128)[None, :]
    tri = np.zeros((128, 3, 128), np.float32)
    tri[:, 0, :] = (l <= t) * NEG_C
    tri[:, 1, :] = (l < t) * NEG_C
    tri[:, 2, :] = NEG_C
    c["c_tri"] = tri
    c["c_onescol"] = np.full((128, 1), NEG_C, np.float32)
    mis = np.zeros((128, 256), np.float32)
    mis[:, 0:128] = (l <= t)
    mis[:, 128:256] = (l < t)
    c["c_maskIS"] = mis.astype(bf)
    c["c_maskSL"] = (l > t).astype(np.float32).astype(bf)
    c["c_ones"] = np.ones((128, 128), np.float32)
    return c


class MK:
    def __init__(self, T, NBS, TS=4):
        self.T = T
        self.NBS = NBS
        self.TS = TS
        self.NS = NBS * TS
        self.k = KB()
        k = self.k
        self.din = {}
        self.dout = {}
        self.c_identb = self.inp("c_identb", [128, 128], BF16)
        self.c_identf = self.inp("c_identf", [128, 128], F32)
        self.c_tri = self.inp("c_tri", [128, 3, 128], F32)
        self.c_onescol = self.inp("c_onescol", [128, 1], F32)
        self.c_maskIS = self.inp("c_maskIS", [128, 256], BF16)
        self.c_maskSL = self.inp("c_maskSL", [128, 128], BF16)
        self.c_ones = self.inp("c_ones", [128, 128], F32)
        self.banks = [k.ps("bank%d" % i, [128, 512], F32) for i in range(6)]
        self.bbanks = [k.ps("bbank%d" % i, [128, 1024], BF16) for i in range(2)]
        self.bank_i = 0
        self.bbank_i = 0
        self.identb = k.sb("identb", [128, 128], BF16)
        self.identf = k.sb("identf", [128, 128], F32)
        self.tri = k.sb("tri", [128, 3, 128], F32)
        self.onescol = k.sb("onescol", [128, 1], F32)
        self.maskIS = k.sb("maskIS", [128, 256], BF16)
        self.maskSL = k.sb("maskSL", [128, 128], BF16)
        self.ones = k.sb("ones", [128, 128], F32)
        for sbt, dr in ((self.identb, self.c_identb), (self.identf, self.c_identf), (self.tri, self.c_tri),
                        (self.onescol, self.c_onescol), (self.maskIS, self.c_maskIS), (self.maskSL, self.c_maskSL),
                        (self.ones, self.c_ones)):
            k.dma("sp", sbt[:], dr[:])
        self._stage_i = 0

    def inp(self, name, shape, dt=F32):
        t = self.k.dram(name, shape, dt, kind="ExternalInput")
        self.din[name] = t
        return t

    def outp(self, name, shape, dt=F32):
        t = self.k.dram(name, shape, dt, kind="ExternalOutput")
        self.dout[name] = t
        return t

    def bank(self):
        b = self.banks[self.bank_i % len(self.banks)]
        self.bank_i += 1
        return b

    def bbank(self):
        b = self.bbanks[self.bbank_i % len(self.bbanks)]
        self.bbank_i += 1
        return b

    def load_w_bf16(self, dst, src_ap, src_res, kc, n, stage, engs=("pool", "act")):
        k = self.k
        src3 = src_ap.rearrange("(c p) n -> p c n", p=128)
        per = max(1, 2048 // n)
        i = 0
        c = 0
        while c < kc:
            cc = min(per, kc - c)
            st = stage[self._stage_i % len(stage)]
            self._stage_i += 1
            stv = V(st.t[:, 0:cc * n].rearrange("p (c n) -> p c n", n=n), st.res)
            k.dma("sp", stv, V(src3[:, c:c + cc, :], src_res))
            k.cp(engs[i % len(engs)], dst[:, c:c + cc, :], stv)
            c += cc
            i += 1

    def load_rows_bf16(self, dst, src_ap, src_res, rows, n, stage):
        k = self.k
        st = stage[self._stage_i % len(stage)]
        self._stage_i += 1
        stv = V(st.t[0:rows, 0:n], st.res)
        k.dma("sp", stv, V(src_ap, src_res))
        k.cp("pool", dst[0:rows, :], stv)

    def bcast_load(self, dst, src_ap, src_res):
        self.k.dma("sp", dst[:], V(src_ap.partition_broadcast(128), src_res))

    def rwkv_phase1(self, j, li, xT_p, xT_s, W, scr, O):
        k = self.k
        T, NS, NBS, TS = self.T, self.NS, self.NBS, self.TS
        has_v = j > 0
        stage = [k.sb("stg%d" % i, [128, 2048], F32) for i in range(2)]
        Wr = k.sb("Wr", [128, KC, D], BF16)
        Wk = k.sb("Wk", [128, KC, D], BF16)
        Wv = k.sb("Wv", [128, KC, D], BF16)
        Wz = k.sb("Wz", [128, KC, D], BF16)
        rkvz = W["rw_w_rkvz"]
        for wi, dst in enumerate((Wr, Wk, Wv, Wz)):
            self.load_w_bf16(dst, rkvz.t[j, wi], rkvz.res, KC, D, stage)
        w1 = k.sb("w1", [128, KC, 64], BF16)
        a1 = k.sb("a1", [128, KC, 64], BF16)
        g1 = k.sb("g1", [128, KC, 160], BF16)
        self.load_w_bf16(w1, W["rw_w1"].t[j], W["rw_w1"].res, KC, 64, stage)
        self.load_w_bf16(a1, W["rw_a1"].t[j], W["rw_a1"].res, KC, 64, stage)
        self.load_w_bf16(g1, W["rw_g1"].t[j], W["rw_g1"].res, KC, 160, stage)
        w2 = k.sb("w2", [64, D], BF16)
        a2 = k.sb("a2", [64, D], BF16)
        g2a = k.sb("g2a", [128, D], BF16)
        g2b = k.sb("g2b", [32, D], BF16)
        self.load_rows_bf16(w2, W["rw_w2"].t[j], W["rw_w2"].res, 64, D, stage)
        self.load_rows_bf16(a2, W["rw_a2"].t[j], W["rw_a2"].res, 64, D, stage)
        self.load_rows_bf16(g2a, W["rw_g2"].t[j, 0:128], W["rw_g2"].res, 128, D, stage)
        self.load_rows_bf16(g2b, W["rw_g2"].t[j, 128:160], W["rw_g2"].res, 32, D, stage)
        if has_v:
            v1 = k.sb("v1", [128, KC, 32], BF16)
            v2 = k.sb("v2", [32, D], BF16)
            self.load_w_bf16(v1, W["rw_v1"].t[j - 1], W["rw_v1"].res, KC, 32, stage)
            self.load_rows_bf16(v2, W["rw_v2"].t[j - 1], W["rw_v2"].res, 32, D, stage)
            v0b = k.sb("v0b", [128, D])
            self.bcast_load(v0b, W["rw_v0"].t[j - 1], W["rw_v0"].res)
        w0b = k.sb("w0b", [128, D])
        a0b = k.sb("a0b", [128, D])
        self.bcast_load(w0b, W["rw_w0"].t[j], W["rw_w0"].res)
        self.bcast_load(a0b, W["rw_a0"].t[j], W["rw_a0"].res)
        mu = k.sb("mu", [128, 6, KC])
        k.dma("sp", mu[:], V(W["mu_fm"].t[j], W["mu_fm"].res))
        nw = k.sb("nw", [128, KC])
        k.dma("sp", nw[:], V(W["nw_fm"].t[li], W["nw_fm"].res))

        NTK = 128
        xT = k.sb("xT", [128, KC, NTK])
        sq = k.sb("sq", [128, KC, NTK])
        rstd = k.sb("rstd", [128, NTK])
        hT = k.sb("hT", [128, KC, NTK + 32])
        xx = k.sb("xx", [128, KC, NTK])
        xi = [k.sb("xi%d" % i, [128, KC, NTK], BF16) for i in range(6)]
        lo_w = k.sb("lo_w", [64, NTK], BF16)
        lo_a = k.sb("lo_a", [64, NTK], BF16)
        lo_ga = k.sb("lo_ga", [128, NTK], BF16)
        lo_gb = k.sb("lo_gb", [32, NTK], BF16)
        lo_v = k.sb("lo_v", [32, NTK], BF16) if has_v else None
        st_r = k.sb("st_r", [128, D])
        st_k = k.sb("st_k", [128, D])
        st_v = k.sb("st_v", [128, D])
        st_w = k.sb("st_w", [128, D])
        st_a = k.sb("st_a", [128, D])
        st_g = k.sb("st_g", [128, D])
        st_z = k.sb("st_z", [128, D])
        st_vf = k.sb("st_vf", [128, D]) if has_v else None
        shiftT = k.sb("shiftT", [128, KC, max(NBS, 1)])
        shc = k.sb("shc", [128, KC, max(NBS, 1)])
        shcp = k.sb("shcp", [128, KC])

        def tile(xT_d, col0, ntok, nb, tl, first, pref, row0, shift_out):
            xTv = V(xT.t[:, :, 0:ntok], xT.res)
            k.dma("sp", xTv, V(xT_d.t[:, col0:col0 + ntok].rearrange("(c p) t -> p c t", p=128), xT_d.res))
            sqv = V(sq.t[:, :, 0:ntok], sq.res)
            k.tt("pool", sqv, xTv, xTv, ALU.mult)
            b = self.bank()
            for c in range(KC):
                k.mm(b[:, 0:ntok], self.ones[:], V(sq.t[:, c, 0:ntok], sq.res), start=(c == 0), stop=(c == KC - 1))
            rs = V(rstd.t[:, 0:ntok], rstd.res)
            k.ts("dve", rs, b[:, 0:ntok], 1.0 / D, RMS_EPS, op0=ALU.mult, op1=ALU.add)
            k.act(rs, rs, AF.Sqrt)
            k.op("dve", lambda g: g.reciprocal(rs.ap, rs.ap), [rs], [rs])
            hv4 = hT.t[:, :, 0:nb * (tl + 1)].rearrange("p c (b t) -> p c b t", t=tl + 1)
            if nb == 1:
                if first:
                    k.memset("pool", V(hv4[:, :, :, 0:1], hT.res), 0.0)
                else:
                    k.cp("pool", V(hv4[:, :, :, 0:1], hT.res), V(hv4[:, :, :, tl:tl + 1], hT.res))
            else:
                k.cp("pool", V(hv4[:, :, :, 0], hT.res), V(shiftT.t[:, :, 0:nb], shiftT.res))
            rs3 = V(rstd.t[:, 0:ntok].rearrange("p (b t) -> p b t", t=tl), rstd.res)
            for c in range(KC):
                k.stt("dve", V(hv4[:, c, :, 1:tl + 1], hT.res),
                      V(xT.t[:, c, 0:ntok].rearrange("p (b t) -> p b t", t=tl), xT.res),
                      nw[:, c:c + 1], rs3, ALU.mult, ALU.mult)
            shift_out(hv4)
            xxv4 = xx.t[:, :, 0:ntok].rearrange("p c (b t) -> p c b t", t=tl)
            for c in range(KC):
                k.tt("dve" if c % 2 == 0 else "pool", V(xxv4[:, c], xx.res), V(hv4[:, c, :, 0:tl], hT.res), V(hv4[:, c, :, 1:tl + 1], hT.res), ALU.subtract)
            for i in range(6):
                for c in range(KC):
                    e = "dve" if (i * KC + c) % 2 == 0 else "pool"
                    k.stt(e, V(xi[i].t[:, c, 0:ntok].rearrange("p (b t) -> p b t", t=tl), xi[i].res),
                          V(xxv4[:, c], xx.res), mu[:, i, c:c + 1], V(hv4[:, c, :, 1:tl + 1], hT.res),
                          ALU.mult, ALU.add)
            xr, xw, xk, xv, xa, xg = xi

            def lora1(xin, w, c0, r, dst, func):
                bb = self.bank()
                for c in range(KC):
                    k.mm(bb[0:r, 0:ntok], V(w.t[:, c, c0:c0 + r], w.res), V(xin.t[:, c, 0:ntok], xin.res), start=(c == 0), stop=(c == KC - 1))
                if func is None:
                    k.cp("act", V(dst.t[0:r, 0:ntok], dst.res), bb[0:r, 0:ntok])
                else:
                    k.act(V(dst.t[0:r, 0:ntok], dst.res), bb[0:r, 0:ntok], func)
            lora1(xw, w1, 0, 64, lo_w, AF.Tanh)
            lora1(xa, a1, 0, 64, lo_a, None)
            lora1(xg, g1, 0, 128, lo_ga, AF.Sigmoid)
            lora1(xg, g1, 128, 32, lo_gb, AF.Sigmoid)
            if has_v:
                lora1(xv, v1, 0, 32, lo_v, None)
                vf = scr[pref + "vf"]
                k.dma("sp", V(st_vf.t[0:ntok, :], st_vf.res), V(vf.t[row0:row0 + ntok, :], vf.res))
            for half in range(2):
                cs = slice(half * 512, (half + 1) * 512)

                def sv(tn):
                    return V(tn.t[0:ntok, cs], tn.res)

                def proj(xin, w):
                    bb = self.bank()
                    for c in range(KC):
                        k.mm(bb[0:ntok, :], V(xin.t[:, c, 0:ntok], xin.res), V(w.t[:, c, cs], w.res), start=(c == 0), stop=(c == KC - 1))
                    return bb
                bb = proj(xr, Wr)
                k.cp("act", sv(st_r), bb[0:ntok, :])
                bb = proj(xk, Wk)
                k.cp("act", sv(st_k), bb[0:ntok, :])
                bb = proj(xv, Wv)
                k.cp("act", sv(st_v), bb[0:ntok, :])
                bb = self.bank()
                k.mm(bb[0:ntok, :], V(lo_w.t[0:64, 0:ntok], lo_w.res), V(w2.t[0:64, cs], w2.res))
                k.tt("dve", sv(st_w), bb[0:ntok, :], sv(w0b), ALU.add)
                k.act(sv(st_w), sv(st_w), AF.Sigmoid)
                bb = self.bank()
                k.mm(bb[0:ntok, :], V(lo_a.t[0:64, 0:ntok], lo_a.res), V(a2.t[0:64, cs], a2.res))
                k.tt("dve", sv(st_a), bb[0:ntok, :], sv(a0b), ALU.add)
                k.act(sv(st_a), sv(st_a), AF.Sigmoid)
                if has_v:
                    bb = self.bank()
                    k.mm(bb[0:ntok, :], V(lo_v.t[0:32, 0:ntok], lo_v.res), V(v2.t[0:32, cs], v2.res))
                    k.tt("dve", sv(st_z), bb[0:ntok, :], sv(v0b), ALU.add)
                    k.act(sv(st_z), sv(st_z), AF.Sigmoid)
                    k.tt("pool", sv(st_vf), sv(st_vf), sv(st_v), ALU.subtract)
                    k.tt("pool", sv(st_vf), sv(st_vf), sv(st_z), ALU.mult)
                    k.tt("pool", sv(st_v), sv(st_v), sv(st_vf), ALU.add)
                bb = proj(xg, Wz)
                k.act(sv(st_z), bb[0:ntok, :], AF.Silu)
                bb = self.bank()
                k.mm(bb[0:ntok, :], V(lo_ga.t[:, 0:ntok], lo_ga.res), V(g2a.t[:, cs], g2a.res), start=True, stop=False)
                k.mm(bb[0:ntok, :], V(lo_gb.t[0:32, 0:ntok], lo_gb.res), V(g2b.t[0:32, cs], g2b.res), start=False, stop=True)
                k.tt("dve", sv(st_g), bb[0:ntok, :], sv(st_z), ALU.mult)
            for nm, st in (("r", st_r), ("k", st_k), ("v", st_v), ("w", st_w), ("a", st_a), ("g", st_g)):
                d = scr[pref + nm]
                k.dma("sp", V(d.t[row0:row0 + ntok, :], d.res), V(st.t[0:ntok, :], st.res), semres=st.res)
                if nm == "v" and not has_v:
                    d = scr[pref + "vf"]
                    k.dma("sp", V(d.t[row0:row0 + ntok, :], d.res), V(st.t[0:ntok, :], st.res), semres=st.res)

        nt = T // 128
        for ti in range(nt):
            def so(hv4, last=(ti == nt - 1)):
                if last:
                    k.cp("pool", shcp[:, :], V(hv4[:, :, 0, 128], hT.res))
                    k.dma("sp", V(O["o_p_shift"].t[j], O["o_p_shift"].res), shcp[:, :], semres=shcp.res)
            tile(xT_p, ti * 128, 128, 1, 128, ti == 0, "p_", ti * 128, so)
        import os
        if NS > 0 and "s" not in os.environ.get("SKIP", ""):
            k.dma("sp", shiftT[:, :, 0:NBS], V(W["shift_fm"].t[j], W["shift_fm"].res))

            def so2(hv4):
                k.cp("pool", shc[:, :, 0:NBS], V(hv4[:, :, :, TS], hT.res))
                k.dma("sp", V(O["o_s_shift"].t[j], O["o_s_shift"].res), shc[:, :, 0:NBS], semres=shc.res)
            tile(xT_s, 0, NS, NBS, TS, True, "s_", 0, so2)

    def rwkv_phase2(self, j, li, xT_p, xT_s, W, scr, O):
        k = self.k
        T, NS, NBS, TS = self.T, self.NS, self.NBS, self.TS
        Wo = k.sb("Wo", [128, KC, D], BF16)
        with k.scope():
            stage = [k.sb("stg%d" % i, [128, 2048], F32) for i in range(2)]
            self.load_w_bf16(Wo, W["rw_w_o"].t[j], W["rw_w_o"].res, KC, D, stage)
        kkb = k.sb("kkb", [128, D])
        kab = k.sb("kab", [128, D])
        rkb = k.sb("rkb", [128, D])
        lnw = k.sb("lnw", [128, D])
        lnb = k.sb("lnb", [128, D])
        self.bcast_load(kkb, W["rw_k_k"].t[j], W["rw_k_k"].res)
        self.bcast_load(kab, W["rw_k_a"].t[j], W["rw_k_a"].res)
        self.bcast_load(rkb, W["rw_r_k"].t[j].rearrange("h c -> (h c)"), W["rw_r_k"].res)
        self.bcast_load(lnw, W["rw_ln_w"].t[j], W["rw_ln_w"].res)
        self.bcast_load(lnb, W["rw_ln_b"].t[j], W["rw_ln_b"].res)
        t_r = k.sb("t_r", [128, D])
        t_k = k.sb("t_k", [128, D])
        t_v = k.sb("t_v", [128, D])
        t_w = k.sb("t_w", [128, D])
        t_a = k.sb("t_a", [128, D])
        t_g = k.sb("t_g", [128, D])
        t_kk = k.sb("t_kk", [128, D])
        t_k2 = k.sb("t_k2", [128, D])
        t_bv = k.sb("t_bv", [128, D])
        t_tmp = k.sb("t_tmp", [128, D])
        t_y = k.sb("t_y", [128, D])
        ob = k.sb("ob", [128, D], BF16)
        oT = k.sb("oT", [128, KC, 128], BF16)
        xT = k.sb("xT", [128, KC, 128])
        sm = {n: k.sb("sm_" + n, [128, 16]) for n in ("ss", "rn", "s1", "s2", "mean", "msq", "var", "rk")}

        def hv(tn, n):
            return V(tn.t[0:n, :].rearrange("p (h c) -> p h c", c=64), tn.res)

        def bc(tn, n):
            return V(tn.t[0:n, :].unsqueeze(2).broadcast_to([n, 16, 64]), tn.res)

        def rows(tn, n):
            return V(tn.t[0:n, :], tn.res)

        def prep(pref, row0, n):
            for nm, tn in (("r", t_r), ("k", t_k), ("v", t_v), ("w", t_w), ("a", t_a), ("g", t_g)):
                d = scr[pref + nm]
                k.dma("sp", rows(tn, n), V(d.t[row0:row0 + n, :], d.res))
            k.tt("pool", rows(t_kk, n), rows(t_k, n), rows(kkb, n), ALU.mult)
            k.tt("pool", rows(t_tmp, n), rows(t_kk, n), rows(t_kk, n), ALU.mult)
            ss = V(sm["ss"].t[0:n, :], sm["ss"].res)
            rn = V(sm["rn"].t[0:n, :], sm["rn"].res)
            k.red("dve", ss, hv(t_tmp, n))
            k.ts("dve", ss, ss, 1e-24, op0=ALU.max)
            k.act(ss, ss, AF.Sqrt)
            k.op("dve", lambda g: g.reciprocal(rn.ap, ss.ap), [ss], [rn])
            k.tt("dve", hv(t_kk, n), hv(t_kk, n), bc(sm["rn"], n), ALU.mult)
            k.stt("pool", rows(t_tmp, n), rows(t_a, n), -1.0, rows(kab, n), ALU.add, ALU.mult)
            k.stt("pool", rows(t_k2, n), rows(t_tmp, n), 1.0, rows(t_k, n), ALU.add, ALU.mult)
            k.tt("pool", rows(t_bv, n), rows(t_kk, n), rows(t_a, n), ALU.mult)

        def post(ysrc, n, xT_d, col0, par=False):
            if par:
                ty4 = t_y.t[0:n, :].rearrange("p (c two n) -> p c two n", two=2, n=64)
                k.cp("act", V(ty4[:, :, 0, :], t_y.res), ysrc[0])
                k.cp("act", V(ty4[:, :, 1, :], t_y.res), ysrc[1])
            else:
                k.cp("act", V(t_y.t[0:n, 0:512], t_y.res), ysrc[0])
                k.cp("act", V(t_y.t[0:n, 512:1024], t_y.res), ysrc[1])
            s1 = V(sm["s1"].t[0:n, :], sm["s1"].res)
            s2 = V(sm["s2"].t[0:n, :], sm["s2"].res)
            mean = V(sm["mean"].t[0:n, :], sm["mean"].res)
            msq = V(sm["msq"].t[0:n, :], sm["msq"].res)
            var = V(sm["var"].t[0:n, :], sm["var"].res)
            rk = V(sm["rk"].t[0:n, :], sm["rk"].res)
            k.red("dve", s1, hv(t_y, n))
            k.tt("pool", rows(t_tmp, n), rows(t_y, n), rows(t_y, n), ALU.mult)
            k.red("dve", s2, hv(t_tmp, n))
            k.ts("dve", mean, s1, 1.0 / 64, op0=ALU.mult)
            k.tt("dve", msq, mean, mean, ALU.mult)
            k.stt("dve", var, s2, 1.0 / 64, msq, ALU.mult, ALU.subtract)
            k.ts("dve", var, var, GN_EPS, op0=ALU.add)
            k.act(var, var, AF.Sqrt)
            k.op("dve", lambda g: g.reciprocal(var.ap, var.ap), [var], [var])
            k.tt("dve", hv(t_y, n), hv(t_y, n), bc(sm["mean"], n), ALU.subtract)
            k.tt("dve", hv(t_y, n), hv(t_y, n), bc(sm["var"], n), ALU.mult)
            k.tt("pool", rows(t_y, n), rows(t_y, n), rows(lnw, n), ALU.mult)
            k.tt("pool", rows(t_y, n), rows(t_y, n), rows(lnb, n), ALU.add)
            k.tt("pool", rows(t_tmp, n), rows(t_r, n), rows(t_k2, n), ALU.mult)
            k.tt("pool", rows(t_tmp, n), rows(t_tmp, n), rows(rkb, n), ALU.mult)
            k.red("dve", rk, hv(t_tmp, n))
            k.tt("dve", hv(t_tmp, n), hv(t_v, n), bc(sm["rk"], n), ALU.mult)
            k.tt("pool", rows(t_y, n), rows(t_y, n), rows(t_tmp, n), ALU.add)
            k.tt("dve", V(ob.t[0:n, :], ob.res), rows(t_y, n), rows(t_g, n), ALU.mult)
            bb = self.bbank()
            for c in range(KC):
                k.tr(bb[:, c * 128:c * 128 + n], V(ob.t[0:n, c * 128:(c + 1) * 128], ob.res), V(self.identb.t[0:n, 0:n], self.identb.res))
            k.cp("act", V(oT.t[:, :, 0:n], oT.res), V(bb.t[:, :].rearrange("p (c t) -> p c t", t=128)[:, :, 0:n], bb.res))
            xTv = V(xT.t[:, :, 0:n], xT.res)
            k.dma("sp", xTv, V(xT_d.t[:, col0:col0 + n].rearrange("(c p) t -> p c t", p=128), xT_d.res))
            for hb in range(2):
                b = self.bank()
                for dq in range(4):
                    dc = hb * 4 + dq
                    for c in range(KC):
                        k.mm(b[:, dq * 128:dq * 128 + n], V(Wo.t[:, c, dc * 128:(dc + 1) * 128], Wo.res), V(oT.t[:, c, 0:n], oT.res), start=(c == 0), stop=(c == KC - 1))
                k.tt("dve", V(xT.t[:, hb * 4:hb * 4 + 4, 0:n], xT.res), V(xT.t[:, hb * 4:hb * 4 + 4, 0:n], xT.res),
                     V(b.t[:, :].rearrange("p (c t) -> p c t", t=128)[:, :, 0:n], b.res), ALU.add)
            k.dma("sp", V(xT_d.t[:, col0:col0 + n].rearrange("(c p) t -> p c t", p=128), xT_d.res), xTv, semres=xT.res)

        import os
        for _once in ([] if "P" in os.environ.get("SKIP", "") else [0]):
          with k.scope():
              rt = k.sb("rt", [128, D], BF16)
              at = k.sb("at", [128, D], BF16)
              kt = k.sb("kt", [128, D], BF16)
              bt = k.sb("bt", [128, D], BF16)
              kg = k.sb("kg", [128, D], BF16)
              bg = k.sb("bg", [128, D], BF16)
              vb = k.sb("vb", [128, D], BF16)
              RA = k.sb("RA", [128, KC, 256], BF16)
              KT = k.sb("KT", [128, KC, 128], BF16)
              BT = k.sb("BT", [128, KC, 128], BF16)
              gC = k.sb("gC", [128, KC])
              A1 = [k.sb("A1_%d" % g, [128, 4, 256], BF16) for g in range(4)]
              A2 = [k.sb("A2_%d" % g, [128, 4, 256], BF16) for g in range(4)]
              P = [[k.sb("P%d_%d" % (i, g), [128, 4, 128], BF16) for g in range(4)] for i in range(2)]
              PT = [[k.sb("PT%d_%d" % (i, g), [128, 4, 128], BF16) for g in range(4)] for i in range(2)]
              X = [k.sb("X_%d" % g, [128, 4, 128]) for g in range(4)]
              Zb = [k.sb("Zb_%d" % g, [128, 4, 128], BF16) for g in range(4)]
              AhT = k.sb("AhT", [128, KC, 128], BF16)
              Ub = k.sb("Ub", [128, 1024], BF16)
              S = k.sb("S", [128, KC, 64])
              Sb = k.sb("Sb", [128, KC, 64], BF16)
              k.memset("pool", S[:], 0.0)
              k.memset("pool", Sb[:], 0.0)
              STOP = int(os.environ.get("STOP", "99"))
              SUB = int(os.environ.get("SUB", "99"))
              for ti in range(T // 128):
                  n = 128
                  prep("p_", ti * 128, n)
                  if STOP <= 0:
                      continue
                  Ea, Eb = t_a, t_w
                  cb = []
                  for which in range(3):
                      for half in range(2):
                          b = self.bank()
                          k.mm(b[:, :], V(self.tri.t[:, which, :], self.tri.res), V(t_w.t[:, half * 512:(half + 1) * 512], t_w.res))
                          cb.append(b)
                  def hs(tn, half):
                      return V(tn.t[:, half * 512:(half + 1) * 512], tn.res)
                  for half in range(2):
                      k.act(hs(Ea, half), cb[half][:, :], AF.Exp)
                  k.tt("dve", rt[:], t_r[:], Ea[:], ALU.mult)
                  for half in range(2):
                      k.act(hs(Ea, half), cb[half][:, :], AF.Exp, scale=-1.0)
                  k.tt("pool", kt[:], t_k2[:], Ea[:], ALU.mult)
                  k.tt("pool", bt[:], t_bv[:], Ea[:], ALU.mult)
                  b = self.bank()
                  for c in range(KC):
                      k.mm(b[:, c:c + 1], V(t_w.t[:, c * 128:(c + 1) * 128], t_w.res), self.onescol[:])
                  k.act(gC[:], b[:, 0:KC], AF.Exp)
                  for half in range(2):
                      k.act(hs(Eb, half), cb[2 + half][:, :], AF.Exp)
                  k.stt("dve", at[:], t_kk[:], -1.0, Eb[:], ALU.mult, ALU.mult)
                  for half in range(2):
                      k.act(hs(Eb, half), cb[4 + half][:, :], AF.Exp)
                  k.tt("pool", Eb[:], Eb[:], Ea[:], ALU.mult)
                  k.tt("dve", kg[:], t_k2[:], Eb[:], ALU.mult)
                  k.tt("pool", bg[:], t_bv[:], Eb[:], ALU.mult)
                  k.cp("pool", vb[:], t_v[:])
                  if STOP <= 1:
                      continue
                  for src, dst, off in ((rt, RA, 0), (at, RA, 128), (kt, KT, 0), (bt, BT, 0)):
                      bb = self.bbank()
                      for c in range(KC):
                          k.tr(bb[:, c * 128:(c + 1) * 128], V(src.t[:, c * 128:(c + 1) * 128], src.res), self.identb[:])
                      k.cp("act", V(dst.t[:, :, off:off + 128], dst.res), V(bb.t[:, :].rearrange("p (c t) -> p c t", t=128), bb.res))
                  if STOP <= 2:
                      continue
                  for g in range(4):
                      for par in range(2):
                          hb = 64 * par
                          b1 = self.bank()
                          b2 = self.bank()
                          for qq in range(2):
                              h = 4 * g + par + 2 * qq
                              c8 = h // 2
                              k.mm(b1[:, qq * 256:(qq + 1) * 256], V(KT.t[hb:hb + 64, c8, :], KT.res), V(RA.t[hb:hb + 64, c8, :], RA.res))
                              k.mm(b2[:, qq * 256:(qq + 1) * 256], V(BT.t[hb:hb + 64, c8, :], BT.res), V(RA.t[hb:hb + 64, c8, :], RA.res))
                          mIS = V(self.maskIS.t[:, :].unsqueeze(1).broadcast_to([128, 2, 256]), self.maskIS.res)
                          k.tt("dve", V(A1[g].t[:, par:4:2, :], A1[g].res), V(b1.t[:, :].rearrange("p (q n) -> p q n", n=256), b1.res), mIS, ALU.mult)
                          k.tt("dve", V(A2[g].t[:, par:4:2, :], A2[g].res), V(b2.t[:, :].rearrange("p (q n) -> p q n", n=256), b2.res), mIS, ALU.mult)
                      for par in range(2):
                          hb = 64 * par
                          b3 = self.bank()
                          for qq in range(2):
                              h = 4 * g + par + 2 * qq
                              c8 = h // 2
                              k.mm(b3[:, qq * 128:(qq + 1) * 128], V(RA.t[hb:hb + 64, c8, 128:256], RA.res), V(BT.t[hb:hb + 64, c8, :], BT.res))
                          mSL = V(self.maskSL.t[:, :].unsqueeze(1).broadcast_to([128, 2, 128]), self.maskSL.res)
                          k.tt("dve", V(P[0][g].t[:, par:4:2, :], P[0][g].res), V(b3.t[:, 0:256].rearrange("p (q n) -> p q n", n=128), b3.res), mSL, ALU.mult)
                  for g in range(4):
                      b4 = self.bank()
                      for q in range(4):
                          h = 4 * g + q
                          k.mm(b4[:, q * 64:(q + 1) * 64], V(A1[g].t[:, q, 128:256], A1[g].res), V(vb.t[:, h * 64:(h + 1) * 64], vb.res))
                      k.cp("act", V(X[g].t[:, :, 64:128], X[g].res), V(b4.t[:, 0:256].rearrange("p (q n) -> p q n", n=64), b4.res))
                      k.cp("pool", V(X[g].t[:, :, 0:64], X[g].res), V(at.t[:, g * 256:(g + 1) * 256].rearrange("p (q n) -> p q n", n=64), at.res))
                      k.cp("pool", Zb[g][:], X[g][:])
                  if STOP <= 3:
                      continue
                  for st in range(7):
                      for g in range(4):
                          cur, nxt = st % 2, (st + 1) % 2
                          Pk = P[cur][g]
                          if st == 0:
                              PTk_v = lambda q, g=g: V(A2[g].t[:, q, 128:256], A2[g].res)
                          else:
                              PTk_v = lambda q, g=g, cur=cur: V(PT[cur][g].t[:, q, :], PT[cur][g].res)
                          bz = self.bank()
                          for q in range(4):
                              k.mm(bz[:, q * 128:(q + 1) * 128], PTk_v(q), V(Zb[g].t[:, q, :], Zb[g].res))
                          if st < 6:
                              bp = self.bank()
                              bpt = self.bank()
                              for q in range(4):
                                  k.mm(bp[:, q * 128:(q + 1) * 128], PTk_v(q), V(Pk.t[:, q, :], Pk.res))
                                  k.mm(bpt[:, q * 128:(q + 1) * 128], V(Pk.t[:, q, :], Pk.res), PTk_v(q))
                          k.tt("dve", X[g][:], X[g][:], V(bz.t[:, :].rearrange("p (q n) -> p q n", n=128), bz.res), ALU.add)
                          k.cp("pool", Zb[g][:], X[g][:])
                          if st < 6:
                              k.cp("act", P[nxt][g][:], V(bp.t[:, :].rearrange("p (q n) -> p q n", n=128), bp.res))
                              k.cp("act", PT[nxt][g][:], V(bpt.t[:, :].rearrange("p (q n) -> p q n", n=128), bpt.res))
                  if STOP <= 4:
                      continue
                  bbs = [self.bbank(), self.bbank()]
                  for h in range(16):
                      g, q = h // 4, h % 4
                      zf = Zb[g].t[:, :, :].rearrange("p q n -> p (q n)")
                      lo = q * 128 - (64 if h % 2 == 1 else 0)
                      k.tr(bbs[h // 8][:, (h % 8) * 128:(h % 8 + 1) * 128], V(zf[:, lo:lo + 128], Zb[g].res), self.identb[:])
                  for bi in range(2):
                      bv3 = bbs[bi].t[:, :].rearrange("p (c two t) -> p c two t", two=2, t=128)
                      k.cp("act", V(AhT.t[0:64, 4 * bi:4 * bi + 4, :], AhT.res), V(bv3[0:64, :, 0, :], bbs[bi].res))
                      k.cp("act", V(AhT.t[64:128, 4 * bi:4 * bi + 4, :], AhT.res), V(bv3[64:128, :, 1, :], bbs[bi].res))
                  if STOP <= 5:
                      continue
                  bU = [self.bank(), self.bank()]
                  for h in range(16):
                      c8, par = h // 2, h % 2
                      hb = 64 * par
                      k.mm(bU[par][:, c8 * 64:(c8 + 1) * 64], V(AhT.t[hb:hb + 64, c8, :], AhT.res), V(Sb.t[hb:hb + 64, c8, :], Sb.res))
                  Ub4 = Ub.t[:, :].rearrange("p (c two n) -> p c two n", two=2, n=64)
                  for g in range(4):
                      for par in range(2):
                          k.tt("dve", V(Ub4[:, 2 * g:2 * g + 2, par, :], Ub.res),
                               V(bU[par].t[:, :].rearrange("p (c n) -> p c n", n=64)[:, 2 * g:2 * g + 2, :], bU[par].res),
                               V(X[g].t[:, par:4:2, 64:128], X[g].res), ALU.add)
                  bY = [self.bank(), self.bank()]
                  for h in range(16):
                      c8, par = h // 2, h % 2
                      hb = 64 * par
                      g, q = h // 4, h % 4
                      o = bY[par][:, c8 * 64:(c8 + 1) * 64]
                      k.mm(o, V(RA.t[hb:hb + 64, c8, 0:128], RA.res), V(Sb.t[hb:hb + 64, c8, :], Sb.res), start=True, stop=False)
                      k.mm(o, V(A2[g].t[:, q, 0:128], A2[g].res), V(Ub.t[:, h * 64:(h + 1) * 64], Ub.res), start=False, stop=False)
                      k.mm(o, V(A1[g].t[:, q, 0:128], A1[g].res), V(vb.t[:, h * 64:(h + 1) * 64], vb.res), start=False, stop=True)
                  bS = [self.bank(), self.bank()]
                  for c8 in range(KC):
                      o = bS[c8 // 4][:, (c8 % 4) * 128:(c8 % 4 + 1) * 128]
                      k.mm(o, V(bg.t[:, c8 * 128:(c8 + 1) * 128], bg.res), V(Ub.t[:, c8 * 128:(c8 + 1) * 128], Ub.res), start=True, stop=False)
                      k.mm(o, V(kg.t[:, c8 * 128:(c8 + 1) * 128], kg.res), V(vb.t[:, c8 * 128:(c8 + 1) * 128], vb.res), start=False, stop=True)
                  for hh in range(2):
                      hb = 64 * hh
                      k.tt("pool", V(S.t[hb:hb + 64, :, :], S.res), V(S.t[hb:hb + 64, :, :], S.res),
                           V(gC.t[hb:hb + 64, :].unsqueeze(2).broadcast_to([64, KC, 64]), gC.res), ALU.mult)
                      for bi in range(2):
                          k.tt("dve", V(S.t[hb:hb + 64, 4 * bi:4 * bi + 4, :], S.res), V(S.t[hb:hb + 64, 4 * bi:4 * bi + 4, :], S.res),
                               V(bS[bi].t[hb:hb + 64, :].rearrange("p (c n) -> p c n", n=128)[:, :, hb:hb + 64], bS[bi].res), ALU.add)
                  k.cp("act", Sb[:], S[:])
                  if STOP <= 6:
                      continue
                  post([V(bY[p_].t[:, :].rearrange("p (c n) -> p c n", n=64), bY[p_].res) for p_ in range(2)], 128, xT_p, ti * 128, par=True)
              k.dma("sp", V(O["o_p_wkv"].t[j], O["o_p_wkv"].res), S[:], semres=S.res)

        import os
        for _once in ([] if "S" in os.environ.get("SKIP", "") else [0]):
          with k.scope():
              n = NS
              NP = NBS * 8
              prep("s_", 0, n)
              k.act(rows(t_w, n), rows(t_w, n), AF.Exp, scale=NEG_C)
              k.ts("pool", rows(t_kk, n), rows(t_kk, n), -1.0, op0=ALU.mult)
              sc = scr["s_scan"]
              for qi, tn in enumerate((t_r, t_w, t_k2, t_v, t_kk, t_bv)):
                  k.dma("sp", V(sc.t[qi], sc.res), rows(tn, n), semres=tn.res)
              Ss = k.sb("Ss", [128, 2, 64, 64])
              tmp = k.sb("Stmp", [128, 2, 64, 64])
              qin = k.sb("qin", [128, 6, TS, 128])
              sa = k.sb("sa", [128, 2, 64])
              ys = k.sb("ys", [128, TS, 128])
              st_in = W["state_wkv"]
              for g in range(8):
                  k.dma("sp", V(Ss.t[g * NBS:(g + 1) * NBS], Ss.res),
                        V(st_in.t[j][:, 2 * g:2 * g + 2], st_in.res))
                  for qi in range(6):
                      k.dma("sp", V(qin.t[g * NBS:(g + 1) * NBS, qi], qin.res),
                            V(sc.t[qi][:, g * 128:(g + 1) * 128].rearrange("(b t) c -> b t c", t=TS), sc.res))

              def bi_(qi, t):
                  return V(qin.t[0:NP, qi, t, :].rearrange("p (h c) -> p h c", c=64).unsqueeze(2).broadcast_to([NP, 2, 64, 64]), qin.res)

              def bj_(ap, res):
                  return V(ap.unsqueeze(3).broadcast_to([NP, 2, 64, 64]), res)
              Sv = V(Ss.t[0:NP], Ss.res)
              Tv = V(tmp.t[0:NP], tmp.res)
              sav = V(sa.t[0:NP], sa.res)
              for t in range(TS):
                  k.tt("dve", Tv, Sv, bi_(4, t), ALU.mult)
                  k.red("dve", sav, Tv)
                  k.tt("pool", Sv, Sv, bi_(1, t), ALU.mult)
                  k.tt("dve", Tv, bj_(sa.t[0:NP], sa.res), bi_(5, t), ALU.mult)
                  k.tt("pool", Sv, Sv, Tv, ALU.add)
                  k.tt("dve", Tv, bj_(qin.t[0:NP, 3, t, :].rearrange("p (h c) -> p h c", c=64), qin.res), bi_(2, t), ALU.mult)
                  k.tt("pool", Sv, Sv, Tv, ALU.add)
                  k.tt("dve", Tv, Sv, bi_(0, t), ALU.mult)
                  k.red("dve", V(ys.t[0:NP, t, :].rearrange("p (h c) -> p h c", c=64), ys.res), Tv)
              yd = scr["s_y"]
              for g in range(8):
                  k.dma("sp", V(O["o_s_wkv"].t[j][:, 2 * g:2 * g + 2], O["o_s_wkv"].res), V(Ss.t[g * NBS:(g + 1) * NBS], Ss.res), semres=Ss.res)
                  k.dma("sp", V(yd.t[:, g * 128:(g + 1) * 128].rearrange("(b t) c -> b t c", t=TS), yd.res), V(ys.t[g * NBS:(g + 1) * NBS], ys.res), semres=ys.res)
              k.dma("sp", rows(t_tmp, n), V(yd.t[:, :], yd.res))
              k.cp("pool", rows(t_kk, n), rows(t_tmp, n))
              post([V(t_kk.t[0:n, 0:512], t_kk.res), V(t_kk.t[0:n, 512:1024], t_kk.res)], n, xT_s, 0)


OA = 2176
LA = 6656
RS1 = LA + 128
RS16 = LA + 2048
OW = 128
LW = 1024
RSW = LW + 128
NEGBIG = -30000.0
NIN = 3632


def rel_bucket_np(d):
    n = np.maximum(d, 0)
    nf = np.maximum(n, 1).astype(np.float32)
    large = 16 + (np.log(nf / np.float32(16)) / np.float32(np.log(1024 / 16)) * np.float32(16)).astype(np.int32)
    large = np.minimum(large, 31)
    return np.where(n < 16, n, large)


def nsa_host_consts(T, TS=4, PAST=2048):
    import ml_dtypes
    bf = ml_dtypes.bfloat16
    c = {}
    dA = np.arange(LA) - OA
    oh = np.zeros((33, LA), np.float32)
    bA = rel_bucket_np(dA)
    oh[bA, np.arange(LA)] = (dA >= 0)
    oh[32] = (dA < 0)
    c["c_ohA"] = oh
    dW = np.arange(LW) - OW
    ohw = np.zeros((33, LW), np.float32)
    okw = (dW >= 0) & (dW < 512)
    ohw[rel_bucket_np(dW), np.arange(LW)] = okw
    ohw[32] = ~okw
    c["c_ohW"] = ohw
    ntile = max(T // 128, (PAST + TS + 127) // 128)
    ex = np.zeros((64, ntile, 128), np.float32)
    for kt in range(ntile):
        for cc in range(128):
            s = 2 * kt + cc // 64
            if s < 64:
                ex[s, kt, cc] = 1
    c["c_expand"] = ex.astype(bf)
    exs = np.zeros((64, 2, 128), np.float32)
    for p_ in range(128):
        exs[p_ // 4, 0, p_] = 1
    exs[32, 1, :] = 1
    c["c_expand_s"] = exs.astype(bf)
    c["c_s8"] = (np.arange(128) % 8).astype(np.int32).reshape(128, 1)
    c["_ntile"] = ntile
    nq = T // 128
    fm = np.zeros((max(nq, 1), 128, 64), np.float32)
    for qt in range(nq):
        qpos = qt * 128 + np.arange(128)
        cur = qpos // 64
        blk = np.arange(64)[None, :]
        forced = (blk == 0) | (blk == cur[:, None]) | (blk == cur[:, None] - 1)
        future = (blk * 64) > qpos[:, None]
        fm[qt] = np.where(future, -1e4, np.where(forced, 1e4, 0.0))
    c["c_fm_p"] = fm
    qpos = PAST + np.arange(TS)
    cur = qpos // 64
    blk = np.arange(64)[None, :]
    forced = (blk == 0) | (blk == cur[:, None]) | (blk == cur[:, None] - 1)
    future = (blk * 64) > qpos[:, None]
    c["c_fm_s"] = np.where(future, -1e4, np.where(forced, 1e4, 0.0)).astype(np.float32)
    def ov(n_cmp, n_sel):
        cs = np.arange(n_cmp)[:, None] * 16
        ss = np.arange(n_sel)[None, :] * 64
        o = np.maximum(np.minimum(cs + 32, ss + 64) - np.maximum(cs, ss), 0)
        return o.astype(np.float32) / 32
    ovp = np.zeros((256, 63), np.float32)
    if T >= 64:
        ncp = (T - 32) // 16 + 1
        ovp[:ncp, :T // 64 - 1] = ov(ncp, T // 64)[:, 1:]
    c["c_ov_p"] = ovp.reshape(2, 128, 63).transpose(1, 0, 2).copy()
    L = PAST + TS
    ncs = (L - 32) // 16 + 1
    nss = -(-L // 64)
    ovs = np.zeros((128, 63), np.float32)
    ovs[:ncs, :nss - 1] = ov(ncs, nss)[:, 1:]
    c["c_ov_s"] = ovs
    return c


def _nsa_setup_tables(self, W):
    k = self.k
    import concourse.bass as bass
    self.tz1 = k.dram("tz1", [16, 128, RS1], BF16)
    self.tz16 = k.dram("tz16", [16, 128, RS16], BF16)
    self.tzw = k.dram("tzw", [16, 128, RSW], BF16)
    with k.scope():
        relb = k.sb("relb", [33, 16])
        k.memset("pool", relb[:], NEGBIG)
        k.dma("sp", relb[0:32, :], W["rel_bias"][:, :])
        RB = k.sb("RB", [33, 16, 128])
        k.cp("dve", RB[:], V(relb.t[:, :].unsqueeze(2).broadcast_to([33, 16, 128]), relb.res))
        ohA = k.sb("ohA", [33, LA])
        ohW = k.sb("ohW", [33, LW])
        k.dma("sp", ohA[:], self.din["c_ohA"][:, :])
        k.dma("sp", ohW[:], self.din["c_ohW"][:, :])
        R = [k.sb("Rrow%d" % i, [128, LA], BF16) for i in range(2)]
        for h in range(16):
            Rr = R[h % 2]
            for ci in range(LA // 512):
                b = self.bank()
                k.mm(b[:, :], V(RB.t[:, h, :], RB.res), V(ohA.t[:, ci * 512:(ci + 1) * 512], ohA.res))
                k.cp("act" if ci % 2 == 0 else "dve", V(Rr.t[:, ci * 512:(ci + 1) * 512], Rr.res), b[:, :])
            d1 = bass.AP(self.tz1.t.tensor, h * 128 * RS1, [[RS1 + 1, 128], [1, LA]])
            k.dma("sp", V(d1, self.tz1.res), Rr[:, :], semres=Rr.res)
            d16 = bass.AP(self.tz16.t.tensor, h * 128 * RS16, [[RS16 + 16, 128], [1, LA]])
            k.dma("sp", V(d16, self.tz16.res), Rr[:, :], semres=Rr.res)
        Rw = [k.sb("Rw%d" % i, [128, LW], BF16) for i in range(2)]
        for h in range(16):
            Rr = Rw[h % 2]
            for ci in range(LW // 512):
                b = self.bank()
                k.mm(b[:, :], V(RB.t[:, h, :], RB.res), V(ohW.t[:, ci * 512:(ci + 1) * 512], ohW.res))
                k.cp("act" if ci % 2 == 0 else "dve", V(Rr.t[:, ci * 512:(ci + 1) * 512], Rr.res), b[:, :])
            dw = bass.AP(self.tzw.t.tensor, h * 128 * RSW, [[RSW + 1, 128], [1, LW]])
            k.dma("sp", V(dw, self.tzw.res), Rr[:, :], semres=Rr.res)


def _bias_tile(self, dst, tz, rs, x0, nk, tqn):
    import concourse.bass as bass
    src = bass.AP(tz.t.tensor, x0, [[rs, nk], [128 * rs, 16], [1, tqn]])
    self.k.dma("sp", dst, V(src, tz.res))


MK.nsa_setup_tables = _nsa_setup_tables
MK.bias_tile = _bias_tile


def _nsa_phase1(self, jn, li, xT_p, xT_s, W, scr, O):
    k = self.k
    T, NS = self.T, self.NS
    Win = k.sb("Win", [128, KC, NIN], BF16)
    with k.scope():
        stage = [k.sb("stg%d" % i, [128, 2048], F32) for i in range(2)]
        wsrc = W["nsa_w_in"].t[jn].rearrange("(c p) n -> p c n", p=128)
        i = 0
        for c in range(KC):
            for half in range(2):
                st = stage[i % 2]
                cs = slice(half * 1816, (half + 1) * 1816)
                k.dma("sp", V(st.t[:, 0:1816], st.res), V(wsrc[:, c, cs], W["nsa_w_in"].res))
                k.cp("pool" if i % 2 == 0 else "act", V(Win.t[:, c, cs], Win.res), V(st.t[:, 0:1816], st.res))
                i += 1
    nw = k.sb("nw", [128, KC])
    k.dma("sp", nw[:], V(W["nw_fm"].t[li], W["nw_fm"].res))
    qwb = k.sb("qwb", [128, D])
    self.bcast_load(qwb, W["qn_t"].t[jn], W["qn_t"].res)
    k.ts("pool", qwb[:], qwb[:], 0.125, op0=ALU.mult)
    knb = k.sb("knb", [128, 2, 256])
    for i in range(2):
        k.dma("sp", knb[:, i, :], V(W["kn_t"].t[jn, i].partition_broadcast(128), W["kn_t"].res))
    xT = k.sb("xT", [128, KC, 128])
    sq = k.sb("sq", [128, KC, 128])
    rstd = k.sb("rstd", [128, 128])
    hTb = k.sb("hTb", [128, KC, 128], BF16)
    qf = k.sb("qf", [128, D])
    qnb = k.sb("qnb", [128, D], BF16)
    tmpq = k.sb("tmpq", [128, D])
    kv = [k.sb("kv%d" % i, [128, 512]) for i in range(3)]
    szt = k.sb("szt", [128, D])
    gt = k.sb("gt", [128, 48])
    ss = k.sb("ssq", [128, 16])

    def tile(xT_d, col0, ntok, pref, row0):
        xTv = V(xT.t[:, :, 0:ntok], xT.res)
        k.dma("sp", xTv, V(xT_d.t[:, col0:col0 + ntok].rearrange("(c p) t -> p c t", p=128), xT_d.res))
        sqv = V(sq.t[:, :, 0:ntok], sq.res)
        k.tt("pool", sqv, xTv, xTv, ALU.mult)
        b = self.bank()
        for c in range(KC):
            k.mm(b[:, 0:ntok], self.ones[:], V(sq.t[:, c, 0:ntok], sq.res), start=(c == 0), stop=(c == KC - 1))
        rs = V(rstd.t[:, 0:ntok], rstd.res)
        k.ts("dve", rs, b[:, 0:ntok], 1.0 / D, RMS_EPS, op0=ALU.mult, op1=ALU.add)
        k.act(rs, rs, AF.Sqrt)
        k.op("dve", lambda g: g.reciprocal(rs.ap, rs.ap), [rs], [rs])
        for c in range(KC):
            k.stt("dve", V(hTb.t[:, c, 0:ntok], hTb.res), V(xT.t[:, c, 0:ntok], xT.res), nw[:, c:c + 1], rs, ALU.mult, ALU.mult)
        for blk in range(8):
            c0 = blk * 512
            cw = min(512, NIN - c0)
            b = self.bank()
            for c in range(KC):
                k.mm(b[0:ntok, 0:cw], V(hTb.t[:, c, 0:ntok], hTb.res), V(Win.t[:, c, c0:c0 + cw], Win.res), start=(c == 0), stop=(c == KC - 1))
            if blk < 2:
                k.cp("act", V(qf.t[0:ntok, c0:c0 + 512], qf.res), b[0:ntok, :])
            elif blk < 5:
                k.cp("act", V(kv[blk - 2].t[0:ntok, :], kv[blk - 2].res), b[0:ntok, :])
            elif blk < 7:
                k.act(V(szt.t[0:ntok, (blk - 5) * 512:(blk - 4) * 512], szt.res), b[0:ntok, :], AF.Silu)
            else:
                k.act(V(gt.t[0:ntok, :], gt.res), b[0:ntok, 0:48], AF.Sigmoid)

        def headnorm(src_v, nh, dst_v, wb_v, eng2):
            n = ntok
            t3 = V(tmpq.t[0:n, 0:nh * 64], tmpq.res)
            k.tt("pool", t3, src_v, src_v, ALU.mult)
            ssv = V(ss.t[0:n, 0:nh], ss.res)
            k.red("dve", ssv, V(tmpq.t[0:n, 0:nh * 64].rearrange("p (h c) -> p h c", c=64), tmpq.res))
            k.ts("dve", ssv, ssv, 1.0 / 64, RMS_EPS, op0=ALU.mult, op1=ALU.add)
            k.act(ssv, ssv, AF.Sqrt)
            k.op("dve", lambda g: g.reciprocal(ssv.ap, ssv.ap), [ssv], [ssv])
            s3 = V(src_v.ap.rearrange("p (h c) -> p h c", c=64), src_v.res)
            k.tt("dve", s3, s3, V(ss.t[0:n, 0:nh].unsqueeze(2).broadcast_to([n, nh, 64]), ss.res), ALU.mult)
            k.tt(eng2, dst_v, src_v, wb_v, ALU.mult)
        headnorm(V(qf.t[0:ntok, :], qf.res), 16, V(qnb.t[0:ntok, :], qnb.res), V(qwb.t[0:ntok, :], qwb.res), "pool")
        headnorm(V(kv[1].t[0:ntok, 0:256], kv[1].res), 4, V(kv[1].t[0:ntok, 0:256], kv[1].res), V(knb.t[0:ntok, 0, :], knb.res), "pool")
        headnorm(V(kv[2].t[0:ntok, 0:256], kv[2].res), 4, V(kv[2].t[0:ntok, 0:256], kv[2].res), V(knb.t[0:ntok, 1, :], knb.res), "pool")
        rs_ = slice(row0, row0 + ntok)
        for nm, src in (("kc", V(kv[0].t[0:ntok, 0:256], kv[0].res)), ("vc", V(kv[0].t[0:ntok, 256:512], kv[0].res)),
                        ("ks", V(kv[1].t[0:ntok, 0:256], kv[1].res)), ("vs", V(kv[1].t[0:ntok, 256:512], kv[1].res)),
                        ("kw", V(kv[2].t[0:ntok, 0:256], kv[2].res)), ("vw", V(kv[2].t[0:ntok, 256:512], kv[2].res)),
                        ("qn", V(qnb.t[0:ntok, :], qnb.res)), ("sz", V(szt.t[0:ntok, :], szt.res)), ("gt", V(gt.t[0:ntok, :], gt.res))):
            d = scr[pref + nm]
            k.dma("sp", V(d.t[rs_, :], d.res), src, semres=src.res[0])

    for ti in range(T // 128):
        tile(xT_p, ti * 128, 128, "np_", ti * 128)
    import os
    if NS > 0 and "s" not in os.environ.get("SKIP", ""):
        tile(xT_s, 0, NS, "ns_", 0)


MK.nsa_phase1 = _nsa_phase1


def _nsa_cmp_weights(self, jn, W):
    k = self.k
    cw = {}
    with k.scope():
        stg = k.sb("cstg", [64, 2048])
        pstg = k.sb("pstg", [32, 64])
        pstb = k.sb("pstb", [32, 64], BF16)
        for kvi in range(2):
            W1 = self._cmpW1[kvi]
            src = W["nsa_cmp_w1"].t[jn, kvi].rearrange("(j d) e -> d j e", d=64)
            k.dma("sp", V(stg.t[:, :].rearrange("p (j e) -> p j e", e=64), stg.res), V(src, W["nsa_cmp_w1"].res))
            k.cp("pool", W1[:], V(stg.t[:, :].rearrange("p (j e) -> p j e", e=64), stg.res))
            W2 = self._cmpW2[kvi]
            k.dma("sp", V(stg.t[:, 0:64], stg.res), V(W["nsa_cmp_w2"].t[jn, kvi], W["nsa_cmp_w2"].res))
            k.cp("pool", W2[:], V(stg.t[:, 0:64], stg.res))
            k.dma("sp", pstg[:], V(W["nsa_cmp_pos"].t[jn, kvi], W["nsa_cmp_pos"].res))
            k.cp("pool", pstb[:], pstg[:])
            bb = self.bbank()
            k.tr(bb[0:64, 0:32], pstb[:], V(self.identb.t[0:32, 0:32], self.identb.res))
            posT = self._cmpPos[kvi]
            k.cp("act", posT[:], bb[0:64, 0:32])
            b = self.bank()
            for j in range(32):
                k.mm(b[0:64, 0:1], V(W1.t[:, j, :], W1.res), V(posT.t[:, j:j + 1], posT.res), start=(j == 0), stop=(j == 31))
            k.cp("act", self._cmpBias[kvi][:], b[0:64, 0:1])
        k.dma("sp", self._kn2[:], V(W["kn2_col"].t[jn], W["kn2_col"].res))


def _nsa_alloc_cmp(self):
    k = self.k
    self._cmpW1 = [k.sb("cW1_%d" % i, [64, 32, 64], BF16) for i in range(2)]
    self._cmpW2 = [k.sb("cW2_%d" % i, [64, 64], BF16) for i in range(2)]
    self._cmpPos = [k.sb("cPos_%d" % i, [64, 32], BF16) for i in range(2)]
    self._cmpBias = [k.sb("cBias_%d" % i, [64, 1]) for i in range(2)]
    self._kn2 = k.sb("kn2", [64, 1])
    self._chT = k.sb("chT", [64, 256], BF16)
    self._csq = k.sb("csq", [64, 256])
    self._crs = k.sb("crs", [64, 256])


def _nsa_compress(self, kcT, vcT, NC, kcmpT, vcmpX):
    k = self.k
    hT, sqt, rs = self._chT, self._csq, self._crs
    for kvi, src in enumerate((kcT, vcT)):
        W1, W2, bias = self._cmpW1[kvi], self._cmpW2[kvi], self._cmpBias[kvi]
        for kvh in range(4):
            b = self.bank()
            for j in range(32):
                k.mm(b[0:64, 0:NC], V(W1.t[:, j, :], W1.res), V(src.t[:, kvh, j:j + 16 * (NC - 1) + 1:16], src.res), start=(j == 0), stop=(j == 31))
            k.act(V(hT.t[:, 0:NC], hT.res), b[0:64, 0:NC], AF.Silu, bias=bias[:, 0:1])
            if kvi == 0:
                b2 = self.bank()
                k.mm(b2[0:64, 0:NC], W2[:], V(hT.t[:, 0:NC], hT.res))
                k.act(V(sqt.t[:, 0:NC], sqt.res), b2[0:64, 0:NC], AF.Square)
                b3 = self.bank()
                k.mm(b3[0:64, 0:NC], V(self.ones.t[0:64, 0:64], self.ones.res), V(sqt.t[:, 0:NC], sqt.res))
                rsv = V(rs.t[:, 0:NC], rs.res)
                k.ts("dve", rsv, b3[0:64, 0:NC], 1.0 / 64, RMS_EPS, op0=ALU.mult, op1=ALU.add)
                k.act(rsv, rsv, AF.Sqrt)
                k.op("dve", lambda g: g.reciprocal(rsv.ap, rsv.ap), [rsv], [rsv])
                k.stt("dve", V(kcmpT.t[:, kvh, 0:NC], kcmpT.res), b2[0:64, 0:NC], self._kn2[:, 0:1], rsv, ALU.mult, ALU.mult)
            else:
                for ci in range((NC + 127) // 128):
                    nk = min(128, NC - ci * 128)
                    b2 = self.bank()
                    k.mm(b2[0:nk, 0:64], V(hT.t[:, ci * 128:ci * 128 + nk], hT.res), W2[:])
                    k.cp("act", V(vcmpX.t[0:nk, ci, kvh, 0:64], vcmpX.res), b2[0:nk, 0:64])


MK.nsa_cmp_weights = _nsa_cmp_weights
MK.nsa_alloc_cmp = _nsa_alloc_cmp
MK.nsa_compress = _nsa_compress


def _nsa_alloc_attn(self, tqn):
    k = self.k
    A = {}
    A["tqn"] = tqn
    NQ = 4 * tqn
    A["Eb"] = [k.sb("Eb%d" % i, [128, NQ], BF16) for i in range(3)]
    A["Ei"] = 0
    A["oacc"] = k.sb("oacc", [128, 16, 64])
    A["accS"] = [k.sb("accS%d" % i, [128, 4, 512]) for i in range(2)]
    A["otmp"] = k.sb("otmp", [128, 4, 64])
    A["den"] = k.sb("den", [128, 4])
    A["coef"] = k.sb("coef", [128, 4])
    A["imp"] = k.sb("imp", [128, 4, 64])
    A["sc"] = k.sb("sc", [128, 64])
    A["scw"] = k.sb("scw", [128, 64])
    A["m8"] = k.sb("m8", [128, 8])
    A["m8b"] = k.sb("m8b", [128, 8])
    A["m30f"] = k.sb("m30f", [128, 64])
    A["m30b"] = k.sb("m30b", [128, 4, 128], BF16)
    k.memset("pool", A["m30b"][:], 0.0)
    A["M30"] = k.sb("M30", [64, 4, 4, tqn], BF16)
    k.memset("pool", A["imp"][:], 0.0)
    return A


def _nsa_attend(self, A, qT, gates, fm, cmp_tiles, sel_tiles, win_tiles):
    k = self.k
    tqn = A["tqn"]
    NQ = 4 * tqn
    acc = self.banks[0:4]
    sbanks = self.banks[4:6]
    oacc = A["oacc"]
    g3 = gates.ap.rearrange("p (h r) -> p h r", r=3)

    def branch(br, tiles, Wd, use_m30):
        units = [(kvh, ti, tdesc, len(tiles[kvh])) for kvh in range(4) for ti, tdesc in enumerate(tiles[kvh])]

        def stage1(u):
            kvh, ti, tdesc, ntl = u
            nk = tdesc["nk"]
            sb = sbanks[self._sbi % 2]
            self._sbi += 1
            if use_m30:
                k.mm(sb[0:nk, 0:NQ], tdesc["kT"], V(qT.ap[:, kvh * NQ:(kvh + 1) * NQ], qT.res), start=True, stop=False)
            else:
                k.mm(sb[0:nk, 0:NQ], tdesc["kT"], V(qT.ap[0:64, kvh * NQ:(kvh + 1) * NQ], qT.res), start=True, stop=False)
            k.mm(sb[0:nk, 0:NQ], V(self.identb.t[0:nk, 0:nk], self.identb.res), tdesc["bias"], start=False, stop=True)
            Eb = A["Eb"][A["Ei"] % 3]
            A["Ei"] += 1
            k.act(V(Eb.t[0:nk, :], Eb.res), sb[0:nk, 0:NQ], AF.Exp)
            return Eb

        def stage2(u, Eb):
            kvh, ti, tdesc, ntl = u
            nk = tdesc["nk"]
            for g in range(4):
                first = (ti == 0 and g == 0)
                k.mm(acc[kvh][0:tqn, g * 128:g * 128 + Wd], V(Eb.t[0:nk, g * tqn:(g + 1) * tqn], Eb.res), tdesc["vX"],
                     start=first, stop=(ti == ntl - 1), sgc=True)
        prev = None
        for u in units:
            Eb = stage1(u)
            if prev is not None:
                stage2(*prev)
            prev = (u, Eb)
        if prev is not None:
            stage2(*prev)
        aS = A["accS"][br % 2]
        for kvh in range(4):
            if len(tiles[kvh]) > 0:
                k.cp("act", V(aS.t[0:tqn, kvh, :], aS.res), acc[kvh][0:tqn, :])

    def fin(br, tiles):
        aS = A["accS"][br % 2]
        for kvh in range(4):
            av = aS.t[0:tqn, kvh, :].rearrange("p (g n) -> p g n", n=128)
            den = V(A["den"].t[0:tqn, :], A["den"].res)
            coef = V(A["coef"].t[0:tqn, :], A["coef"].res)
            if len(tiles[kvh]) == 0:
                if br == 0:
                    k.memset("pool", V(oacc.t[0:tqn, 4 * kvh:4 * kvh + 4, :], oacc.res), 0.0)
                continue
            k.ts("dve", den, V(av[:, :, 64], aS.res), 1e-30, op0=ALU.max)
            k.op("dve", lambda g_: g_.reciprocal(den.ap, den.ap), [den], [den])
            k.tt("dve", coef, den, V(g3[:, 4 * kvh:4 * kvh + 4, br], gates.res), ALU.mult)
            cb = V(A["coef"].t[0:tqn, :].unsqueeze(2).broadcast_to([tqn, 4, 64]), A["coef"].res)
            ov_ = V(oacc.t[0:tqn, 4 * kvh:4 * kvh + 4, :], oacc.res)
            if br == 0:
                k.tt("dve", ov_, V(av[:, :, 0:64], aS.res), cb, ALU.mult)
            else:
                ot = V(A["otmp"].t[0:tqn], A["otmp"].res)
                k.tt("dve", ot, V(av[:, :, 0:64], aS.res), cb, ALU.mult)
                k.tt("pool", ov_, ov_, ot, ALU.add)
            if br == 0:
                impv = V(A["imp"].t[0:tqn, kvh, 1:64], A["imp"].res)
                for g in range(4):
                    if g == 0:
                        k.ts("dve", impv, V(av[:, g, 65:128], aS.res), V(A["den"].t[0:tqn, g:g + 1], A["den"].res), op0=ALU.mult)
                    else:
                        k.stt("dve", impv, V(av[:, g, 65:128], aS.res), V(A["den"].t[0:tqn, g:g + 1], A["den"].res), impv, ALU.mult, ALU.add)

    self._sbi = getattr(self, "_sbi", 0)
    branch(0, cmp_tiles, 128, False)
    fin(0, cmp_tiles)
    for kvh in range(4):
        sc = V(A["sc"].t[0:tqn, :], A["sc"].res)
        scw = V(A["scw"].t[0:tqn, :], A["scw"].res)
        m8 = V(A["m8"].t[0:tqn, :], A["m8"].res)
        m8b = V(A["m8b"].t[0:tqn, :], A["m8b"].res)
        k.tt("dve", sc, V(A["imp"].t[0:tqn, kvh, :], A["imp"].res), fm, ALU.add)
        k.op("dve", lambda g_: g_.max(m8.ap, sc.ap), [sc], [m8])
        k.op("dve", lambda g_: g_.match_replace(scw.ap, m8.ap, sc.ap, -1e9), [m8, sc], [scw])
        k.op("dve", lambda g_: g_.max(m8b.ap, scw.ap), [scw], [m8b])
        m30f = V(A["m30f"].t[0:tqn, :], A["m30f"].res)
        k.ts("dve", m30f, sc, V(A["m8b"].t[0:tqn, 7:8], A["m8b"].res), op0=ALU.is_ge)
        k.ts("dve", V(A["m30b"].t[0:tqn, kvh, 64:128], A["m30b"].res), m30f, -1.0, -NEGBIG, op0=ALU.add, op1=ALU.mult)
    branch(2, win_tiles, 65, False)
    bb = self.bbank()
    for kvh in range(4):
        k.tr(bb[:, kvh * 128:kvh * 128 + tqn], V(A["m30b"].t[0:tqn, kvh, :], A["m30b"].res), V(self.identb.t[0:tqn, 0:tqn], self.identb.res))
    k.cp("dve", V(qT.ap[64:128, :].rearrange("p (v g t) -> p v g t", g=4, t=tqn), qT.res),
         V(bb.t[64:128, 0:512].rearrange("p (v t) -> p v t", t=128)[:, :, 0:tqn].unsqueeze(2).broadcast_to([64, 4, 4, tqn]), bb.res))
    fin(2, win_tiles)
    branch(1, sel_tiles, 65, True)
    fin(1, sel_tiles)


MK.nsa_alloc_attn = _nsa_alloc_attn
MK.nsa_attend = _nsa_attend


def _outproj(self, ob, n, xT_d, col0, Wo, xT, oT):
    k = self.k
    bb = self.bbank()
    for c in range(KC):
        k.tr(bb[:, c * 128:c * 128 + n], V(ob.t[0:n, c * 128:(c + 1) * 128], ob.res), V(self.identb.t[0:n, 0:n], self.identb.res))
    k.cp("act", V(oT.t[:, :, 0:n], oT.res), V(bb.t[:, :].rearrange("p (c t) -> p c t", t=128)[:, :, 0:n], bb.res))
    xTv = V(xT.t[:, :, 0:n], xT.res)
    k.dma("sp", xTv, V(xT_d.t[:, col0:col0 + n].rearrange("(c p) t -> p c t", p=128), xT_d.res))
    for hb in range(2):
        b = self.bank()
        for dq in range(4):
            dc = hb * 4 + dq
            for c in range(KC):
                k.mm(b[:, dq * 128:dq * 128 + n], V(Wo.t[:, c, dc * 128:(dc + 1) * 128], Wo.res), V(oT.t[:, c, 0:n], oT.res), start=(c == 0), stop=(c == KC - 1))
        k.tt("dve", V(xT.t[:, hb * 4:hb * 4 + 4, 0:n], xT.res), V(xT.t[:, hb * 4:hb * 4 + 4, 0:n], xT.res),
             V(b.t[:, :].rearrange("p (c t) -> p c t", t=128)[:, :, 0:n], b.res), ALU.add)
    k.dma("sp", V(xT_d.t[:, col0:col0 + n].rearrange("(c p) t -> p c t", p=128), xT_d.res), xTv, semres=xT.res)


MK.outproj = _outproj


def _nsa_phase2_prompt(self, jn, li, xT_p, W, scr, O):
    k = self.k
    T = self.T
    NT = T // 128
    NC = (T - 32) // 16 + 1
    Wo = k.sb("Wo", [128, KC, D], BF16)
    with k.scope():
        stage = [k.sb("stg%d" % i, [128, 2048], F32) for i in range(2)]
        self.load_w_bf16(Wo, W["nsa_w_o"].t[jn], W["nsa_w_o"].res, KC, D, stage)
    kcmpT = k.sb("kcmpT", [64, 4, 256], BF16)
    vcmpX = k.sb("vcmpX", [128, 2, 4, 128], BF16)
    k.memset("pool", kcmpT[:], 0.0)
    k.memset("pool", vcmpX[:], 0.0)
    ovp = k.sb("ovp", [128, 2, 63])
    k.dma("sp", ovp[:], self.din["c_ov_p"][:])
    k.memset("pool", V(vcmpX.t[:, :, :, 64:65], vcmpX.res), 1.0)
    for kvh in range(4):
        k.cp("pool", V(vcmpX.t[:, :, kvh, 65:128], vcmpX.res), ovp[:])
    ldf = k.sb("ldf", [128, 512])
    ldb = k.sb("ldb", [128, 512], BF16)

    def build_T(dst, names, ntiles):
        for ti in range(ntiles):
            for i, nm in enumerate(names):
                d = scr[nm]
                k.dma("sp", V(ldf.t[:, i * 256:(i + 1) * 256], ldf.res), V(d.t[ti * 128:(ti + 1) * 128, :], d.res))
            w = 256 * len(names)
            k.cp("pool", V(ldb.t[:, 0:w], ldb.res), V(ldf.t[:, 0:w], ldf.res))
            for i, nm in enumerate(names):
                bb = self.bbank()
                for kvh in range(4):
                    k.tr(bb[0:64, kvh * 128:(kvh + 1) * 128], V(ldb.t[:, i * 256 + kvh * 64:i * 256 + (kvh + 1) * 64], ldb.res), self.identb[:])
                k.cp("act", V(dst[i].t[0:64, :, ti * 128:(ti + 1) * 128], dst[i].res), V(bb.t[0:64, 0:512].rearrange("p (v t) -> p v t", t=128), bb.res))
    with k.scope():
        self.nsa_alloc_cmp()
        self.nsa_cmp_weights(jn, W)
        kcT = k.sb("kcT", [64, 4, T], BF16)
        vcT = k.sb("vcT", [64, 4, T], BF16)
        build_T([kcT, vcT], ["np_kc", "np_vc"], NT)
        self.nsa_compress(kcT, vcT, NC, kcmpT, vcmpX)
    ksT = k.sb("ksT", [128, 4, T], BF16)
    build_T([ksT], ["np_ks"], NT)
    for kvh in range(4):
        k.dma("sp", V(ksT.t[64:128, kvh, :].rearrange("p (q t) -> p q t", t=128), ksT.res), V(self.din["c_expand"].t[:, 0:NT, :], self.din["c_expand"].res))
    vsX = k.sb("vsX", [128, NT, 4, 65], BF16)
    k.memset("pool", V(vsX.t[:, :, :, 64:65], vsX.res), 1.0)
    for ti in range(NT):
        d = scr["np_vs"]
        k.dma("sp", V(ldf.t[:, 0:256], ldf.res), V(d.t[ti * 128:(ti + 1) * 128, :], d.res))
        k.cp("pool", V(vsX.t[:, ti, :, 0:64], vsX.res), V(ldf.t[:, 0:256].rearrange("p (v c) -> p v c", c=64), ldf.res))
    kwT = k.sb("kwT", [64, 4, 5 * 128], BF16)
    vwX = k.sb("vwX", [128, 5, 4, 65], BF16)
    k.memset("pool", V(vwX.t[:, :, :, 64:65], vwX.res), 1.0)
    selB = k.sb("selB", [128, 10, 16 * 128], BF16)
    winB = k.sb("winB", [128, 5, 16 * 128], BF16)
    for dlt in range(10):
        self.bias_tile(V(selB.t[:, dlt, :].rearrange("p (h t) -> p h t", t=128), selB.res), self.tz1, RS1, OA + 128 * dlt, 128, 128)
    for dlt in range(5):
        self.bias_tile(V(winB.t[:, dlt, :].rearrange("p (h t) -> p h t", t=128), winB.res), self.tzw, RSW, OW + 128 * dlt, 128, 128)
    cmpB = [k.sb("cmpB%d" % i, [128, 16 * 128], BF16) for i in range(2)]
    fm = k.sb("fm", [128, 64])
    A = self.nsa_alloc_attn(128)
    qn = k.sb("qn", [128, D], BF16)
    qT = k.sb("qT", [128, 16 * 128], BF16)
    gt = k.sb("gt", [128, 48])
    szt = k.sb("szt", [128, D])
    ob = k.sb("ob", [128, D], BF16)
    oT = k.sb("oT", [128, KC, 128], BF16)
    xT = k.sb("xT", [128, KC, 128])
    for qt in range(NT):
        rs_ = slice(qt * 128, (qt + 1) * 128)
        k.dma("sp", qn[:], V(scr["np_qn"].t[rs_, :], scr["np_qn"].res))
        k.dma("sp", gt[:], V(scr["np_gt"].t[rs_, :], scr["np_gt"].res))
        k.dma("sp", szt[:], V(scr["np_sz"].t[rs_, :], scr["np_sz"].res))
        k.dma("sp", fm[:], V(self.din["c_fm_p"].t[qt], self.din["c_fm_p"].res))
        for half in range(2):
            bb = self.bbank()
            for hh in range(8):
                h = half * 8 + hh
                k.tr(bb[0:64, hh * 128:(hh + 1) * 128], V(qn.t[:, h * 64:(h + 1) * 64], qn.res), self.identb[:])
            k.cp("act", V(qT.t[0:64, half * 1024:(half + 1) * 1024], qT.res), bb[0:64, :])
        slot = qt % 5
        k.dma("sp", V(ldf.t[:, 0:256], ldf.res), V(scr["np_kw"].t[rs_, :], scr["np_kw"].res))
        k.dma("sp", V(ldf.t[:, 256:512], ldf.res), V(scr["np_vw"].t[rs_, :], scr["np_vw"].res))
        k.cp("pool", V(ldb.t[:, 0:256], ldb.res), V(ldf.t[:, 0:256], ldf.res))
        k.cp("pool", V(vwX.t[:, slot, :, 0:64], vwX.res), V(ldf.t[:, 256:512].rearrange("p (v c) -> p v c", c=64), ldf.res))
        bb = self.bbank()
        for kvh in range(4):
            k.tr(bb[0:64, kvh * 128:(kvh + 1) * 128], V(ldb.t[:, kvh * 64:(kvh + 1) * 64], ldb.res), self.identb[:])
        k.cp("act", V(kwT.t[:, :, slot * 128:(slot + 1) * 128], kwT.res), V(bb.t[0:64, 0:512].rearrange("p (v t) -> p v t", t=128), bb.res))
        cmp_tiles = [[] for _ in range(4)]
        sel_tiles = [[] for _ in range(4)]
        win_tiles = [[] for _ in range(4)]
        for c in range((NC + 127) // 128):
            if 8 * qt + 6 < 128 * c:
                continue
            nk = min(128, NC - 128 * c)
            cb = cmpB[c]
            self.bias_tile(V(cb.t[0:nk, :].rearrange("p (h t) -> p h t", t=128), cb.res), self.tz16, RS16, OA + 128 * qt - 31 - 2048 * c, nk, 128)
            for kvh in range(4):
                cmp_tiles[kvh].append(dict(nk=nk, kT=V(kcmpT.t[:, kvh, c * 128:c * 128 + nk], kcmpT.res), vX=V(vcmpX.t[0:nk, c, kvh, :], vcmpX.res),
                                           bias=V(cb.t[0:nk, kvh * 512:(kvh + 1) * 512], cb.res)))
        for kt in range(qt + 1):
            dlt = min(qt - kt, 9)
            for kvh in range(4):
                sel_tiles[kvh].append(dict(nk=128, kT=V(ksT.t[:, kvh, kt * 128:(kt + 1) * 128], ksT.res), vX=V(vsX.t[:, kt, kvh, :], vsX.res),
                                           bias=V(selB.t[:, dlt, kvh * 512:(kvh + 1) * 512], selB.res)))
        for kt in range(max(0, qt - 4), qt + 1):
            sl = kt % 5
            for kvh in range(4):
                win_tiles[kvh].append(dict(nk=128, kT=V(kwT.t[:, kvh, sl * 128:(sl + 1) * 128], kwT.res), vX=V(vwX.t[:, sl, kvh, :], vwX.res),
                                           bias=V(winB.t[:, qt - kt, kvh * 512:(kvh + 1) * 512], winB.res)))
        self.nsa_attend(A, qT[:, :], gt[:, :], fm[:, :], cmp_tiles, sel_tiles, win_tiles)
        k.tt("dve", ob[:], V(A["oacc"].t[:, :, :].rearrange("p h c -> p (h c)"), A["oacc"].res), szt[:], ALU.mult)
        self.outproj(ob, 128, xT_p, qt * 128, Wo, xT, oT)
    wb = min(512, T)
    k.dma("sp", V(O["o_p_win_k"].t[jn], O["o_p_win_k"].res), V(scr["np_kw"].t[T - wb:T, :], scr["np_kw"].res))
    k.dma("sp", V(O["o_p_win_v"].t[jn], O["o_p_win_v"].res), V(scr["np_vw"].t[T - wb:T, :], scr["np_vw"].res))


MK.nsa_phase2_prompt = _nsa_phase2_prompt


def _nsa_phase2_sample(self, jn, li, xT_s, W, scr, O):
    k = self.k
    nc = k.nc
    import concourse.bass as bass
    NBS, TS, NS = self.NBS, self.TS, self.NS
    PAST = 2048
    NPG = PAST // 128
    L = PAST + TS
    NC = (L - 32) // 16 + 1
    NKT = NPG + 1
    Wo = k.sb("Wo", [128, KC, D], BF16)
    with k.scope():
        stage = [k.sb("stg%d" % i, [128, 2048], F32) for i in range(2)]
        self.load_w_bf16(Wo, W["nsa_w_o"].t[jn], W["nsa_w_o"].res, KC, D, stage)
    self.nsa_alloc_cmp()
    self.nsa_cmp_weights(jn, W)
    kcmpT = k.sb("kcmpT", [64, 4, 256], BF16)
    vcmpX = k.sb("vcmpX", [128, 2, 4, 128], BF16)
    k.memset("pool", kcmpT[:], 0.0)
    k.memset("pool", vcmpX[:], 0.0)
    ovs = k.sb("ovs", [128, 63])
    k.dma("sp", ovs[:], self.din["c_ov_s"][:, :])
    k.memset("pool", V(vcmpX.t[:, :, :, 64:65], vcmpX.res), 1.0)
    for kvh in range(4):
        k.cp("pool", V(vcmpX.t[:, 0, kvh, 65:128], vcmpX.res), ovs[:])
    pts = k.sb("pts", [128, NBS], I32)
    s8s = k.sb("s8s", [128, 1], I32)
    idx = k.sb("idx", [128, NBS], I32)
    idf = k.sb("idf", [128, NBS])
    s8f = k.sb("s8f", [128, 1])
    k.dma("sp", pts[:], W["pt8T"][:, :])
    k.dma("sp", s8s[:], self.din["c_s8"][:, :])
    k.cp("dve", idf[:], pts[:])
    k.cp("dve", s8f[:], s8s[:])
    k.ts("dve", idf[:], idf[:], 8.0, s8f[:, 0:1], op0=ALU.mult, op1=ALU.add)
    k.cp("dve", idx[:], idf[:])
    pg2 = [k.sb("pgbuf%d" % i, [128, NPG + 1, 256]) for i in range(2)]
    pg = [pg2[0], pg2[1], pg2[0], pg2[1]]
    ldbs = [k.sb("ldb%d" % i, [128, 256], BF16) for i in range(4)]
    ldbi = [0]

    def next_ldb():
        ldbi[0] += 1
        return ldbs[ldbi[0] % 4]
    kcT = k.sb("kcT", [64, 4, PAST], BF16)
    vcT = k.sb("vcT", [64, 4, PAST], BF16)
    ksT = k.sb("ksT", [128, 4, NKT * 128], BF16)
    vsX = k.sb("vsX", [128, NKT, 4, 65], BF16)
    k.memset("pool", V(vsX.t[:, :, :, 64:65], vsX.res), 1.0)
    wbuf = [k.sb("wbuf%d" % i, [128, 5, 256]) for i in range(2)]
    kwT = k.sb("kwT", [64, 4, 5 * 128], BF16)
    vwX = k.sb("vwX", [128, 5, 4, 65], BF16)
    k.memset("pool", V(vwX.t[:, :, :, 64:65], vwX.res), 1.0)
    selB = k.sb("selBs", [128, NKT, 16 * TS], BF16)
    winB = k.sb("winBs", [128, 5, 16 * TS], BF16)
    cmpB = k.sb("cmpBs", [128, 16 * TS], BF16)
    for kt in range(NKT):
        if kt < NPG:
            self.bias_tile(V(selB.t[:, kt, :].rearrange("p (h t) -> p h t", t=TS), selB.res), self.tz16, RS16, OA + PAST - kt, 128, TS)
        else:
            self.bias_tile(V(selB.t[0:TS, kt, :].rearrange("p (h t) -> p h t", t=TS), selB.res), self.tz1, RS1, OA, TS, TS)
    for wt in range(5):
        nk = 128 if wt < 4 else TS
        self.bias_tile(V(winB.t[0:nk, wt, :].rearrange("p (h t) -> p h t", t=TS), winB.res), self.tzw, RSW, OW + 512 - 128 * wt, nk, TS)
    self.bias_tile(V(cmpB.t[0:NC, :].rearrange("p (h t) -> p h t", t=TS), cmpB.res), self.tz16, RS16, OA + PAST - 31, NC, TS)
    for kvh in range(4):
        for kt in range(NKT):
            k.dma("sp", V(ksT.t[64:128, kvh, kt * 128:(kt + 1) * 128], ksT.res),
                  V(self.din["c_expand_s"].t[:, 0 if kt < NPG else 1, :], self.din["c_expand_s"].res))
    fm = k.sb("fms", [TS, 64])
    k.dma("sp", fm[:], self.din["c_fm_s"][:, :])
    A = self.nsa_alloc_attn(TS)
    qn = k.sb("qn", [TS, D], BF16)
    qT = k.sb("qT", [128, 16 * TS], BF16)
    gt = k.sb("gt", [TS, 48])
    szt = k.sb("szt", [TS, D])
    obs = k.sb("obs", [TS, D], BF16)
    caches = [W["cache_cmp_k"], W["cache_cmp_v"], W["cache_sel_k"], W["cache_sel_v"]]
    newn = ["ns_kc", "ns_vc", "ns_ks", "ns_vs"]
    NPOOL = caches[0].t.shape[1]
    for bl in range(NBS):
        rs_ = slice(bl * TS, (bl + 1) * TS)
        for ci, dst in ((0, kcT), (1, vcT), (2, ksT), (3, None)):
            src = caches[ci].t.rearrange("l n (s t) v c -> (l n s) (t v c)", s=8)
            k.idma(V(pg[ci].t[:, 0:NPG, :].rearrange("p t c -> p (t c)"), pg[ci].res), src, caches[ci].res, idx[:, bl:bl + 1],
                   element_offset=jn * NPOOL * 8 * 4096)
            if ci >= 2:
                d = scr[newn[ci]]
                k.dma("sp", V(pg[ci].t[0:TS, NPG, :], pg[ci].res), V(d.t[rs_, :], d.res))
            if ci < 3:
                for pi in range(NKT):
                    nr = 128 if pi < NPG else TS
                    if pi == NPG and ci < 2:
                        continue
                    ldb = next_ldb()
                    k.cp("pool" if pi % 2 == 0 else "dve", V(ldb.t[0:nr, :], ldb.res), V(pg[ci].t[0:nr, pi, :], pg[ci].res))
                    bb = self.bbank()
                    for kvh in range(4):
                        k.tr(bb[0:64, kvh * 128:kvh * 128 + nr], V(ldb.t[0:nr, kvh * 64:(kvh + 1) * 64], ldb.res), V(self.identb.t[0:nr, 0:nr], self.identb.res))
                    if ci < 2:
                        dv = V(dst.t[:, :, pi:PAST:16], dst.res)
                    else:
                        dv = V(dst.t[0:64, :, pi * 128:pi * 128 + nr], dst.res)
                    k.cp("act", dv, V(bb.t[0:64, 0:512].rearrange("p (v t) -> p v t", t=128)[:, :, 0:nr], bb.res))
            else:
                for pi in range(NKT):
                    nr = 128 if pi < NPG else TS
                    k.cp("pool", V(vsX.t[0:nr, pi, :, 0:64], vsX.res), V(pg[3].t[0:nr, pi, :].rearrange("p (v c) -> p v c", c=64), pg[3].res))
        self.nsa_compress(kcT, vcT, NC, kcmpT, vcmpX)
        for wi, (stn, nn) in enumerate((("state_win_k", "ns_kw"), ("state_win_v", "ns_vw"))):
            k.dma("sp", V(wbuf[wi].t[:, 0:4, :], wbuf[wi].res), V(W[stn].t[jn, bl].rearrange("(t p) c -> p t c", p=128), W[stn].res))
            k.dma("sp", V(wbuf[wi].t[0:TS, 4, :], wbuf[wi].res), V(scr[nn].t[rs_, :], scr[nn].res))
        for wt in range(5):
            nr = 128 if wt < 4 else TS
            ldb = next_ldb()
            k.cp("pool", V(ldb.t[0:nr, :], ldb.res), V(wbuf[0].t[0:nr, wt, :], wbuf[0].res))
            bb = self.bbank()
            for kvh in range(4):
                k.tr(bb[0:64, kvh * 128:kvh * 128 + nr], V(ldb.t[0:nr, kvh * 64:(kvh + 1) * 64], ldb.res), V(self.identb.t[0:nr, 0:nr], self.identb.res))
            k.cp("act", V(kwT.t[:, :, wt * 128:wt * 128 + nr], kwT.res), V(bb.t[0:64, 0:512].rearrange("p (v t) -> p v t", t=128)[:, :, 0:nr], bb.res))
            k.cp("pool", V(vwX.t[0:nr, wt, :, 0:64], vwX.res), V(wbuf[1].t[0:nr, wt, :].rearrange("p (v c) -> p v c", c=64), wbuf[1].res))
        for wi, on in enumerate(("o_s_win_k", "o_s_win_v")):
            od = O[on]
            for wt in range(4):
                lo = wt * 128 - TS
                if wt == 0:
                    k.dma("sp", V(od.t[jn, bl, 0:128 - TS, :], od.res), V(wbuf[wi].t[TS:128, 0, :], wbuf[wi].res), semres=wbuf[wi].res)
                else:
                    k.dma("sp", V(od.t[jn, bl, lo:lo + 128, :], od.res), V(wbuf[wi].t[:, wt, :], wbuf[wi].res), semres=wbuf[wi].res)
            k.dma("sp", V(od.t[jn, bl, 512 - TS:512, :], od.res), V(wbuf[wi].t[0:TS, 4, :], wbuf[wi].res), semres=wbuf[wi].res)
        k.dma("sp", qn[:], V(scr["ns_qn"].t[rs_, :], scr["ns_qn"].res))
        k.dma("sp", gt[:], V(scr["ns_gt"].t[rs_, :], scr["ns_gt"].res))
        k.dma("sp", szt[:], V(scr["ns_sz"].t[rs_, :], scr["ns_sz"].res))
        bb = self.bbank()
        for h in range(16):
            k.tr(bb[0:64, h * TS:(h + 1) * TS], V(qn.t[:, h * 64:(h + 1) * 64], qn.res), V(self.identb.t[0:TS, 0:TS], self.identb.res))
        k.cp("act", qT[0:64, :], bb[0:64, 0:16 * TS])
        cmp_tiles = [[dict(nk=NC, kT=V(kcmpT.t[:, kvh, 0:NC], kcmpT.res), vX=V(vcmpX.t[0:NC, 0, kvh, :], vcmpX.res),
                           bias=V(cmpB.t[0:NC, kvh * 4 * TS:(kvh + 1) * 4 * TS], cmpB.res))] for kvh in range(4)]
        sel_tiles = [[] for _ in range(4)]
        win_tiles = [[] for _ in range(4)]
        for kt in range(NKT):
            nk = 128 if kt < NPG else TS
            for kvh in range(4):
                sel_tiles[kvh].append(dict(nk=nk, kT=V(ksT.t[:, kvh, kt * 128:kt * 128 + nk], ksT.res), vX=V(vsX.t[0:nk, kt, kvh, :], vsX.res),
                                           bias=V(selB.t[0:nk, kt, kvh * 4 * TS:(kvh + 1) * 4 * TS], selB.res)))
        for wt in range(5):
            nk = 128 if wt < 4 else TS
            for kvh in range(4):
                win_tiles[kvh].append(dict(nk=nk, kT=V(kwT.t[:, kvh, wt * 128:wt * 128 + nk], kwT.res), vX=V(vwX.t[0:nk, wt, kvh, :], vwX.res),
                                           bias=V(winB.t[0:nk, wt, kvh * 4 * TS:(kvh + 1) * 4 * TS], winB.res)))
        self.nsa_attend(A, qT[:, :], gt[:, :], fm[:, :], cmp_tiles, sel_tiles, win_tiles)
        k.tt("dve", obs[:], V(A["oacc"].t[0:TS, :, :].rearrange("p h c -> p (h c)"), A["oacc"].res), szt[:], ALU.mult)
        k.dma("sp", V(scr["ns_ob"].t[rs_, :], scr["ns_ob"].res), obs[:], semres=obs.res)
    ob = k.sb("ob", [128, D], BF16)
    oT = k.sb("oT", [128, KC, 128], BF16)
    xT = k.sb("xT", [128, KC, 128])
    k.dma("sp", V(ob.t[0:NS, :], ob.res), V(scr["ns_ob"].t[:, :], scr["ns_ob"].res))
    self.outproj(ob, NS, xT_s, 0, Wo, xT, oT)


MK.nsa_phase2_sample = _nsa_phase2_sample


T_FULL = 4096
NBS_FULL = 16
NCORES = 8
RW_NAMES = ["rw_w_rkvz", "rw_w0", "rw_w1", "rw_w2", "rw_a0", "rw_a1", "rw_a2", "rw_v0", "rw_v1", "rw_v2", "rw_g1", "rw_g2",
            "rw_k_k", "rw_k_a", "rw_r_k", "rw_ln_w", "rw_ln_b", "rw_w_o"]
NSA_NAMES = ["nsa_w_in", "nsa_cmp_pos", "nsa_cmp_w1", "nsa_cmp_w2", "nsa_w_o", "rel_bias"]
CACHE_NAMES = ["cache_cmp_k", "cache_cmp_v", "cache_sel_k", "cache_sel_v"]


def build_full(T, NBS, shapes, n_layers=4):
    m = MK(T, NBS)
    k = m.k
    NS = m.NS
    W = {}
    for n in RW_NAMES + NSA_NAMES + CACHE_NAMES:
        W[n] = m.inp(n, list(shapes[n]))
    W["mu_fm"] = m.inp("mu_fm", [2, 128, 6, 8])
    W["nw_fm"] = m.inp("nw_fm", [4, 128, 8])
    W["shift_fm"] = m.inp("shift_fm", [2, 128, 8, NBS])
    W["state_wkv"] = m.inp("state_wkv", [2, NBS, 16, 64, 64])
    W["qn_t"] = m.inp("qn_t", [2, 1024])
    W["kn_t"] = m.inp("kn_t", [2, 3, 256])
    W["kn2_col"] = m.inp("kn2_col", [2, 64, 1])
    W["pt8T"] = m.inp("pt8T", [128, NBS], I32)
    W["state_win_k"] = m.inp("state_win_k", [2, NBS, 512, 256])
    W["state_win_v"] = m.inp("state_win_v", [2, NBS, 512, 256])
    hc = nsa_host_consts(T)
    for n_ in ("c_ohA", "c_ohW", "c_expand", "c_expand_s", "c_s8", "c_fm_p", "c_fm_s", "c_ov_p", "c_ov_s"):
        a = hc[n_]
        dt = I32 if a.dtype == np.int32 else (BF16 if a.dtype != np.float32 else F32)
        m.inp(n_, list(a.shape), dt)
    xin_p = m.inp("xT_p_in", [1024, T])
    xin_s = m.inp("xT_s_in", [1024, NS])
    xo_p = m.outp("xT_p", [1024, T])
    xo_s = m.outp("xT_s", [1024, NS])
    k.dma("sp", xo_p[:, :], xin_p[:, :])
    k.dma("sp", xo_s[:, :], xin_s[:, :])
    wb = min(512, T)
    O = {"o_p_shift": m.outp("o_p_shift", [2, 128, 8]), "o_s_shift": m.outp("o_s_shift", [2, 128, 8, NBS]),
         "o_p_wkv": m.outp("o_p_wkv", [2, 128, 8, 64]), "o_s_wkv": m.outp("o_s_wkv", [2, NBS, 16, 64, 64]),
         "o_p_win_k": m.outp("o_p_win_k", [2, wb, 256]), "o_p_win_v": m.outp("o_p_win_v", [2, wb, 256]),
         "o_s_win_k": m.outp("o_s_win_k", [2, NBS, 512, 256]), "o_s_win_v": m.outp("o_s_win_v", [2, NBS, 512, 256])}
    scr = {}
    for pref, n in (("p_", T), ("s_", NS)):
        for nm in ("r", "k", "v", "w", "a", "g", "vf"):
            scr[pref + nm] = k.dram("scr_" + pref + nm, [n, 1024])
    scr["s_scan"] = k.dram("scr_s_scan", [6, NS, 1024])
    scr["s_y"] = k.dram("scr_s_y", [NS, 1024])
    nscr = []
    for jn in range(2):
        d = {}
        for nm in ("kc", "vc", "ks", "vs"):
            d["np_" + nm] = m.outp("o_p_%s_%d" % (nm, jn), [T, 256])
            d["ns_" + nm] = m.outp("o_s_%s_%d" % (nm, jn), [NS, 256])
        nscr.append(d)
    shared = {}
    for pref, n in (("np_", T), ("ns_", NS)):
        for nm in ("kw", "vw"):
            shared[pref + nm] = k.dram("scr_" + pref + nm, [n, 256])
        shared[pref + "qn"] = k.dram("scr_" + pref + "qn", [n, 1024], BF16)
        shared[pref + "sz"] = k.dram("scr_" + pref + "sz", [n, 1024])
        shared[pref + "gt"] = k.dram("scr_" + pref + "gt", [n, 48])
    shared["ns_ob"] = k.dram("scr_ns_ob", [NS, 1024], BF16)
    m.nsa_setup_tables(W)
    for li in range(n_layers):
        j = li // 2
        if li % 2 == 0:
            with k.scope():
                m.rwkv_phase1(j, li, xo_p, xo_s, W, scr, O)
            with k.scope():
                m.rwkv_phase2(j, li, xo_p, xo_s, W, scr, O)
        else:
            s2 = dict(shared)
            s2.update(nscr[j])
            with k.scope():
                m.nsa_phase1(j, li, xo_p, xo_s, W, s2, O)
            with k.scope():
                m.nsa_phase2_prompt(j, li, xo_p, W, s2, O)
            with k.scope():
                m.nsa_phase2_sample(j, li, xo_s, W, s2, O)
    nc = k.finish()
    return m, nc, hc


def host_prep(inputs, T, NBS, ncores, hc):
    f32 = np.float32
    common = dict(host_consts())
    for n_ in ("c_ohA", "c_ohW", "c_expand", "c_expand_s", "c_s8", "c_fm_p", "c_fm_s", "c_ov_p", "c_ov_s"):
        common[n_] = hc[n_]
    for n in RW_NAMES + NSA_NAMES + CACHE_NAMES:
        common[n] = np.ascontiguousarray(np.asarray(inputs[n], dtype=f32))
    mu = np.asarray(inputs["rw_mu"], f32)
    common["mu_fm"] = np.ascontiguousarray(mu.reshape(2, 6, 8, 128).transpose(0, 3, 1, 2))
    nw = np.asarray(inputs["norm_w"], f32)
    common["nw_fm"] = np.ascontiguousarray(nw.reshape(4, 8, 128).transpose(0, 2, 1))
    qn = np.asarray(inputs["nsa_q_norm"], f32)
    common["qn_t"] = np.ascontiguousarray(np.tile(qn, (1, 16)))
    kn = np.asarray(inputs["nsa_k_norm"], f32)
    common["kn_t"] = np.ascontiguousarray(np.tile(kn, (1, 1, 4)))
    common["kn2_col"] = np.ascontiguousarray(kn[:, 2, :, None])
    xp = np.asarray(inputs["x_prompt"], f32)
    xs = np.asarray(inputs["x_sample"], f32)
    sh = np.asarray(inputs["state_shift"], f32)
    swkv = np.asarray(inputs["state_wkv"], f32)
    pt = np.asarray(inputs["page_table"]).astype(np.int32)
    wk = np.asarray(inputs["state_win_k"], f32)
    wv = np.asarray(inputs["state_win_v"], f32)
    NS = NBS * 4
    maps = []
    for c in range(ncores):
        d = dict(common)
        b = c % xp.shape[0]
        bs = slice(c * NBS, (c + 1) * NBS)
        d["xT_p_in"] = np.ascontiguousarray(xp[b, :T].T)
        d["xT_s_in"] = np.ascontiguousarray(xs[bs].reshape(NS, 1024).T)
        d["shift_fm"] = np.ascontiguousarray(sh[:, bs].reshape(2, NBS, 8, 128).transpose(0, 3, 2, 1))
        d["state_wkv"] = np.ascontiguousarray(swkv[:, bs])
        d["pt8T"] = np.ascontiguousarray(np.repeat(pt[bs], 8, axis=1).T)
        d["state_win_k"] = np.ascontiguousarray(wk[:, bs].reshape(2, NBS, 512, 256))
        d["state_win_v"] = np.ascontiguousarray(wv[:, bs].reshape(2, NBS, 512, 256))
        maps.append(d)
    return maps


def assemble(results, T, NBS, ncores, nb_prompt):
    f32 = np.float32
    NS = NBS * 4
    B = nb_prompt
    DB = ncores * NBS
    y_p = np.zeros((B, T, 1024), f32)
    y_s = np.zeros((DB, 4, 1024), f32)
    p_wkv = np.zeros((2, B, 16, 64, 64), f32)
    p_shift = np.zeros((2, B, 1024), f32)
    p_kv = [np.zeros((2, B, T, 4, 64), f32) for _ in range(4)]
    wb = min(512, T)
    p_win = [np.zeros((2, B, wb, 4, 64), f32) for _ in range(2)]
    s_wkv = np.zeros((2, DB, 16, 64, 64), f32)
    s_shift = np.zeros((2, DB, 1024), f32)
    s_kv = [np.zeros((2, DB, 4, 4, 64), f32) for _ in range(4)]
    s_win = [np.zeros((2, DB, 512, 4, 64), f32) for _ in range(2)]
    for c in range(ncores):
        r = results[c]
        bs = slice(c * NBS, (c + 1) * NBS)
        y_s[bs] = r["xT_s"].T.reshape(NBS, 4, 1024)
        s_wkv[:, bs] = r["o_s_wkv"]
        s_shift[:, bs] = r["o_s_shift"].transpose(0, 3, 2, 1).reshape(2, NBS, 1024)
        for i, nm in enumerate(("kc", "vc", "ks", "vs")):
            for jn in range(2):
                s_kv[i][jn, bs] = r["o_s_%s_%d" % (nm, jn)].reshape(NBS, 4, 4, 64)
        s_win[0][:, bs] = r["o_s_win_k"].reshape(2, NBS, 512, 4, 64)
        s_win[1][:, bs] = r["o_s_win_v"].reshape(2, NBS, 512, 4, 64)
        if c < B:
            b = c
            y_p[b] = r["xT_p"].T
            for j in range(2):
                p_wkv[j, b] = r["o_p_wkv"][j].reshape(2, 64, 8, 64).transpose(2, 0, 3, 1).reshape(16, 64, 64)
                p_shift[j, b] = r["o_p_shift"][j].transpose(1, 0).reshape(1024)
            for i, nm in enumerate(("kc", "vc", "ks", "vs")):
                for jn in range(2):
                    p_kv[i][jn, b] = r["o_p_%s_%d" % (nm, jn)].reshape(T, 4, 64)
            p_win[0][:, b] = r["o_p_win_k"].reshape(2, wb, 4, 64)
            p_win[1][:, b] = r["o_p_win_v"].reshape(2, wb, 4, 64)
    return (y_p, y_s, p_wkv, p_shift, p_kv[0], p_kv[1], p_kv[2], p_kv[3], p_win[0], p_win[1],
            s_wkv, s_shift, s_kv[0], s_kv[1], s_kv[2], s_kv[3], s_win[0], s_win[1])


_CACHE = {}


def kernel(**inputs):
    from concourse.bass_utils import run_bass_kernel_spmd
    T, NBS = T_FULL, NBS_FULL
    shapes = {n: tuple(np.asarray(inputs[n]).shape) for n in RW_NAMES + NSA_NAMES + CACHE_NAMES}
    key = (T, NBS)
    if key not in _CACHE:
        _CACHE[key] = build_full(T, NBS, shapes)
    m, nc, hc = _CACHE[key]
    maps = host_prep(inputs, T, NBS, NCORES, hc)
    res = run_bass_kernel_spmd(nc, maps, core_ids=list(range(NCORES)))
    return assemble(res.results, T, NBS, NCORES, np.asarray(inputs["x_prompt"]).shape[0])
```
